# Optimizing a Trainium2 kernel written in Bass

```python
import math
import jax, jax.numpy as jnp
from jax import lax
import numpy as np

D_MODEL = 1024
BATCH = 4
SEQ = 8192
DEPTH = 2
DEC_BATCH = 8
DEC_SEQ = 16
PAST_LEN = 1024

CHUNK = 64
EPS = 1e-6
A_WIDTH = 256
A_GROUPS = 4
A_GDIM = A_WIDTH // A_GROUPS
A_CHUNK = 128
B_WIDTH = 256
B_KW = 31
C_HEADS = 8
C_NOPE = 64
C_ROPE = 32
C_VDIM = 64
C_QRANK = 256
C_KVRANK = 128
C_WIDTH = C_HEADS * C_VDIM
ROPE_THETA = 10000.0
Q_BLOCK = 128
MEM_LEN = 256
MEM_HEADS = 4
MEM_HDIM = 128
MEM_WIDTH = MEM_HEADS * MEM_HDIM
D_FF = 4 * D_MODEL

IN_WIDTH = 2 * A_WIDTH + 2 * B_WIDTH + C_QRANK + C_KVRANK + C_ROPE
MIX_WIDTH = A_WIDTH + B_WIDTH + C_WIDTH

kernel_name = 'hybrid_streaming_encoder_step'


def _rmsnorm(x, g):
    xf = x.astype(jnp.float32)
    y = xf * lax.rsqrt(jnp.mean(xf * xf, axis=-1, keepdims=True) + EPS)
    return (y * g.astype(jnp.float32)).astype(x.dtype)


def _layernorm(x, g, b):
    xf = x.astype(jnp.float32)
    xc = xf - jnp.mean(xf, axis=-1, keepdims=True)
    y = xc * lax.rsqrt(jnp.mean(xc * xc, axis=-1, keepdims=True) + EPS)
    return (y * g.astype(jnp.float32) + b.astype(jnp.float32)).astype(x.dtype)


def _rope(x, pos):
    half = C_ROPE // 2
    inv = ROPE_THETA ** (-jnp.arange(half, dtype=jnp.float32) / half)
    ang = pos.astype(jnp.float32)[:, None] * inv[None, :]
    shape = (pos.shape[0],) + (1,) * (x.ndim - 3) + (half,)
    cos = jnp.cos(ang).reshape(shape)
    sin = jnp.sin(ang).reshape(shape)
    xf = x.astype(jnp.float32)
    x1, x2 = xf[..., :half], xf[..., half:]
    return jnp.concatenate([x1 * cos - x2 * sin, x1 * sin + x2 * cos], axis=-1).astype(x.dtype)


def _gmlp(u, v, g, ws, bs):
    b, t, _ = u.shape
    L = min(t, A_CHUNK)
    u = jax.nn.gelu(u, approximate=False)
    v = _rmsnorm(jax.nn.gelu(v, approximate=False), g)
    mask = jnp.tril(jnp.ones((A_CHUNK, A_CHUNK), dtype=bool))
    wm = jnp.where(mask[None], ws, jnp.zeros_like(ws))[:, :L, :L]
    vc = v.reshape(b, t // L, L, A_GROUPS, A_GDIM)
    mixed = jnp.einsum('gij,bcjgd->bcigd', wm, vc) + bs[:, :L].T[None, None, :, :, None]
    return u * mixed.reshape(b, t, A_WIDTH), v


def _conv_module(a, gate, hist, w, bias, ln_g, ln_b):
    z = a * jax.nn.sigmoid(gate)
    if hist is None:
        hist = jnp.zeros((z.shape[0], B_KW - 1, B_WIDTH), z.dtype)
    zp = jnp.concatenate([hist.astype(z.dtype), z], axis=1)
    y = lax.conv_general_dilated(zp, w[:, None, :].astype(z.dtype), window_strides=(1,), padding='VALID',
                                 dimension_numbers=('NWC', 'WIO', 'NWC'),
                                 feature_group_count=B_WIDTH) + bias
    y = jax.nn.silu(_layernorm(y, ln_g, ln_b))
    return y, zp[:, zp.shape[1] - (B_KW - 1):]


def _mla_attend(q_nope, q_rope, qpos, k_nope, k_rope, v, kpos):
    s = jnp.einsum('bqhd,bkhd->bhqk', q_nope, k_nope) + jnp.einsum('bqhr,bkr->bhqk', q_rope, k_rope)
    s = s.astype(jnp.float32) * (1.0 / math.sqrt(C_NOPE + C_ROPE))
    mask = (kpos[None, :] // CHUNK) <= (qpos[:, None] // CHUNK)
    s = jnp.where(mask[None, None], s, -1e30)
    p = jax.nn.softmax(s, axis=-1).astype(v.dtype)
    o = jnp.einsum('bhqk,bkhd->bqhd', p, v)
    return o.reshape(o.shape[0], o.shape[1], C_WIDTH)


def _mem_kv(mem, l, W):
    b, m, _ = mem.shape
    mn = _rmsnorm(mem, W['mem_norm_g'][l])
    k = _rmsnorm((mn @ W['w_mk'][l]).reshape(b, m, MEM_HEADS, MEM_HDIM), W['m_k_g'][l])
    v = (mn @ W['w_mv'][l]).reshape(b, m, MEM_HEADS, MEM_HDIM)
    return k, v


def _layer(x, l, W, mem_k, mem_v, conv_hist, lat_past, kr_past, start):
    b, t, _ = x.shape
    h = _rmsnorm(x, W['norm_mix_g'][l]) @ W['w_in'][l]
    c0 = 2 * A_WIDTH + 2 * B_WIDTH
    cuts = [A_WIDTH, 2 * A_WIDTH, 2 * A_WIDTH + B_WIDTH, c0, c0 + C_QRANK, c0 + C_QRANK + C_KVRANK]
    a_u, a_v, b_a, b_g, c_q, c_kv, c_kr = jnp.split(h, cuts, axis=-1)
    y_a, v_rows = _gmlp(a_u, a_v, W['a_norm_g'][l], W['a_ws'][l], W['a_bs'][l])
    y_b, conv_state = _conv_module(b_a, b_g, conv_hist, W['b_dw_w'][l], W['b_dw_b'][l],
                                   W['b_ln_g'][l], W['b_ln_b'][l])
    pos = start + jnp.arange(t)
    q = (_rmsnorm(c_q, W['c_qa_g'][l]) @ W['c_w_uq'][l]).reshape(b, t, C_HEADS, C_NOPE + C_ROPE)
    q_nope = _rmsnorm(q[..., :C_NOPE], W['c_qn_g'][l])
    q_rope = _rope(_rmsnorm(q[..., C_NOPE:], W['c_qr_g'][l]), pos)
    lat_new = _rmsnorm(c_kv, W['c_kva_g'][l])
    kr_new = _rope(_rmsnorm(c_kr, W['c_kr_g'][l]), pos)
    if lat_past is None:
        lat_all, kr_all, kpos = lat_new, kr_new, pos
    else:
        lat_all = jnp.concatenate([lat_past.astype(lat_new.dtype), lat_new], axis=1)
        kr_all = jnp.concatenate([kr_past.astype(kr_new.dtype), kr_new], axis=1)
        kpos = jnp.arange(lat_past.shape[1] + t)
    kv = (lat_all @ W['c_w_ukv'][l]).reshape(b, lat_all.shape[1], C_HEADS, C_NOPE + C_VDIM)
    k_nope = _rmsnorm(kv[..., :C_NOPE], W['c_kn_g'][l])
    v = kv[..., C_NOPE:]
    if t > Q_BLOCK and t % Q_BLOCK == 0:
        nb = t // Q_BLOCK
        qn_b = q_nope.reshape(b, nb, Q_BLOCK, C_HEADS, C_NOPE).transpose(1, 0, 2, 3, 4)
        qr_b = q_rope.reshape(b, nb, Q_BLOCK, C_HEADS, C_ROPE).transpose(1, 0, 2, 3, 4)
        qp_b = pos.reshape(nb, Q_BLOCK)
        y_c = lax.map(lambda xs: _mla_attend(xs[0], xs[1], xs[2], k_nope, kr_all, v, kpos), (qn_b, qr_b, qp_b))
        y_c = y_c.transpose(1, 0, 2, 3).reshape(b, t, C_WIDTH)
    else:
        y_c = _mla_attend(q_nope, q_rope, pos, k_nope, kr_all, v, kpos)
    x = x + jnp.concatenate([y_a, y_b, y_c], axis=-1) @ W['w_out'][l]
    qm = _rmsnorm((_rmsnorm(x, W['norm_mem_g'][l]) @ W['w_mq'][l]).reshape(b, t, MEM_HEADS, MEM_HDIM),
                  W['m_q_g'][l])
    sm = jnp.einsum('bqhd,bkhd->bhqk', qm, mem_k.astype(qm.dtype)).astype(jnp.float32) * (1.0 / math.sqrt(MEM_HDIM))
    pm = jax.nn.softmax(sm, axis=-1).astype(x.dtype)
    om = jnp.einsum('bhqk,bkhd->bqhd', pm, mem_v.astype(x.dtype)).reshape(b, t, MEM_WIDTH)
    x = x + om @ W['w_mo'][l]
    x = x + jnp.square(jax.nn.relu(_rmsnorm(x, W['norm_ffn_g'][l]) @ W['w_ff1'][l])) @ W['w_ff2'][l]
    return x, v_rows, conv_state, lat_new, kr_new


def setup_inputs(seed: int = 0) -> dict:
    key = jax.random.key(seed)
    ks = iter(jax.random.split(key, 64))

    def nrm(shape, scale=1.0):
        return scale * jax.random.normal(next(ks), shape, jnp.float32)

    def gain(shape):
        return 1.0 + nrm(shape, 0.02)

    return {
        'x_prompt': nrm((BATCH, SEQ, D_MODEL)),
        'x_sample': nrm((DEC_BATCH, DEC_SEQ, D_MODEL)),
        'mem_prompt': nrm((BATCH, MEM_LEN, D_MODEL)),
        'cache_mla_latent': nrm((DEPTH, DEC_BATCH, PAST_LEN, C_KVRANK)),
        'cache_mla_krope': nrm((DEPTH, DEC_BATCH, PAST_LEN, C_ROPE)),
        'cache_conv': nrm((DEPTH, DEC_BATCH, B_KW - 1, B_WIDTH), 0.5),
        'cache_mem_k': nrm((DEPTH, DEC_BATCH, MEM_LEN, MEM_HEADS, MEM_HDIM)),
        'cache_mem_v': nrm((DEPTH, DEC_BATCH, MEM_LEN, MEM_HEADS, MEM_HDIM)),
        'norm_mix_g': gain((DEPTH, D_MODEL)),
        'w_in': nrm((DEPTH, D_MODEL, IN_WIDTH), D_MODEL ** -0.5),
        'a_norm_g': gain((DEPTH, A_WIDTH)),
        'a_ws': nrm((DEPTH, A_GROUPS, A_CHUNK, A_CHUNK), A_CHUNK ** -0.5),
        'a_bs': gain((DEPTH, A_GROUPS, A_CHUNK)),
        'b_dw_w': nrm((DEPTH, B_KW, B_WIDTH), B_KW ** -0.5),
        'b_dw_b': nrm((DEPTH, B_WIDTH), 0.02),
        'b_ln_g': gain((DEPTH, B_WIDTH)),
        'b_ln_b': nrm((DEPTH, B_WIDTH), 0.02),
        'c_qa_g': gain((DEPTH, C_QRANK)),
        'c_w_uq': nrm((DEPTH, C_QRANK, C_HEADS * (C_NOPE + C_ROPE)), C_QRANK ** -0.5),
        'c_kva_g': gain((DEPTH, C_KVRANK)),
        'c_w_ukv': nrm((DEPTH, C_KVRANK, C_HEADS * (C_NOPE + C_VDIM)), C_KVRANK ** -0.5),
        'c_qn_g': gain((DEPTH, C_NOPE)),
        'c_qr_g': gain((DEPTH, C_ROPE)),
        'c_kn_g': gain((DEPTH, C_NOPE)),
        'c_kr_g': gain((DEPTH, C_ROPE)),
        'w_out': nrm((DEPTH, MIX_WIDTH, D_MODEL), MIX_WIDTH ** -0.5),
        'norm_mem_g': gain((DEPTH, D_MODEL)),
        'mem_norm_g': gain((DEPTH, D_MODEL)),
        'w_mq': nrm((DEPTH, D_MODEL, MEM_WIDTH), D_MODEL ** -0.5),
        'w_mk': nrm((DEPTH, D_MODEL, MEM_WIDTH), D_MODEL ** -0.5),
        'w_mv': nrm((DEPTH, D_MODEL, MEM_WIDTH), D_MODEL ** -0.5),
        'w_mo': nrm((DEPTH, MEM_WIDTH, D_MODEL), MEM_WIDTH ** -0.5),
        'm_q_g': gain((DEPTH, MEM_HDIM)),
        'm_k_g': gain((DEPTH, MEM_HDIM)),
        'norm_ffn_g': gain((DEPTH, D_MODEL)),
        'w_ff1': nrm((DEPTH, D_MODEL, D_FF), D_MODEL ** -0.5),
        'w_ff2': nrm((DEPTH, D_FF, D_MODEL), D_FF ** -0.5),
    }


def reference(x_prompt, x_sample, mem_prompt, cache_mla_latent, cache_mla_krope, cache_conv,
              cache_mem_k, cache_mem_v, norm_mix_g, w_in, a_norm_g, a_ws, a_bs, b_dw_w, b_dw_b,
              b_ln_g, b_ln_b, c_qa_g, c_w_uq, c_kva_g, c_w_ukv, c_qn_g, c_qr_g, c_kn_g, c_kr_g,
              w_out, norm_mem_g, mem_norm_g, w_mq, w_mk, w_mv, w_mo, m_q_g, m_k_g, norm_ffn_g,
              w_ff1, w_ff2):
    W = dict(norm_mix_g=norm_mix_g, w_in=w_in, a_norm_g=a_norm_g, a_ws=a_ws, a_bs=a_bs,
             b_dw_w=b_dw_w, b_dw_b=b_dw_b, b_ln_g=b_ln_g, b_ln_b=b_ln_b, c_qa_g=c_qa_g,
             c_w_uq=c_w_uq, c_kva_g=c_kva_g, c_w_ukv=c_w_ukv, c_qn_g=c_qn_g, c_qr_g=c_qr_g,
             c_kn_g=c_kn_g, c_kr_g=c_kr_g, w_out=w_out, norm_mem_g=norm_mem_g,
             mem_norm_g=mem_norm_g, w_mq=w_mq, w_mk=w_mk, w_mv=w_mv, w_mo=w_mo, m_q_g=m_q_g,
             m_k_g=m_k_g, norm_ffn_g=norm_ffn_g, w_ff1=w_ff1, w_ff2=w_ff2)
    past = cache_mla_latent.shape[2]
    hp, hs = x_prompt, x_sample
    lat_p, kr_p, conv_p, mk_p, mv_p = [], [], [], [], []
    lat_s, kr_s, conv_s, gv_s = [], [], [], []
    for l in range(DEPTH):
        mk, mv = _mem_kv(mem_prompt, l, W)
        hp, _, cst, lat, kr = _layer(hp, l, W, mk, mv, None, None, None, 0)
        lat_p.append(lat); kr_p.append(kr); conv_p.append(cst); mk_p.append(mk); mv_p.append(mv)
        hs, gv, cst, lat, kr = _layer(hs, l, W, cache_mem_k[l], cache_mem_v[l], cache_conv[l],
                                      cache_mla_latent[l], cache_mla_krope[l], past)
        lat_s.append(lat); kr_s.append(kr); conv_s.append(cst); gv_s.append(gv)
    return (hp, hs,
            jnp.stack(lat_p), jnp.stack(kr_p), jnp.stack(conv_p), jnp.stack(mk_p), jnp.stack(mv_p),
            jnp.stack(lat_s), jnp.stack(kr_s), jnp.stack(conv_s), jnp.stack(gv_s))
```

```python
import contextlib
import math
import types
import numpy as np
import ml_dtypes
import concourse.bass as bass
import concourse.mybir as mybir
from concourse.bass_utils import run_bass_kernel_spmd

F32 = mybir.dt.float32
BF16 = mybir.dt.bfloat16
ALU = mybir.AluOpType
AF = mybir.ActivationFunctionType
AX = mybir.AxisListType

D = 1024
DEPTH = 2
EPS = 1e-6
IN_W = 1440
NPAST = 1024
TS = 16
SBUF_LO = 16512
SBUF_HI = 229344


def _freeze(fn):
    if fn.__closure__ is None:
        return fn
    cells = []
    for c in fn.__closure__:
        try:
            cells.append(types.CellType(c.cell_contents))
        except ValueError:
            cells.append(c)
    return types.FunctionType(fn.__code__, fn.__globals__, fn.__name__, fn.__defaults__, tuple(cells))


class Sched:
    ENGS = ("pe", "act", "dve", "pool", "sp")

    def __init__(self, nc):
        self.nc = nc
        self.q = {e: [] for e in self.ENGS}
        self.cnt = {}
        self.last_w = {}
        self.readers = {}
        self.seen = {e: {} for e in self.ENGS}
        self.pending = {e: {} for e in self.ENGS}
        self.n_ops = 0

    def _deps(self, eng, reads, writes):
        need = dict(self.pending[eng])
        writes = list(writes) + [r for r in reads if r.startswith("pf") or r.startswith("pb")]
        self.pending[eng] = {}

        def add(tok):
            if tok is not None and need.get(tok[0], 0) < tok[1]:
                need[tok[0]] = tok[1]
        for r in reads:
            add(self.last_w.get(r))
        for w in writes:
            add(self.last_w.get(w))
            for t in self.readers.get(w, ()):
                add(t)
        seen = self.seen[eng]
        waits = []
        if eng == "pe":
            need.pop("pe", None)
        for k, v in need.items():
            if seen.get(k, 0) < v:
                seen[k] = v
                waits.append((k, v))
        return waits

    def _commit(self, tok, reads, writes):
        writes = list(writes) + [r for r in reads if r.startswith("pf") or r.startswith("pb")]
        for r in reads:
            self.readers.setdefault(r, []).append(tok)
        for w in writes:
            self.last_w[w] = tok
            self.readers[w] = []

    def op(self, eng, fn, reads=(), writes=(), signal=True):
        waits = self._deps(eng, reads, writes)
        if signal:
            self.cnt[eng] = self.cnt.get(eng, 0) + 1
            tok = (eng, self.cnt[eng])
        else:
            tok = (eng, self.cnt.get(eng, 0) + 1)
        self.q[eng].append((_freeze(fn), waits, eng, 1 if signal else 0))
        self._commit(tok, reads, writes)
        self.n_ops += 1
        return tok

    def dma(self, tag, fn, reads=(), writes=(), eng="sp", inc=16):
        key = "dma:" + tag
        if self.cnt.get(key, 0):
            p = self.pending[eng]
            p[key] = max(p.get(key, 0), self.cnt[key])
        waits = self._deps(eng, reads, writes)
        self.cnt[key] = self.cnt.get(key, 0) + inc
        tok = (key, self.cnt[key])
        self.q[eng].append((_freeze(fn), waits, key, inc))
        self._commit(tok, reads, writes)
        self.n_ops += 1
        return tok

    def barrier(self):
        for e in self.ENGS:
            p = self.pending[e]
            for k, v in self.cnt.items():
                if p.get(k, 0) < v:
                    p[k] = v

    def emit(self):
        nc = self.nc
        with contextlib.ExitStack() as st:
            sems = {}
            for k in self.cnt:
                sems[k] = st.enter_context(nc.semaphore("s_" + k.replace(":", "_")))
            block = st.enter_context(nc.Block())
            fin = list(self.cnt.items())

            def run(name, e):
                for fn, waits, key, inc in self.q[name]:
                    for k, v in waits:
                        e.wait_ge(sems[k], v)
                    ins = fn(e)
                    if inc:
                        ins.then_inc(sems[key], inc)
                if name == "sp":
                    for k, v in fin:
                        e.wait_ge(sems[k], v)

            @block.tensor
            def _(e):
                run("pe", e)

            @block.scalar
            def _(e):
                run("act", e)

            @block.vector
            def _(e):
                run("dve", e)

            @block.gpsimd
            def _(e):
                run("pool", e)

            @block.sync
            def _(e):
                run("sp", e)


class Arena:
    def __init__(self, nc):
        self.nc = nc
        self.off = SBUF_LO
        self.n = 0
        self.peak = 0

    def alloc(self, name, shape, dt):
        per = 1
        for s in shape[1:]:
            per *= s
        nb = per * (4 if dt == F32 else 2)
        nb = (nb + 63) // 64 * 64
        assert self.off + nb <= SBUF_HI, f"SBUF overflow at {name}: {self.off + nb}"
        self.n += 1
        t = self.nc.alloc_sbuf_tensor_at(f"{name}_{self.n}", list(shape), dt, offset=self.off)
        self.off += nb
        self.peak = max(self.peak, self.off)
        return t

    def mark(self):
        return self.off

    def reset(self, m):
        self.off = m


class Grp:
    pass


def build_nc(TP):
    nc = bass.Bass("TRN2", target_bir_lowering=False)
    S = Sched(nc)
    AR = Arena(nc)

    def din(name, shape, dt=F32):
        return nc.dram_tensor(name, list(shape), dt, kind="ExternalInput").ap()

    def dout(name, shape, dt=F32):
        return nc.dram_tensor(name, list(shape), dt, kind="ExternalOutput").ap()

    def dscr(name, shape, dt):
        return nc.dram_tensor(name, list(shape), dt).ap()

    x_p = din("x_p", [TP, D]); x_s = din("x_s", [TS, D]); mem_in = din("mem", [256, D])
    c_lat = din("c_lat", [DEPTH, NPAST, 128]); c_kr = din("c_kr", [DEPTH, NPAST, 32])
    c_conv = din("c_conv", [DEPTH, 30, 256])
    c_mk = din("c_mk", [DEPTH, 256, 512]); c_mv = din("c_mv", [DEPTH, 256, 512])
    W = {}
    for nm, shp in [("norm_mix_g", [DEPTH, D]), ("w_in", [DEPTH, D, IN_W]), ("a_norm_g", [DEPTH, 256]),
                    ("a_ws", [DEPTH, 4, 128, 128]), ("a_bs", [DEPTH, 4, 128]), ("b_dw_w", [DEPTH, 31, 256]),
                    ("b_dw_b", [DEPTH, 256]), ("b_ln_g", [DEPTH, 256]), ("b_ln_b", [DEPTH, 256]),
                    ("c_qa_g", [DEPTH, 256]), ("c_w_uq", [DEPTH, 256, 768]), ("c_kva_g", [DEPTH, 128]),
                    ("c_w_ukv", [DEPTH, 128, 1024]), ("c_qn_g", [DEPTH, 64]), ("c_qr_g", [DEPTH, 32]),
                    ("c_kn_g", [DEPTH, 64]), ("c_kr_g", [DEPTH, 32]), ("w_out", [DEPTH, D, D]),
                    ("norm_mem_g", [DEPTH, D]), ("mem_norm_g", [DEPTH, D]), ("w_mq", [DEPTH, D, 512]),
                    ("w_mk", [DEPTH, D, 512]), ("w_mv", [DEPTH, D, 512]), ("w_mo", [DEPTH, 512, D]),
                    ("m_q_g", [DEPTH, 128]), ("m_k_g", [DEPTH, 128]), ("norm_ffn_g", [DEPTH, D]),
                    ("w_ff1", [DEPTH, D, 4 * D]), ("w_ff2", [DEPTH, 4 * D, D])]:
        W[nm] = din(nm, shp)
    ident_d = din("ident", [128, 128], BF16)
    tril_d = din("tril", [128, 128])
    cs_p = din("cs_p", [TP, 32]); cs_s = din("cs_s", [TS, 32])
    flag_d = din("flag", [128, 1])
    mask_d = din("mask", [128, 8, 512], BF16)

    y_p = dout("y_p", [TP, D]); y_s = dout("y_s", [TS, D])
    o_lat_p = dout("o_lat_p", [DEPTH, TP, 128]); o_kr_p = dout("o_kr_p", [DEPTH, TP, 32])
    o_conv_p = dout("o_conv_p", [DEPTH, 30, 256])
    o_mk_p = dout("o_mk_p", [DEPTH, 256, 512]); o_mv_p = dout("o_mv_p", [DEPTH, 256, 512])
    o_lat_s = dout("o_lat_s", [DEPTH, TS, 128]); o_kr_s = dout("o_kr_s", [DEPTH, TS, 32])
    o_conv_s = dout("o_conv_s", [DEPTH, 30, 256]); o_gv_s = dout("o_gv_s", [DEPTH, TS, 256])

    def mk_group(name, T, NP, x_in, cs, y_out, o_lat, o_kr, o_conv):
        G = Grp()
        G.name, G.T, G.NP = name, T, NP
        G.P = min(128, T); G.BS = min(512, T); G.NSUB = G.BS // G.P; G.NBLK = T // G.BS
        G.NK = NP + T
        G.x_in, G.cs, G.y_out, G.o_lat, G.o_kr, G.o_conv = x_in, cs, y_out, o_lat, o_kr, o_conv
        G.QT = dscr("QT_" + name, [8, 128, T], BF16)
        G.mixT = dscr("mixT_" + name, [D, T], BF16)
        G.x1 = dscr("x1_" + name, [T, D], F32)
        G.xm = dscr("xm_" + name, [T, D], F32)
        G.xnf = dscr("xnf_" + name, [D, T], BF16)
        G.remote = False
        return G
    GP = mk_group("p", TP, TP, x_p, cs_p, y_p, o_lat_p, o_kr_p, o_conv_p)
    GP.remote = True
    NBP = TP // 512
    GP.xin = [nc.dram_tensor("xin_l", [128, TP], BF16), nc.dram_tensor("xin_k", [32, TP], BF16), nc.dram_tensor("xin_z", [128, NBP * 60], BF16)]
    GP.xout = [nc.dram_tensor("xout_l", [256, TP], BF16), nc.dram_tensor("xout_k", [64, TP], BF16), nc.dram_tensor("xout_z", [256, NBP * 60], BF16)]
    GS = mk_group("s", TS, NPAST, x_s, cs_s, y_s, o_lat_s, o_kr_s, o_conv_s)
    GROUPS = [GP, GS]

    PF = [nc.alloc_psum_tensor(f"pf{i}", [128, 512], F32) for i in range(6)]
    PB = [nc.alloc_psum_tensor(f"pb{i}", [128, 8, 128], BF16) for i in range(2)]

    rot = {"i": 0}

    def R(t):
        return t.name

    def PE(fn, r, w, sig=True):
        return S.op("pe", fn, r, w, signal=sig)

    def ACT(fn, r, w):
        return S.op("act", fn, r, w)

    def DVE(fn, r, w):
        return S.op("dve", fn, r, w)

    def POOL(fn, r, w):
        return S.op("pool", fn, r, w)

    def DMA(tag, out, in_, r, w, eng="sp", slow=False):
        if slow:
            return S.dma(tag, lambda e: e.dma_start(out=out, in_=in_, allow_slow_non_contiguous=True), r, w, eng)
        return S.dma(tag, lambda e: e.dma_start(out=out, in_=in_), r, w, eng)

    def rsqrt_inplace(ap, n, names):
        ACT(lambda e: e.activation(out=ap, in_=ap, func=AF.Sqrt, bias=EPS, scale=1.0 / n), names, names)
        DVE(lambda e: e.reciprocal(out=ap, in_=ap), names, names)

    ident = AR.alloc("ident", [128, 128], BF16)
    ident_f = AR.alloc("identf", [128, 128], F32)
    tril = AR.alloc("tril", [128, 128], F32)
    ones_bf = AR.alloc("ones_bf", [128, 128], BF16)
    ones_f = AR.alloc("ones_f", [128, 64], F32)
    stg = [AR.alloc(f"stg{i}", [128, 1024], F32) for i in range(2)]
    DMA("c0", ident[:], ident_d, [], [R(ident)])
    DMA("c0", tril[:], tril_d, [], [R(tril)])
    flag = AR.alloc("flag", [128, 1], F32)
    nflag = AR.alloc("nflag", [128, 1], F32)
    DMA("c0", flag[:], flag_d, [], [R(flag)])
    POOL(lambda e: e.tensor_scalar(out=nflag[:], in0=flag[:], scalar1=-1.0, scalar2=1.0, op0=ALU.mult, op1=ALU.add), [R(flag)], [R(nflag)])
    POOL(lambda e: e.memset(ones_bf[:], 1.0), [], [R(ones_bf)])
    POOL(lambda e: e.memset(ones_f[:], 1.0), [], [R(ones_f)])
    POOL(lambda e: e.tensor_copy(out=ident_f[:], in_=ident[:]), [R(ident)], [R(ident_f)])
    BASE0 = AR.mark()

    cast_rot = {"i": 0}

    def cast(out, in_, r, w, gain=None):
        i = cast_rot["i"] = (cast_rot["i"] + 1) % 3
        if gain is None:
            if i <= 1:
                DVE(lambda e: e.tensor_copy(out=out, in_=in_), r, w)
            else:
                ACT(lambda e: e.activation(out=out, in_=in_, func=AF.Copy), r, w)
        else:
            if i == 0:
                DVE(lambda e: e.tensor_single_scalar(out=out, in_=in_, scalar=gain, op=ALU.mult), r, w)
            elif i == 1:
                DVE(lambda e: e.tensor_single_scalar(out=out, in_=in_, scalar=gain, op=ALU.mult), r, w)
            else:
                ACT(lambda e: e.activation(out=out, in_=in_, func=AF.Copy, scale=gain), r, w)

    stg_i = {"i": 0}

    def load_w(dst, src2d, KT, N, gain=None, gname=None):
        CW = min(N, 1024)
        for k in range(KT):
            for c0 in range(0, N, CW):
                cw = min(CW, N - c0)
                st = stg[stg_i["i"] % 2]
                stg_i["i"] += 1
                DMA("stg" + str(stg_i["i"] % 2), st[:, 0:cw], src2d[k * 128:(k + 1) * 128, c0:c0 + cw], [], [R(st)])
                rr = [R(st)] + ([gname] if gain is not None else [])
                cast(dst[:, k, c0:c0 + cw], st[:, 0:cw], rr, [R(dst)],
                     gain=None if gain is None else gain[:, k:k + 1])

    def load_gain_pk(dst, vec, KT):
        DMA("small", dst[:, 0:KT], vec.rearrange("(k p) -> p k", p=128), [], [R(dst)], slow=True)

    def load_bcast(dst, vec, n, P=128):
        DMA("small", dst[0:P, 0:n], vec.partition_broadcast(P), [], [R(dst)])

    def transposes(pb, src, nslot, P, r):
        for j in range(nslot):
            PE(lambda e, j=j: e.transpose(out=pb[:, j, 0:P], in_=src[0:P, j, :], identity=ident[0:P, 0:P]),
               r + [R(ident)], [R(pb)], sig=(j == nslot - 1))

    def rope(out1, out2, x1, x2, cos, sin, tmp, r, w):
        (t1, n1), (t2, n2), (t3, n3), (t4, n4) = tmp
        DVE(lambda e: e.tensor_tensor(out=t1, in0=x1, in1=cos, op=ALU.mult), r, [n1])
        POOL(lambda e: e.tensor_tensor(out=t2, in0=x2, in1=sin, op=ALU.mult), r, [n2])
        DVE(lambda e: e.tensor_tensor(out=t3, in0=x1, in1=sin, op=ALU.mult), r, [n3])
        POOL(lambda e: e.tensor_tensor(out=t4, in0=x2, in1=cos, op=ALU.mult), r, [n4])
        DVE(lambda e: e.tensor_tensor(out=out1, in0=t1, in1=t2, op=ALU.subtract), [n1, n2] + w, w)
        POOL(lambda e: e.tensor_tensor(out=out2, in0=t3, in1=t4, op=ALU.add), [n3, n4] + w, w)

    def rms_to_bf16(x_ap, xn_ap, junk_ap, ss_ap, P, n, r, names, on_pool=False):
        ACT(lambda e: e.activation(out=junk_ap, in_=x_ap, func=AF.Square, accum_out=ss_ap), r, [names[0], names[1]])
        rsqrt_inplace(ss_ap, n, [names[1]])
        if on_pool:
            DVE(lambda e: e.tensor_single_scalar(out=xn_ap, in_=x_ap, scalar=ss_ap, op=ALU.mult), r + [names[1]], [names[2]])
        else:
            ACT(lambda e: e.activation(out=xn_ap, in_=x_ap, func=AF.Copy, scale=ss_ap), r + [names[1]], [names[2]])

    for l in range(DEPTH):
        S.barrier()
        AR.reset(BASE0)
        for G in GROUPS:
            G.latT = AR.alloc("latT" + G.name, [128, G.NK], BF16)
            G.krT = AR.alloc("krT" + G.name, [128, G.NK], BF16)
        BASE = AR.mark()
        Win = AR.alloc("Win", [128, 8, IN_W], BF16)
        Wuq = AR.alloc("Wuq", [128, 2, 768], BF16)
        wmT = AR.alloc("wmT", [128, 4, 128], BF16)
        cdiag = AR.alloc("cdiag", [128, 2, 31, 128], BF16)
        g_mix = AR.alloc("g_mix", [128, 8], F32)
        g_qa = AR.alloc("g_qa", [128, 2], F32)
        ga_bc = AR.alloc("ga_bc", [128, 256], F32)
        gqn_bc = AR.alloc("gqn_bc", [128, 64], F32)
        gqr_bc = AR.alloc("gqr_bc", [128, 32], F32)
        gkva_bc = AR.alloc("gkva_bc", [128, 128], F32)
        gkr_bc = AR.alloc("gkr_bc", [128, 32], F32)
        bsT = AR.alloc("bsT", [128, 4], F32)
        bdw = AR.alloc("bdw", [128, 2], F32)
        lng = AR.alloc("lng", [128, 2], F32)
        lnb = AR.alloc("lnb", [128, 2], F32)
        wdw = AR.alloc("wdw", [128, 2, 31], F32)
        wnat = AR.alloc("wnat", [128, 256], F32)
        wnat_b = AR.alloc("wnat_b", [128, 2, 128], BF16)
        wsb = AR.alloc("wsb", [128, 4, 128], BF16)

        load_gain_pk(g_mix, W["norm_mix_g"][l], 8)
        load_gain_pk(g_qa, W["c_qa_g"][l], 2)
        load_gain_pk(bdw, W["b_dw_b"][l], 2)
        load_gain_pk(lng, W["b_ln_g"][l], 2)
        load_gain_pk(lnb, W["b_ln_b"][l], 2)
        load_bcast(ga_bc, W["a_norm_g"][l], 256)
        load_bcast(gqn_bc, W["c_qn_g"][l], 64)
        load_bcast(gqr_bc, W["c_qr_g"][l], 32)
        load_bcast(gkva_bc, W["c_kva_g"][l], 128)
        load_bcast(gkr_bc, W["c_kr_g"][l], 32)
        DMA("small", bsT[:, :], W["a_bs"][l].rearrange("g i -> i g"), [], [R(bsT)], slow=True)
        load_w(Win, W["w_in"][l], 8, IN_W, gain=g_mix, gname=R(g_mix))
        for k in range(2):
            st = stg[stg_i["i"] % 2]
            stg_i["i"] += 1
            src = W["c_w_uq"][l][k * 128:(k + 1) * 128, :].rearrange("p (h d) -> p h d", d=96)
            tg = "stg" + str(stg_i["i"] % 2)
            DMA(tg, st[:, 0:512].rearrange("p (h d) -> p h d", d=64), src[:, :, 0:64], [], [R(st)])
            DMA(tg, st[:, 512:768].rearrange("p (h d) -> p h d", d=32), src[:, :, 64:96], [], [R(st)])
            cast(Wuq[:, k, :], st[:, 0:768], [R(st), R(g_qa)], [R(Wuq)], gain=g_qa[:, k:k + 1])
        for g in range(4):
            st = stg[stg_i["i"] % 2]
            stg_i["i"] += 1
            DMA("stg" + str(stg_i["i"] % 2), st[:, 0:128], W["a_ws"][l, g], [], [R(st)])
            DVE(lambda e, st=st, g=g: e.tensor_tensor(out=wsb[:, g, :], in0=st[:, 0:128], in1=tril[:], op=ALU.mult),
                [R(st), R(tril)], [R(wsb)])
        transposes(PB[0], wsb, 4, 128, [R(wsb)])
        DVE(lambda e: e.tensor_copy(out=wmT[:], in_=PB[0][:, 0:4, 0:128]), [R(PB[0])], [R(wmT)])
        POOL(lambda e: e.memset(wnat[:], 0.0), [], [R(wnat)])
        DMA("small", wnat[0:31, :], W["b_dw_w"][l], [R(wnat)], [R(wnat)])
        POOL(lambda e: e.tensor_copy(out=wnat_b[:].rearrange("p a b -> p (a b)"), in_=wnat[:]), [R(wnat)], [R(wnat_b)])
        transposes(PB[0], wnat_b, 2, 128, [R(wnat_b)])
        DVE(lambda e: e.tensor_copy(out=wdw[:], in_=PB[0][:, 0:2, 0:31]), [R(PB[0])], [R(wdw)])
        for ft in range(2):
            for k in range(31):
                eng = POOL if (k % 2 == 0) else DVE
                eng(lambda e, ft=ft, k=k: e.tensor_single_scalar(out=cdiag[:, ft, k, :], in_=ident_f[:],
                                                                 scalar=wdw[:, ft, k:k + 1], op=ALU.mult),
                    [R(ident_f), R(wdw)], [R(cdiag)])

        S1 = AR.mark()
        for G in GROUPS:
            S.barrier()
            AR.reset(S1)
            P, BS, NSUB, T, NP = G.P, G.BS, G.NSUB, G.T, G.NP
            latT, krT = G.latT, G.krT
            x_src = G.x_in if l == 0 else G.x1
            xs = [AR.alloc("x_sb", [128, D], F32) for _ in range(3)]
            junk_2 = [AR.alloc("junk", [128, D], F32)] * 2
            ss_2 = [AR.alloc("ss", [128, 8], F32) for _ in range(2)]
            xn_2 = [AR.alloc("xn", [128, 8, 128], BF16) for _ in range(2)]
            xnT_2 = [AR.alloc("xnT", [128, 8, 128], BF16) for _ in range(2)]
            u_sb_2 = [AR.alloc("u_sb", [128, 256], F32) for _ in range(2)]
            gv_sb_2 = [AR.alloc("gv_sb", [128, 256], F32) for _ in range(2)]
            v32_2 = [AR.alloc("v32", [128, 256], F32) for _ in range(2)]
            v_bf_2 = [AR.alloc("v_bf", [128, 256], BF16) for _ in range(2)]
            sg_2 = [AR.alloc("sg", [128, 256], F32) for _ in range(2)]
            z32_2 = [AR.alloc("z32", [128, 256], F32) for _ in range(2)]
            TM_2 = [AR.alloc("TM", [128, 8, 128], BF16) for _ in range(2)]
            cqnT_2 = [AR.alloc("cqnT", [128, 2, 128], BF16) for _ in range(2)]
            sq_2 = [AR.alloc("sq", [128, 768], F32) for _ in range(2)]
            ssn_2 = [AR.alloc("ssn", [128, 16], F32) for _ in range(2)]
            qn32_2 = [AR.alloc("qn32", [128, 8, 64], F32) for _ in range(2)]
            qr32_2 = [AR.alloc("qr32", [128, 8, 32], F32) for _ in range(2)]
            rt_2 = [[AR.alloc(f"rt{i}", [128, 8, 16], F32) for i in range(4)] for _ in range(2)]
            Qtm_2 = [AR.alloc("Qtm", [128, 8, 128], BF16) for _ in range(2)]
            lat32_2 = [AR.alloc("lat32", [128, 128], F32) for _ in range(2)]
            kr32_2 = [AR.alloc("kr32", [128, 32], F32) for _ in range(2)]
            kro32_2 = [AR.alloc("kro32", [128, 32], F32) for _ in range(2)]
            cst_2 = [AR.alloc("cst", [128, 32], F32) for _ in range(4)]
            mixblk = AR.alloc("mixblk", [128, 4, 512], BF16)
            QTblk = AR.alloc("QTblk", [128, 8, 512], BF16)
            zTb = AR.alloc("zTb", [128, 2, 30 + 512], BF16)
            y32 = AR.alloc("y32", [128, 2, 512], F32)
            ybf = AR.alloc("ybf", [128, 2, 512], BF16)
            ysq = AR.alloc("ysq", [128, 2, 512], BF16)
            mean = AR.alloc("mean", [128, 512], F32)
            msq = AR.alloc("msq", [128, 512], F32)
            rstd = AR.alloc("rstd", [128, 512], F32)
            tln = AR.alloc("tln", [128, 512], F32)
            hb0, hb1, cv0, cv1 = PF[0], PF[1], PF[3], PF[4]
            HB = [PF[0:3], PF[3:6]]
            AUXF = [p_[:].rearrange("p a b -> p (a b)").bitcast(F32) for p_ in PB]
            pbA, pbB = PB

            POOL(lambda e: e.memset(TM_2[0][:], 0.0), [], [R(TM_2[0])])
            POOL(lambda e: e.memset(TM_2[1][:], 0.0), [], [R(TM_2[1])])
            POOL(lambda e: e.memset(Qtm_2[0][:], 0.0), [], [R(Qtm_2[0])])
            POOL(lambda e: e.memset(Qtm_2[1][:], 0.0), [], [R(Qtm_2[1])])
            zTs = [zTb] * G.NBLK
            if G.remote:
                zTs = [zTb] + [AR.alloc("zTall", [128, 2, 30 + 512], BF16) for _ in range(G.NBLK - 1)]
                mixblk0_2 = [AR.alloc("mixblk0", [128, 2, 512], BF16) for _ in range(2)]
                tz = AR.alloc("tz", [128, 2, G.NBLK * 60], BF16)
            elif NP > 0:
                npt = NP // 128
                pl = AR.alloc("pl", [128, npt, 128], F32)
                pk_ = AR.alloc("pk", [128, npt, 32], F32)
                plb = AR.alloc("plb", [128, npt, 128], BF16)
                pkb = AR.alloc("pkb", [128, npt, 128], BF16)
                DMA("past", pl[:], c_lat[l].rearrange("(t p) c -> p t c", p=128), [], [R(pl)])
                DMA("past", pk_[:], c_kr[l].rearrange("(t p) c -> p t c", p=128), [], [R(pk_)])
                POOL(lambda e: e.tensor_copy(out=plb[:], in_=pl[:]), [R(pl)], [R(plb)])
                POOL(lambda e: e.memset(pkb[:], 0.0), [], [R(pkb)])
                POOL(lambda e: e.tensor_copy(out=pkb[:, :, 64:96], in_=pk_[:]), [R(pk_), R(pkb)], [R(pkb)])
                transposes(pbA, plb, npt, 128, [R(plb)])
                DVE(lambda e: e.tensor_copy(out=latT[:, 0:NP].rearrange("p (t c) -> p t c", c=128), in_=pbA[:, 0:npt, :]),
                    [R(pbA)], [R(latT)])
                transposes(pbA, pkb, npt, 128, [R(pkb)])
                DVE(lambda e: e.tensor_copy(out=krT[:, 0:NP].rearrange("p (t c) -> p t c", c=128), in_=pbA[:, 0:npt, :]),
                    [R(pbA)], [R(krT)])
                ch = AR.alloc("ch", [128, 256], F32)
                chb = AR.alloc("chb", [128, 2, 128], BF16)
                DMA("past", ch[0:30, :], c_conv[l], [], [R(ch)])
                POOL(lambda e: e.tensor_copy(out=chb[0:30].rearrange("p a b -> p (a b)"), in_=ch[0:30, :]), [R(ch)], [R(chb)])
                for ft in range(2):
                    PE(lambda e, ft=ft: e.transpose(out=pbA[:, ft, 0:30], in_=chb[0:30, ft, :], identity=ident[0:30, 0:30]),
                       [R(chb), R(ident)], [R(pbA)])
                DVE(lambda e: e.tensor_copy(out=zTb[:, :, 0:30], in_=pbA[:, 0:2, 0:30]), [R(pbA)], [R(zTb)])
                DMA("oc", G.o_conv[l, 0:14, :], c_conv[l, 16:30, :], [], [])
            else:
                POOL(lambda e: e.memset(zTb[:, :, 0:30], 0.0), [], [R(zTb)])

            def load_sub(n_):
                if n_ >= G.NBLK * NSUB:
                    return
                t0_ = n_ * P
                DMA("x" + str(n_ % 3), xs[n_ % 3][0:P, :], x_src[t0_:t0_ + P, :], [G.name + "x1"] if l else [], [R(xs[n_ % 3])])
                DMA("cs" + str(n_ % 4), cst_2[n_ % 4][0:P, :], G.cs[t0_:t0_ + P, :], [], [R(cst_2[n_ % 4])])
            load_sub(0)
            load_sub(1)
            for blk in range(G.NBLK):
                def subtile(blk, s, par):
                    (junk, ss, xn, xnT, u_sb, gv_sb, v32, v_bf, sg, z32, TM, cqnT, sq, ssn, qn32, qr32, Qtm, lat32, kr32, kro32, cst) = (
                        junk_2[par], ss_2[par], xn_2[par], xnT_2[par], u_sb_2[par], gv_sb_2[par], v32_2[par], v_bf_2[par], sg_2[par], z32_2[par],
                        TM_2[par], cqnT_2[par], sq_2[par], ssn_2[par], qn32_2[par], qr32_2[par], Qtm_2[par], lat32_2[par], kr32_2[par], kro32_2[par], cst_2[(blk * NSUB + s) % 4])
                    rt = rt_2[par]
                    hb0, hb1, hb2 = HB[par]
                    pm = AUXF[par]
                    pbA = pbB = PB[par]
                    ssx = f"ssx{par}_"
                    zTb = zTs[blk]
                    t0 = blk * BS + s * P
                    x_sb = xs[(blk * NSUB + s) % 3]
                    load_sub(blk * NSUB + s + 2)
                    rms_to_bf16(x_sb[0:P, :], xn[0:P].rearrange("p a b -> p (a b)"), junk[0:P, :], ss[0:P, 0:1], P, D,
                                [R(x_sb)], [R(junk), ssx + "ss0", R(xn)], on_pool=True)
                    yield
                    transposes(pbA, xn, 8, P, [R(xn)])
                    DVE(lambda e: e.tensor_copy(out=xnT[:, :, 0:P], in_=pbA[:, :, 0:P]), [R(pbA)], [R(xnT)])
                    yield
                    for c, (c0, cw) in enumerate([(0, 512), (512, 512), (1024, 416)]):
                        hb = HB[par][c]
                        for k in range(8):
                            PE(lambda e, hb=hb, k=k, c0=c0, cw=cw: e.matmul(hb[0:P, 0:cw], lhsT=xnT[:, k, 0:P], rhs=Win[:, k, c0:c0 + cw],
                                                                            start=(k == 0), stop=(k == 7)),
                               [R(xnT), R(Win)], [R(hb)], sig=(k == 7))
                    yield
                    ACT(lambda e: e.activation(out=u_sb[0:P, :], in_=hb0[0:P, 0:256], func=AF.Gelu), [R(hb0)], [R(u_sb)])
                    ACT(lambda e: e.activation(out=gv_sb[0:P, :], in_=hb0[0:P, 256:512], func=AF.Gelu), [R(hb0)], [R(gv_sb)])
                    ACT(lambda e: e.activation(out=junk[0:P, 0:256], in_=gv_sb[0:P, :], func=AF.Square, accum_out=ss[0:P, 1:2]),
                        [R(gv_sb)], [R(junk), ssx + "ss1"])
                    rsqrt_inplace(ss[0:P, 1:2], 256, [ssx + "ss1"])
                    DVE(lambda e: e.scalar_tensor_tensor(out=v32[0:P, :], in0=gv_sb[0:P, :], scalar=ss[0:P, 1:2], in1=ga_bc[0:P, :],
                                                         op0=ALU.mult, op1=ALU.mult),
                        [R(gv_sb), ssx + "ss1", R(ga_bc)], [R(v32)])
                    if G is GS:
                        DMA("ogv", o_gv_s[l, t0:t0 + P, :], v32[0:P, :], [R(v32)], [])
                    yield
                    POOL(lambda e: e.tensor_copy(out=v_bf[0:P, :], in_=v32[0:P, :]), [R(v32)], [R(v_bf)])
                    for g in range(4):
                        PE(lambda e, g=g: e.matmul(pm[0:P, g * 64:(g + 1) * 64], lhsT=wmT[0:P, g, 0:P], rhs=v_bf[0:P, g * 64:(g + 1) * 64],
                                                   start=True, stop=True),
                           [R(wmT), R(v_bf)], [R(pm)])
                    for g in range(4):
                        DVE(lambda e, g=g: e.scalar_tensor_tensor(out=TM[0:P, g // 2, (g % 2) * 64:(g % 2) * 64 + 64],
                                                                  in0=pm[0:P, g * 64:(g + 1) * 64], scalar=bsT[0:P, g:g + 1],
                                                                  in1=u_sb[0:P, g * 64:(g + 1) * 64], op0=ALU.add, op1=ALU.mult),
                            [R(pm), R(bsT), R(u_sb)], [R(TM)])
                    yield
                    ACT(lambda e: e.activation(out=sg[0:P, :], in_=hb1[0:P, 256:512], func=AF.Sigmoid), [R(hb1)], [R(sg)])
                    DVE(lambda e: e.tensor_tensor(out=z32[0:P, :], in0=hb1[0:P, 0:256], in1=sg[0:P, :], op=ALU.mult),
                        [R(hb1), R(sg)], [R(z32)])
                    POOL(lambda e: e.tensor_copy(out=TM[0:P, 2:4, :].rearrange("p a b -> p (a b)"), in_=z32[0:P, :]), [R(z32), R(TM)], [R(TM)])
                    if blk == G.NBLK - 1 and s == NSUB - 1:
                        if NP > 0 and not G.remote:
                            DMA("oc", G.o_conv[l, 14:30, :], z32[0:16, :], [R(z32)], [])
                        else:
                            DMA("oc", G.o_conv[l, :, :], z32[P - 30:P, :], [R(z32)], [])
                    yield
                    ACT(lambda e: e.activation(out=junk[0:P, 0:256], in_=hb2[0:P, 0:256], func=AF.Square, accum_out=ss[0:P, 2:3]),
                        [R(hb2)], [R(junk), ssx + "ss2"])
                    rsqrt_inplace(ss[0:P, 2:3], 256, [ssx + "ss2"])
                    ACT(lambda e: e.activation(out=TM[0:P, 4:6, :].rearrange("p a b -> p (a b)"), in_=hb2[0:P, 0:256], func=AF.Copy, scale=ss[0:P, 2:3]),
                        [R(hb2), ssx + "ss2", R(TM)], [R(TM)])
                    yield
                    ACT(lambda e: e.activation(out=junk[0:P, 0:128], in_=hb2[0:P, 256:384], func=AF.Square, accum_out=ss[0:P, 3:4]),
                        [R(hb2)], [R(junk), ssx + "ss3"])
                    rsqrt_inplace(ss[0:P, 3:4], 128, [ssx + "ss3"])
                    DVE(lambda e: e.scalar_tensor_tensor(out=lat32[0:P, :], in0=hb2[0:P, 256:384], scalar=ss[0:P, 3:4], in1=gkva_bc[0:P, :],
                                                         op0=ALU.mult, op1=ALU.mult),
                        [R(hb2), ssx + "ss3", R(gkva_bc)], [R(lat32)])
                    DMA("olat", G.o_lat[l, t0:t0 + P, :], lat32[0:P, :], [R(lat32)], [])
                    POOL(lambda e: e.tensor_copy(out=TM[0:P, 6, :], in_=lat32[0:P, :]), [R(lat32), R(TM)], [R(TM)])
                    yield
                    ACT(lambda e: e.activation(out=junk[0:P, 0:32], in_=hb2[0:P, 384:416], func=AF.Square, accum_out=ss[0:P, 4:5]),
                        [R(hb2)], [R(junk), ssx + "ss4"])
                    rsqrt_inplace(ss[0:P, 4:5], 32, [ssx + "ss4"])
                    DVE(lambda e: e.scalar_tensor_tensor(out=kr32[0:P, :], in0=hb2[0:P, 384:416], scalar=ss[0:P, 4:5], in1=gkr_bc[0:P, :],
                                                         op0=ALU.mult, op1=ALU.mult),
                        [R(hb2), ssx + "ss4", R(gkr_bc)], [R(kr32)])
                    yield
                    rope(kro32[0:P, 0:16], kro32[0:P, 16:32], kr32[0:P, 0:16], kr32[0:P, 16:32], cst[0:P, 0:16], cst[0:P, 16:32],
                         [(rt[i][0:P, 0, :], R(rt[i])) for i in range(4)], [R(kr32), R(cst)], [R(kro32)])
                    DMA("okr", G.o_kr[l, t0:t0 + P, :], kro32[0:P, :], [R(kro32)], [])
                    POOL(lambda e: e.tensor_copy(out=TM[0:P, 7, 64:96], in_=kro32[0:P, :]), [R(kro32), R(TM)], [R(TM)])
                    yield
                    transposes(pbB, TM, 8, P, [R(TM)])
                    DVE(lambda e, s=s: e.tensor_copy(out=mixblk[:, 0:2, s * P:(s + 1) * P], in_=pbB[:, 0:2, 0:P]), [R(pbB)], [R(mixblk)])
                    DVE(lambda e, s=s: e.tensor_copy(out=zTb[:, :, 30 + s * P:30 + (s + 1) * P], in_=pbB[:, 2:4, 0:P]), [R(pbB)], [R(zTb)])
                    DVE(lambda e: e.tensor_copy(out=cqnT[:, :, 0:P], in_=pbB[:, 4:6, 0:P]), [R(pbB)], [R(cqnT)])
                    DVE(lambda e, t0=t0: e.tensor_copy(out=latT[:, NP + t0:NP + t0 + P], in_=pbB[:, 6, 0:P]), [R(pbB)], [R(latT)])
                    DVE(lambda e, t0=t0: e.tensor_copy(out=krT[:, NP + t0:NP + t0 + P], in_=pbB[:, 7, 0:P]), [R(pbB)], [R(krT)])
                    yield
                    for (hb, c0, cw) in [(hb0, 0, 512), (hb1, 512, 256)]:
                        for k in range(2):
                            PE(lambda e, hb=hb, k=k, c0=c0, cw=cw: e.matmul(hb[0:P, 0:cw], lhsT=cqnT[:, k, 0:P], rhs=Wuq[:, k, c0:c0 + cw],
                                                                            start=(k == 0), stop=(k == 1)),
                               [R(cqnT), R(Wuq)], [R(hb)], sig=(k == 1))
                    yield
                    ACT(lambda e: e.activation(out=sq[0:P, 0:512], in_=hb0[0:P, :], func=AF.Square), [R(hb0)], [R(sq)])
                    ACT(lambda e: e.activation(out=sq[0:P, 512:768], in_=hb1[0:P, 0:256], func=AF.Square), [R(hb1)], [R(sq)])
                    DVE(lambda e: e.tensor_reduce(out=ssn[0:P, 0:8], in_=sq[0:P, 0:512].rearrange("p (h d) -> p h d", d=64), axis=AX.X, op=ALU.add),
                        [R(sq)], [ssx + "ssn"])
                    DVE(lambda e: e.tensor_reduce(out=ssn[0:P, 8:16], in_=sq[0:P, 512:768].rearrange("p (h d) -> p h d", d=32), axis=AX.X, op=ALU.add),
                        [R(sq)], [ssx + "ssr"])
                    yield
                    rsqrt_inplace(ssn[0:P, 0:8], 64, [ssx + "ssn"])
                    rsqrt_inplace(ssn[0:P, 8:16], 32, [ssx + "ssr"])
                    DVE(lambda e: e.tensor_tensor(out=qn32[0:P], in0=hb0[0:P, :].rearrange("p (h d) -> p h d", d=64),
                                                  in1=ssn[0:P, 0:8].unsqueeze(2).to_broadcast([P, 8, 64]), op=ALU.mult),
                        [R(hb0), ssx + "ssn"], [R(qn32)])
                    POOL(lambda e: e.tensor_tensor(out=Qtm[0:P, :, 0:64], in0=qn32[0:P], in1=gqn_bc[0:P, :].unsqueeze(1).to_broadcast([P, 8, 64]), op=ALU.mult),
                         [R(qn32), R(gqn_bc), R(Qtm)], [R(Qtm)])
                    yield
                    DVE(lambda e: e.tensor_tensor(out=qr32[0:P], in0=hb1[0:P, 0:256].rearrange("p (h d) -> p h d", d=32),
                                                  in1=ssn[0:P, 8:16].unsqueeze(2).to_broadcast([P, 8, 32]), op=ALU.mult),
                        [R(hb1), ssx + "ssr"], [R(qr32)])
                    POOL(lambda e: e.tensor_tensor(out=qr32[0:P], in0=qr32[0:P], in1=gqr_bc[0:P, :].unsqueeze(1).to_broadcast([P, 8, 32]), op=ALU.mult),
                         [R(qr32), R(gqr_bc)], [R(qr32)])
                    yield
                    cosb = cst[0:P, 0:16].unsqueeze(1).to_broadcast([P, 8, 16])
                    sinb = cst[0:P, 16:32].unsqueeze(1).to_broadcast([P, 8, 16])
                    rope(Qtm[0:P, :, 64:80], Qtm[0:P, :, 80:96], qr32[0:P, :, 0:16], qr32[0:P, :, 16:32], cosb, sinb,
                         [(rt[i][0:P], R(rt[i])) for i in range(4)], [R(qr32), R(cst)], [R(Qtm)])
                    yield
                    transposes(pbA, Qtm, 8, P, [R(Qtm)])
                    DVE(lambda e, s=s: e.tensor_copy(out=QTblk[:, :, s * P:(s + 1) * P], in_=pbA[:, :, 0:P]), [R(pbA)], [R(QTblk)])
                gens = [subtile(blk, s_, (blk * NSUB + s_) % 2) for s_ in range(NSUB)]
                active = [gens.pop(0)]
                for _ in range(8):
                    next(active[0])
                while gens or active:
                    if len(active) < 2 and gens:
                        active.append(gens.pop(0))
                    for g_ in list(active):
                        try:
                            next(g_)
                        except StopIteration:
                            active.remove(g_)
                def conv_epilogue(zsrc, dst, dslot):
                    for ft in range(2):
                        cv = (cv0, cv1)[ft]
                        for k in range(31):
                            PE(lambda e, cv=cv, ft=ft, k=k: e.matmul(cv[:, 0:BS], lhsT=cdiag[:, ft, k, :], rhs=zsrc[:, ft, k:k + BS],
                                                                     start=(k == 0), stop=(k == 30)),
                               [R(cdiag), R(zsrc)], [R(cv)], sig=(k == 30))
                        ACT(lambda e, cv=cv, ft=ft: e.activation(out=y32[:, ft, 0:BS], in_=cv[:, 0:BS], func=AF.Identity, bias=bdw[:, ft:ft + 1]),
                            [R(cv), R(bdw)], [R(y32)])
                        POOL(lambda e, ft=ft: e.tensor_copy(out=ybf[:, ft, 0:BS], in_=y32[:, ft, 0:BS]), [R(y32)], [R(ybf)])
                        ACT(lambda e, ft=ft: e.activation(out=ysq[:, ft, 0:BS], in_=y32[:, ft, 0:BS], func=AF.Square), [R(y32)], [R(ysq)])
                    for ft in range(2):
                        PE(lambda e, ft=ft: e.matmul(hb0[:, 0:BS], lhsT=ones_bf[:], rhs=ybf[:, ft, 0:BS], start=(ft == 0), stop=(ft == 1)),
                           [R(ones_bf), R(ybf)], [R(hb0)], sig=(ft == 1))
                    for ft in range(2):
                        PE(lambda e, ft=ft: e.matmul(hb1[:, 0:BS], lhsT=ones_bf[:], rhs=ysq[:, ft, 0:BS], start=(ft == 0), stop=(ft == 1)),
                           [R(ones_bf), R(ysq)], [R(hb1)], sig=(ft == 1))
                    ACT(lambda e: e.activation(out=mean[:, 0:BS], in_=hb0[:, 0:BS], func=AF.Copy, scale=1.0 / 256), [R(hb0)], [R(mean)])
                    POOL(lambda e: e.tensor_tensor(out=msq[:, 0:BS], in0=mean[:, 0:BS], in1=mean[:, 0:BS], op=ALU.mult), [R(mean)], [R(msq)])
                    DVE(lambda e: e.scalar_tensor_tensor(out=rstd[:, 0:BS], in0=hb1[:, 0:BS], scalar=1.0 / 256, in1=msq[:, 0:BS],
                                                         op0=ALU.mult, op1=ALU.subtract),
                        [R(hb1), R(msq)], [R(rstd)])
                    ACT(lambda e: e.activation(out=rstd[:, 0:BS], in_=rstd[:, 0:BS], func=AF.Sqrt, bias=EPS, scale=1.0), [R(rstd)], [R(rstd)])
                    DVE(lambda e: e.reciprocal(out=rstd[:, 0:BS], in_=rstd[:, 0:BS]), [R(rstd)], [R(rstd)])
                    for ft in range(2):
                        POOL(lambda e, ft=ft: e.tensor_tensor(out=tln[:, 0:BS], in0=y32[:, ft, 0:BS], in1=mean[:, 0:BS], op=ALU.subtract),
                             [R(y32), R(mean)], [R(tln)])
                        DVE(lambda e: e.tensor_tensor(out=tln[:, 0:BS], in0=tln[:, 0:BS], in1=rstd[:, 0:BS], op=ALU.mult), [R(tln), R(rstd)], [R(tln)])
                        ACT(lambda e, ft=ft: e.activation(out=dst[:, dslot + ft, 0:BS], in_=tln[:, 0:BS], func=AF.Silu,
                                                          bias=lnb[:, ft:ft + 1], scale=lng[:, ft:ft + 1]),
                            [R(tln), R(lnb), R(lng)], [R(dst)])
                c0 = blk * BS
                if G.remote:
                    DMA("mixw", G.mixT[0:256, c0:c0 + BS].rearrange("(k p) t -> p k t", p=128), mixblk[:, 0:2, 0:BS], [R(mixblk)], [G.name + "mixT"])
                else:
                    conv_epilogue(zTb, mixblk, 2)
                    DMA("mixw", G.mixT[0:512, c0:c0 + BS].rearrange("(k p) t -> p k t", p=128), mixblk[:, :, 0:BS], [R(mixblk)], [G.name + "mixT"])
                DMA("qtw", G.QT[:, :, c0:c0 + BS].rearrange("h p t -> p h t"), QTblk[:, :, 0:BS], [R(QTblk)], [G.name + "QT"])
                if blk + 1 < G.NBLK and not G.remote:
                    POOL(lambda e: e.tensor_copy(out=zTb[:, :, 0:30], in_=zTb[:, :, BS:BS + 30]), [R(zTb)], [R(zTb)])
            if G.remote:
                xin = [t.ap() for t in G.xin]
                xout = [t.ap() for t in G.xout]
                DMA("xi0", xin[0][:, :], latT[:, NP:NP + T], [R(latT)], [G.name + "xin0"])
                DMA("xi1", xin[1][:, :], krT[64:96, NP:NP + T], [R(krT)], [G.name + "xin1"])
                for j in range(G.NBLK):
                    DMA("xi2", xin[2][:, j * 60:(j + 1) * 60].rearrange("p (a b) -> p a b", a=2), zTs[j][:, :, BS:BS + 30], [R(zTs[j])], [G.name + "xin2"])
                for i in range(3):
                    S.dma("cc", lambda e, i=i: e.collective_compute("AllGather", ALU.bypass, replica_groups=[[0, 1], [2, 3], [4, 5], [6, 7]],
                                                                    ins=[G.xin[i].ap().opt()], outs=[G.xout[i].ap().opt()]),
                          [G.name + f"xin{i}"], [G.name + f"xout{i}"], eng="pool", inc=1)
                DMA("xo0", latT[:, 0:2 * T].rearrange("p (r t) -> p r t", r=2), xout[0].rearrange("(r p) t -> p r t", p=128), [G.name + "xout0"], [R(latT)])
                DMA("xo1", krT[64:96, 0:2 * T].rearrange("p (r t) -> p r t", r=2), xout[1].rearrange("(r p) t -> p r t", p=32), [G.name + "xout1"], [R(krT)])
                DMA("xo2", tz[:, :, :], xout[2].rearrange("(r p) t -> p r t", p=128), [G.name + "xout2"], [R(tz)])
                for j in range(G.NBLK):
                    zj = zTs[j]
                    POOL(lambda e, j=j, zj=zj: e.tensor_single_scalar(out=zj[:, :, 0:30], in_=tz[:, 0, j * 60:(j + 1) * 60].rearrange("p (a b) -> p a b", a=2),
                                                                      scalar=flag[:, 0:1], op=ALU.mult),
                         [R(tz), R(flag), R(zj)], [R(zj)])
                    if j >= 1:
                        DVE(lambda e, j=j, zj=zj: e.scalar_tensor_tensor(out=zj[:, :, 0:30], in0=tz[:, 1, (j - 1) * 60:j * 60].rearrange("p (a b) -> p a b", a=2),
                                                                          scalar=nflag[:, 0:1], in1=zj[:, :, 0:30], op0=ALU.mult, op1=ALU.add),
                             [R(tz), R(nflag), R(zj)], [R(zj)])
                    mb0 = mixblk0_2[j % 2]
                    conv_epilogue(zj, mb0, 0)
                    DMA("mixw0" + str(j % 2), G.mixT[256:512, j * BS:(j + 1) * BS].rearrange("(k p) t -> p k t", p=128), mb0[:, :, 0:BS], [R(mb0)], [G.name + "mixT"])

        S.barrier()
        AR.reset(BASE)
        Wkv_t = AR.alloc("Wkv", [128, 1, 1024], BF16)
        Wkv = Wkv_t[:, 0, :].rearrange("p (h c) -> p h c", c=128)
        gkn = AR.alloc("gkn", [128, 1], F32)
        load_w(Wkv_t, W["c_w_ukv"][l], 1, 1024)
        DMA("small", gkn[0:64, :], W["c_kn_g"][l].rearrange("(p o) -> p o", o=1), [], [R(gkn)])
        S2 = AR.mark()
        sm_scale = 1.0 / math.sqrt(96.0)
        for G in GROUPS:
            S.barrier()
            AR.reset(S2)
            P, BS, T, NP, NK = G.P, G.BS, G.T, G.NP, G.NK
            latT, krT = G.latT, G.krT
            ktiles = [(k0, min(128, NK - k0)) for k0 in range(0, NK, 128)]
            NKT = len(ktiles)
            KT2 = [[AR.alloc(f"KT{e}", [128, NK], BF16) for e in range(2)] for _ in range(2)]
            Vp2 = [AR.alloc("Vp", [128, NKT, 2, 66], BF16) for _ in range(2)]
            QTs2 = [[AR.alloc(f"QTs{e}", [128, T], BF16) for e in range(2)] for _ in range(2)]
            sqb1 = AR.alloc("sqb", [128, 512], BF16)
            sd1 = AR.alloc("sd", [128, 512], F32)
            PT = [AR.alloc(f"PT{i}", [128, 512], BF16) for i in range(6)]
            o_sb2 = [AR.alloc(f"o_sb{i}", [128, 512], F32) for i in range(2)]
            ycT = [AR.alloc(f"ycT{i}", [128, 512], BF16) for i in range(2)]
            pk, pst, pv, ps0, ps1, ps2 = PF
            SB = [p_[:].rearrange("p a b -> p (a b)").bitcast(F32) for p_ in PB] + [ps0, pv]
            pending_epi = []
            NRT = 0
            if G.remote:
                maskt = AR.alloc("maskt", [128, 8, 512], BF16)
                DMA("maskl", maskt[:], mask_d, [], [R(maskt)])
            for st_ in range(2):
                POOL(lambda e, st_=st_: e.memset(Vp2[st_][:], 1.0), [], [R(Vp2[st_])])
                if NRT:
                    POOL(lambda e, st_=st_: e.tensor_single_scalar(out=Vp2[st_][:, 0:NRT, :, 64:66], in_=Vp2[st_][:, 0:NRT, :, 64:66], scalar=flag[:, 0:1], op=ALU.mult),
                         [R(Vp2[st_]), R(flag)], [R(Vp2[st_])])
                for e_ in range(2):
                    POOL(lambda e, e_=e_, st_=st_: e.tensor_copy(out=KT2[st_][e_][64:96, :], in_=krT[64:96, :]), [R(krT)], [R(KT2[st_][e_])])
            pti = 0

            def kvmat(hg):
                st_ = hg % 2
                KT, Vp, QTs = KT2[st_], Vp2[st_], QTs2[st_]
                for e_ in range(2):
                    h = 2 * hg + e_
                    DMA("qtr" + str(st_) + str(e_), QTs[e_][:, :], G.QT[h], [G.name + "QT"], [R(QTs[e_])])
                    for k0 in range(0, NK, 512):
                        kw = min(512, NK - k0)
                        PE(lambda e, h=h, k0=k0, kw=kw: e.matmul(pk[0:64, 0:kw], lhsT=Wkv[:, h, 0:64], rhs=latT[:, k0:k0 + kw], start=True, stop=True),
                           [R(Wkv_t), R(latT)], [R(pk)])
                        ACT(lambda e, kw=kw: e.activation(out=sqb1[0:64, 0:kw], in_=pk[0:64, 0:kw], func=AF.Square), [R(pk)], [R(sqb1)])
                        PE(lambda e, kw=kw: e.matmul(pst[0:64, 0:kw], lhsT=ones_bf[0:64, 0:64], rhs=sqb1[0:64, 0:kw], start=True, stop=True),
                           [R(ones_bf), R(sqb1)], [R(pst)])
                        ACT(lambda e, kw=kw: e.activation(out=sd1[0:64, 0:kw], in_=pst[0:64, 0:kw], func=AF.Ln, bias=EPS, scale=1.0 / 64), [R(pst)], [R(sd1)])
                        ACT(lambda e, kw=kw: e.activation(out=sd1[0:64, 0:kw], in_=sd1[0:64, 0:kw], func=AF.Exp, scale=-0.5), [R(sd1)], [R(sd1)])
                        DVE(lambda e, e_=e_, k0=k0, kw=kw: e.scalar_tensor_tensor(out=KT[e_][0:64, k0:k0 + kw], in0=pk[0:64, 0:kw], scalar=gkn[0:64, 0:1],
                                                                                   in1=sd1[0:64, 0:kw], op0=ALU.mult, op1=ALU.mult),
                            [R(pk), R(gkn), R(sd1)], [R(KT[e_])])
                        yield
                for j0 in range(0, NKT, 4):
                    grp = ktiles[j0:j0 + 4]
                    for j, (k0, ksz) in enumerate(grp):
                        PE(lambda e, j=j, k0=k0, ksz=ksz, hg=hg: e.matmul(pk[0:ksz, j * 128:(j + 1) * 128], lhsT=latT[:, k0:k0 + ksz],
                                                                          rhs=Wkv[:, 2 * hg:2 * hg + 2, 64:128], start=True, stop=True),
                           [R(latT), R(Wkv_t)], [R(pk)])
                    full = [g_ for g_ in grp if g_[1] == 128]
                    if full:
                        n = len(full)
                        if j0 < NRT:
                            DVE(lambda e, j0=j0, n=n: e.tensor_single_scalar(out=Vp[:, j0:j0 + n, :, 0:64],
                                                                             in_=pk[:, 0:n * 128].rearrange("p (t h d) -> p t h d", h=2, d=64),
                                                                             scalar=flag[:, 0:1], op=ALU.mult),
                                [R(pk), R(flag)], [R(Vp)])
                        else:
                            DVE(lambda e, j0=j0, n=n: e.tensor_copy(out=Vp[:, j0:j0 + n, :, 0:64],
                                                                    in_=pk[:, 0:n * 128].rearrange("p (t h d) -> p t h d", h=2, d=64)),
                                [R(pk)], [R(Vp)])
                    for j, (k0, ksz) in enumerate(grp):
                        if ksz < 128:
                            DVE(lambda e, j=j, j0=j0, ksz=ksz: e.tensor_copy(out=Vp[0:ksz, j0 + j, :, 0:64],
                                                                             in_=pk[0:ksz, j * 128:(j + 1) * 128].rearrange("p (h d) -> p h d", d=64)),
                                [R(pk)], [R(Vp)])
                    yield

            for _ in kvmat(0):
                pass
            for hg in range(4):
                KT, Vp, QTs = KT2[hg % 2], Vp2[hg % 2], QTs2[hg % 2]
                nxt = kvmat(hg + 1) if hg + 1 < 4 else None
                ucount = 0
                for e_ in range(2):
                    h = 2 * hg + e_
                    for qb in range(G.NBLK):
                        q0 = qb * BS
                        if G.remote:
                            NT = T // 128
                            vis = []
                            for part in range(2):
                                for jb in range(qb + 1):
                                    for r in range(4):
                                        kt = part * NT + jb * 4 + r
                                        vis.append((kt, ktiles[kt][0], 128, (part * 4 + r) if jb == qb else None))
                        elif NP > 0:
                            vis = [(kt, k0, ksz, None) for kt, (k0, ksz) in enumerate(ktiles)]
                        else:
                            vis = []
                            for kt, (k0, ksz) in enumerate(ktiles):
                                if k0 >= q0 + BS:
                                    break
                                vis.append((kt, k0, ksz, (k0 - q0) // 128 if k0 >= q0 else None))
                        acc = (ps1, ps2)[(h * G.NBLK + qb) % 2]
                        n = len(vis)
                        LA = 3
                        units = []
                        for i in range(n + LA):
                            if i < n:
                                kt, k0, ksz, dg = vis[i]
                                c0 = dg * 128 if (dg is not None and not G.remote) else 0
                                pt = PT[pti % len(PT)]
                                sb = SB[pti % len(SB)]
                                pti += 1
                                units.append((pt, kt, ksz, c0))
                                PE(lambda e, e_=e_, k0=k0, ksz=ksz, c0=c0, q0=q0, sb=sb: e.matmul(sb[0:ksz, c0:BS], lhsT=KT[e_][0:96, k0:k0 + ksz],
                                                                                                  rhs=QTs[e_][0:96, q0 + c0:q0 + BS], start=True, stop=True),
                                   [R(KT[e_]), R(QTs[e_])], [R(sb)])
                                ACT(lambda e, pt=pt, ksz=ksz, c0=c0, sb=sb: e.activation(out=pt[0:ksz, c0:BS], in_=sb[0:ksz, c0:BS], func=AF.Exp, scale=sm_scale),
                                    [R(sb)], [R(pt)])
                                if dg is not None and G.remote:
                                    DVE(lambda e, pt=pt, dg=dg: e.tensor_tensor(out=pt[:, 0:BS], in0=pt[:, 0:BS], in1=maskt[:, dg, 0:BS], op=ALU.mult),
                                        [R(pt), R(maskt)], [R(pt)])
                                elif dg is not None:
                                    POOL(lambda e, pt=pt, c0=c0: e.memset(pt[64:128, c0:c0 + 64], 0.0), [R(pt)], [R(pt)])
                            j = i - LA
                            if j >= 0:
                                pt, kt, ksz, c0 = units[j]
                                PE(lambda e, pt=pt, kt=kt, ksz=ksz, c0=c0, e_=e_, acc=acc, j=j, n=n: e.matmul(
                                    acc[0:65, c0:BS], lhsT=Vp[0:ksz, kt, e_, 0:65], rhs=pt[0:ksz, c0:BS], start=(j == 0), stop=(j == n - 1)),
                                   [R(Vp), R(pt)], [R(acc)])
                            if i == min(14, n - 1) and pending_epi:
                                pending_epi.pop(0)()
                            ucount += 1
                            if nxt is not None and ucount % 10 == 0:
                                if next(nxt, "done") == "done":
                                    nxt = None

                        par = (h * G.NBLK + qb) % 2
                        yc, osb = ycT[par], o_sb2[par]
                        rd = osb
                        DVE(lambda e: e.tensor_copy(out=osb[0:65, 0:BS], in_=acc[0:65, 0:BS]), [R(acc)], [R(osb)])
                        DVE(lambda e: e.reciprocal(out=rd[64:65, 0:BS], in_=osb[64:65, 0:BS]), [R(osb)], [R(rd)])

                        def epilogue(h=h, q0=q0, par=par, yc=yc, osb=osb, rd=rd):
                            PE(lambda e: e.matmul(pst[0:64, 0:BS], lhsT=ones_f[64:65, 0:64], rhs=rd[64:65, 0:BS], start=True, stop=True),
                               [R(ones_f), R(rd)], [R(pst)])
                            DVE(lambda e: e.tensor_tensor(out=yc[0:64, 0:BS], in0=pst[0:64, 0:BS], in1=osb[0:64, 0:BS], op=ALU.mult),
                                [R(pst), R(osb)], [R(yc)])
                            DMA("ycw" + str(par), G.mixT[512 + h * 64:512 + (h + 1) * 64, q0:q0 + BS], yc[0:64, 0:BS], [R(yc)], [G.name + "mixT"])
                        pending_epi.append(epilogue)
                while pending_epi:
                    pending_epi.pop(0)()
                if nxt is not None:
                    for _ in nxt:
                        pass

        S.barrier()
        AR.reset(BASE0)
        Wout = AR.alloc("Wout", [128, 8, D], BF16)
        Wmq = AR.alloc("Wmq", [128, 8, 512], BF16)
        Wmo = AR.alloc("Wmo", [128, 4, D], BF16)
        Wmk = AR.alloc("Wmk", [128, 8, 512], BF16)
        Wmv = AR.alloc("Wmv", [128, 8, 512], BF16)
        g_nm = AR.alloc("g_nm", [128, 8], F32)
        g_mn = AR.alloc("g_mn", [128, 8], F32)
        g_mq = AR.alloc("g_mq", [128, 1], F32)
        gmk_bc = AR.alloc("gmk_bc", [128, 128], F32)
        load_gain_pk(g_nm, W["norm_mem_g"][l], 8)
        load_gain_pk(g_mn, W["mem_norm_g"][l], 8)
        DMA("small", g_mq[:, :], W["m_q_g"][l].rearrange("(p o) -> p o", o=1), [], [R(g_mq)])
        load_bcast(gmk_bc, W["m_k_g"][l], 128)
        load_w(Wmk, W["w_mk"][l], 8, 512, gain=g_mn, gname=R(g_mn))
        load_w(Wmv, W["w_mv"][l], 8, 512, gain=g_mn, gname=R(g_mn))
        load_w(Wout, W["w_out"][l], 8, D)
        load_w(Wmq, W["w_mq"][l], 8, 512, gain=g_nm, gname=R(g_nm))
        load_w(Wmo, W["w_mo"][l], 4, D)
        mm_scale = 1.0 / math.sqrt(128.0)
        S3 = AR.mark()
        for G in GROUPS:
            S.barrier()
            AR.reset(S3)
            P, BS, NSUB, T = G.P, G.BS, G.NSUB, G.T
            memKT = AR.alloc("memKT", [128, 4, 256], BF16)
            memV = AR.alloc("memV", [128, 2, 512], BF16)
            mx = AR.alloc("mx", [128, 2, D], F32)
            mxn = AR.alloc("mxn", [128, 8, 128], BF16)
            mnT = AR.alloc("mnT", [128, 8, 256], BF16)
            junk = AR.alloc("junk", [128, D], F32)
            ss = AR.alloc("ss", [128, 8], F32)
            mk32 = AR.alloc("mk32", [128, 512], F32)
            mkb = AR.alloc("mkb", [128, 4, 128], BF16)
            mv32 = AR.alloc("mv32", [128, 512], F32)
            pA, pB, pC, pD, pE_, pF_ = PF
            pbA, pbB = PB
            for s in range(2):
                if G is GP:
                    DMA("mx", mx[:, s, :], mem_in[s * 128:(s + 1) * 128, :], [], [R(mx)])
                    rms_to_bf16(mx[:, s, :], mxn[:].rearrange("p a b -> p (a b)"), junk[:, :], ss[:, 0:1], 128, D, [R(mx)], [R(junk), "ss0", R(mxn)])
                    transposes(pbA, mxn, 8, 128, [R(mxn)])
                    DVE(lambda e, s=s: e.tensor_copy(out=mnT[:, :, s * 128:(s + 1) * 128], in_=pbA[:, :, 0:128]), [R(pbA)], [R(mnT)])
                    for k in range(8):
                        PE(lambda e, k=k, s=s: e.matmul(pA[:, :], lhsT=mnT[:, k, s * 128:(s + 1) * 128], rhs=Wmk[:, k, :], start=(k == 0), stop=(k == 7)),
                           [R(mnT), R(Wmk)], [R(pA)], sig=(k == 7))
                    for k in range(8):
                        PE(lambda e, k=k, s=s: e.matmul(pB[:, :], lhsT=mnT[:, k, s * 128:(s + 1) * 128], rhs=Wmv[:, k, :], start=(k == 0), stop=(k == 7)),
                           [R(mnT), R(Wmv)], [R(pB)], sig=(k == 7))
                    ACT(lambda e: e.activation(out=junk[:, 0:512], in_=pA[:, :], func=AF.Square), [R(pA)], [R(junk)])
                    DVE(lambda e: e.tensor_reduce(out=ss[:, 4:8], in_=junk[:, 0:512].rearrange("p (h d) -> p h d", d=128), axis=AX.X, op=ALU.add),
                        [R(junk)], ["ss4"])
                    rsqrt_inplace(ss[:, 4:8], 128, ["ss4"])
                    DVE(lambda e: e.tensor_tensor(out=mk32[:].rearrange("p (h d) -> p h d", d=128), in0=pA[:, :].rearrange("p (h d) -> p h d", d=128),
                                                  in1=ss[:, 4:8].unsqueeze(2).to_broadcast([128, 4, 128]), op=ALU.mult),
                        [R(pA), "ss4"], [R(mk32)])
                    POOL(lambda e: e.tensor_tensor(out=mk32[:].rearrange("p (h d) -> p h d", d=128), in0=mk32[:].rearrange("p (h d) -> p h d", d=128),
                                                   in1=gmk_bc[:, :].unsqueeze(1).to_broadcast([128, 4, 128]), op=ALU.mult),
                         [R(mk32), R(gmk_bc)], [R(mk32)])
                    ACT(lambda e: e.activation(out=mv32[:, :], in_=pB[:, :], func=AF.Copy), [R(pB)], [R(mv32)])
                    DMA("omk", o_mk_p[l, s * 128:(s + 1) * 128, :], mk32[:, :], [R(mk32)], [])
                    DMA("omv", o_mv_p[l, s * 128:(s + 1) * 128, :], mv32[:, :], [R(mv32)], [])
                else:
                    DMA("mx", mk32[:, :], c_mk[l, s * 128:(s + 1) * 128, :], [], [R(mk32)])
                    DMA("mx", mv32[:, :], c_mv[l, s * 128:(s + 1) * 128, :], [], [R(mv32)])
                POOL(lambda e: e.tensor_copy(out=mkb[:].rearrange("p a b -> p (a b)"), in_=mk32[:, :]), [R(mk32)], [R(mkb)])
                POOL(lambda e, s=s: e.tensor_copy(out=memV[:, s, :], in_=mv32[:, :]), [R(mv32)], [R(memV)])
                transposes(pbB, mkb, 4, 128, [R(mkb)])
                DVE(lambda e, s=s: e.tensor_copy(out=memKT[:, :, s * 128:(s + 1) * 128], in_=pbB[:, 0:4, 0:128]), [R(pbB)], [R(memKT)])
            xb2 = [AR.alloc("xb", [128, 4, D], F32) for _ in range(2)]
            mixb2 = [AR.alloc("mixb", [128, 8, 512], BF16) for _ in range(2)]
            xn2 = [AR.alloc("xn", [128, 8, 128], BF16) for _ in range(2)]
            xnT2 = [AR.alloc("xnT", [128, 8, 512], BF16) for _ in range(2)]
            junk2 = [AR.alloc("junk", [128, D], F32) for _ in range(2)]
            ssb2 = [AR.alloc("ssb", [128, 8], F32) for _ in range(2)]
            sqb2 = [AR.alloc("sqb", [128, 512], BF16) for _ in range(2)]
            sd2 = [AR.alloc("sd", [128, 512], F32) for _ in range(2)]
            qmn2 = [AR.alloc("qmn", [128, 512], BF16) for _ in range(2)]
            PTm2 = [[AR.alloc(f"PTm{i}", [128, 512], BF16) for i in range(2)] for _ in range(2)]
            rden2 = [AR.alloc("rden", [128, 512], F32) for _ in range(2)]
            omT2 = [AR.alloc("omT", [128, 4, 512], BF16) for _ in range(2)]
            XB = [PF[0:3], PF[3:6]]
            AUXF3 = [p_[:].rearrange("p a b -> p (a b)").bitcast(F32) for p_ in PB]
            x_src = G.x_in if l == 0 else G.x1

            def block3a(blk, par):
                xb, mixb, xn, xnT, junk, ss, sqb, sd, qmn, PTm, rden, omT = (xb2[par], mixb2[par], xn2[par], xnT2[par], junk2[par], ssb2[par],
                                                                              sqb2[par], sd2[par], qmn2[par], PTm2[par], rden2[par], omT2[par])
                X0, X1, X2 = XB[par]
                pbT, auxf = PB[par], AUXF3[par]
                ssn_ = f"ss3a{par}"
                c0 = blk * BS
                DMA("xb" + str(par), xb[0:P, 0:NSUB, :], x_src[c0:c0 + BS, :].rearrange("(s p) d -> p s d", p=P), [G.name + "x1"] if l else [], [R(xb)])
                DMA("mixr" + str(par), mixb[:, :, 0:BS], G.mixT[:, c0:c0 + BS].rearrange("(k p) t -> p k t", p=128), [G.name + "mixT"], [R(mixb)])
                yield
                for s in range(NSUB):
                    for nh in range(2):
                        acc = (X0, X1)[nh]
                        for k in range(8):
                            PE(lambda e, k=k, s=s, nh=nh, acc=acc: e.matmul(acc[0:P, :], lhsT=mixb[:, k, s * P:(s + 1) * P], rhs=Wout[:, k, nh * 512:(nh + 1) * 512],
                                                                            start=(k == 0), stop=(k == 7)),
                               [R(mixb), R(Wout)], [R(acc)], sig=(k == 7))
                        DVE(lambda e, s=s, nh=nh, acc=acc: e.tensor_tensor(out=xb[0:P, s, nh * 512:(nh + 1) * 512], in0=acc[0:P, :],
                                                                           in1=xb[0:P, s, nh * 512:(nh + 1) * 512], op=ALU.add),
                            [R(acc), R(xb)], [R(xb)])
                    yield
                    rms_to_bf16(xb[0:P, s, :], xn[0:P].rearrange("p a b -> p (a b)"), junk[0:P, :], ss[0:P, 0:1], P, D, [R(xb)], [R(junk), ssn_, R(xn)])
                    yield
                    transposes(pbT, xn, 8, P, [R(xn)])
                    DVE(lambda e, s=s: e.tensor_copy(out=xnT[:, :, s * P:(s + 1) * P], in_=pbT[:, :, 0:P]), [R(pbT)], [R(xnT)])
                    yield
                for h in range(4):
                    for k in range(8):
                        PE(lambda e, k=k, h=h: e.matmul(X1[:, 0:BS], lhsT=Wmq[:, k, h * 128:(h + 1) * 128], rhs=xnT[:, k, 0:BS], start=(k == 0), stop=(k == 7)),
                           [R(Wmq), R(xnT)], [R(X1)], sig=(k == 7))
                    ACT(lambda e: e.activation(out=sqb[:, 0:BS], in_=X1[:, 0:BS], func=AF.Square), [R(X1)], [R(sqb)])
                    yield
                    PE(lambda e: e.matmul(X2[:, 0:BS], lhsT=ones_bf[:, :], rhs=sqb[:, 0:BS], start=True, stop=True), [R(ones_bf), R(sqb)], [R(X2)])
                    ACT(lambda e: e.activation(out=sd[:, 0:BS], in_=X2[:, 0:BS], func=AF.Ln, bias=EPS, scale=1.0 / 128), [R(X2)], [R(sd)])
                    ACT(lambda e: e.activation(out=sd[:, 0:BS], in_=sd[:, 0:BS], func=AF.Exp, scale=-0.5), [R(sd)], [R(sd)])
                    yield
                    DVE(lambda e: e.scalar_tensor_tensor(out=qmn[:, 0:BS], in0=X1[:, 0:BS], scalar=g_mq[:, 0:1], in1=sd[:, 0:BS], op0=ALU.mult, op1=ALU.mult),
                        [R(X1), R(g_mq), R(sd)], [R(qmn)])
                    yield
                    for kt in range(2):
                        sc = (X1, X2)[kt]
                        PE(lambda e, kt=kt, h=h, sc=sc: e.matmul(sc[:, 0:BS], lhsT=memKT[:, h, kt * 128:(kt + 1) * 128], rhs=qmn[:, 0:BS], start=True, stop=True),
                           [R(memKT), R(qmn)], [R(sc)])
                        ACT(lambda e, kt=kt, sc=sc: e.activation(out=PTm[kt][:, 0:BS], in_=sc[:, 0:BS], func=AF.Exp, scale=mm_scale), [R(sc)], [R(PTm[kt])])
                    yield
                    for kt in range(2):
                        PE(lambda e, kt=kt, h=h: e.matmul(X0[:, 0:BS], lhsT=memV[:, kt, h * 128:(h + 1) * 128], rhs=PTm[kt][:, 0:BS], start=(kt == 0), stop=(kt == 1)),
                           [R(memV), R(PTm[kt])], [R(X0)], sig=(kt == 1))
                    for kt in range(2):
                        PE(lambda e, kt=kt: e.matmul(auxf[:, 0:BS], lhsT=ones_bf[:, :], rhs=PTm[kt][:, 0:BS], start=(kt == 0), stop=(kt == 1)),
                           [R(ones_bf), R(PTm[kt])], [R(auxf)], sig=(kt == 1))
                    yield
                    DVE(lambda e: e.reciprocal(out=rden[:, 0:BS], in_=auxf[:, 0:BS]), [R(auxf)], [R(rden)])
                    DVE(lambda e, h=h: e.tensor_tensor(out=omT[:, h, 0:BS], in0=X0[:, 0:BS], in1=rden[:, 0:BS], op=ALU.mult), [R(X0), R(rden)], [R(omT)])
                    yield
                for s in range(NSUB):
                    for nh in range(2):
                        acc = (X0, X1)[nh]
                        for h in range(4):
                            PE(lambda e, h=h, s=s, nh=nh, acc=acc: e.matmul(acc[0:P, :], lhsT=omT[:, h, s * P:(s + 1) * P], rhs=Wmo[:, h, nh * 512:(nh + 1) * 512],
                                                                            start=(h == 0), stop=(h == 3)),
                               [R(omT), R(Wmo)], [R(acc)], sig=(h == 3))
                        DVE(lambda e, s=s, nh=nh, acc=acc: e.tensor_tensor(out=xb[0:P, s, nh * 512:(nh + 1) * 512], in0=acc[0:P, :],
                                                                           in1=xb[0:P, s, nh * 512:(nh + 1) * 512], op=ALU.add),
                            [R(acc), R(xb)], [R(xb)])
                    yield
                DMA("xmw" + str(par), G.xm[c0:c0 + BS, :].rearrange("(s p) d -> p s d", p=P), xb[0:P, 0:NSUB, :], [R(xb)], [G.name + "xm"])
                for s in range(NSUB):
                    rms_to_bf16(xb[0:P, s, :], xn[0:P].rearrange("p a b -> p (a b)"), junk[0:P, :], ss[0:P, 0:1], P, D, [R(xb)], [R(junk), ssn_, R(xn)])
                    yield
                    transposes(pbT, xn, 8, P, [R(xn)])
                    DVE(lambda e, s=s: e.tensor_copy(out=xnT[:, :, s * P:(s + 1) * P], in_=pbT[:, :, 0:P]), [R(pbT)], [R(xnT)])
                    yield
                DMA("xnfw" + str(par), G.xnf[:, c0:c0 + BS].rearrange("(k p) t -> p k t", p=128), xnT[:, :, 0:BS], [R(xnT)], [G.name + "xnf"])

            gens = [block3a(blk, blk % 2) for blk in range(G.NBLK)]
            active = [gens.pop(0)]
            for _ in range(14):
                next(active[0])
            while gens or active:
                if len(active) < 2 and gens:
                    active.append(gens.pop(0))
                for g_ in list(active):
                    try:
                        next(g_)
                    except StopIteration:
                        active.remove(g_)

        S.barrier()
        AR.reset(BASE0)
        W1 = AR.alloc("W1", [128, 8, 4 * D], BF16)
        W2 = AR.alloc("W2", [128, 32, D], BF16)
        g_ff = AR.alloc("g_ff", [128, 8], F32)
        load_gain_pk(g_ff, W["norm_ffn_g"][l], 8)
        load_w(W1, W["w_ff1"][l], 8, 4 * D, gain=g_ff, gname=R(g_ff))
        load_w(W2, W["w_ff2"][l], 32, D)
        S4 = AR.mark()
        for G in GROUPS:
            S.barrier()
            AR.reset(S4)
            P, BS, NSUB, T = G.P, G.BS, G.NSUB, G.T
            xb = AR.alloc("xb", [128, 4, D], F32)
            xnT2b = [AR.alloc("xnT", [128, 8, 512], BF16) for _ in range(2)]
            rl = [AR.alloc(f"rl{i}", [128, 512], F32) for i in range(2)]
            h1T = AR.alloc("h1T", [128, 16, 512], BF16)
            pA, pB, pC, pD, pE_, pF_ = PF
            pbA, pbB = PB
            x_dst = G.x1 if l == 0 else G.y_out

            def load_xn(b_):
                if b_ < G.NBLK:
                    DMA("xnfr" + str(b_ % 2), xnT2b[b_ % 2][:, :, 0:BS], G.xnf[:, b_ * BS:(b_ + 1) * BS].rearrange("(k p) t -> p k t", p=128),
                        [G.name + "xnf"], [R(xnT2b[b_ % 2])])
            load_xn(0)
            for blk in range(G.NBLK):
                c0 = blk * BS
                xnT = xnT2b[blk % 2]
                load_xn(blk + 1)
                DMA("xb", xb[0:P, 0:NSUB, :], G.xm[c0:c0 + BS, :].rearrange("(s p) d -> p s d", p=P), [G.name + "xm"], [R(xb)])
                for fh in range(2):
                    for f in range(16):
                        ff = fh * 16 + f
                        pp = (pA, pB)[f % 2]
                        r_ = rl[f % 2]
                        for k in range(8):
                            PE(lambda e, k=k, ff=ff, pp=pp: e.matmul(pp[:, 0:BS], lhsT=W1[:, k, ff * 128:(ff + 1) * 128], rhs=xnT[:, k, 0:BS],
                                                                     start=(k == 0), stop=(k == 7)),
                               [R(W1), R(xnT)], [R(pp)], sig=(k == 7))
                        ACT(lambda e, pp=pp, r_=r_: e.activation(out=r_[:, 0:BS], in_=pp[:, 0:BS], func=AF.Relu), [R(pp)], [R(r_)])
                        DVE(lambda e, pp=pp, r_=r_, f=f: e.tensor_tensor(out=h1T[:, f, 0:BS], in0=pp[:, 0:BS], in1=r_[:, 0:BS], op=ALU.mult),
                            [R(pp), R(r_)], [R(h1T)])
                    for s in range(NSUB):
                        for nh in range(2):
                            pp = (pC, pD)[nh]
                            for f in range(16):
                                PE(lambda e, f=f, s=s, nh=nh, pp=pp, fh=fh: e.matmul(pp[0:P, :], lhsT=h1T[:, f, s * P:(s + 1) * P],
                                                                                     rhs=W2[:, fh * 16 + f, nh * 512:(nh + 1) * 512],
                                                                                     start=(f == 0), stop=(f == 15)),
                                   [R(h1T), R(W2)], [R(pp)], sig=(f == 15))
                            DVE(lambda e, s=s, nh=nh, pp=pp: e.tensor_tensor(out=xb[0:P, s, nh * 512:(nh + 1) * 512], in0=pp[0:P, :],
                                                                             in1=xb[0:P, s, nh * 512:(nh + 1) * 512], op=ALU.add),
                                [R(pp), R(xb)], [R(xb)])
                wr = [G.name + "x1"] if l == 0 else []
                DMA("xow", x_dst[c0:c0 + BS, :].rearrange("(s p) d -> p s d", p=P), xb[0:P, 0:NSUB, :], [R(xb)], wr)

    S.emit()
    nc._n_ops = S.n_ops
    nc._peak = AR.peak
    return nc


def _rope_table(pos):
    half = 16
    inv = (np.float32(10000.0) ** (-(np.arange(half, dtype=np.float32)) / np.float32(half))).astype(np.float32)
    ang = (pos.astype(np.float32)[:, None] * inv[None, :]).astype(np.float32)
    return np.concatenate([np.cos(ang.astype(np.float64)), np.sin(ang.astype(np.float64))], axis=1).astype(np.float32)


_NC_CACHE = {}


def kernel(**inp):
    inp = {k: np.asarray(v) for k, v in inp.items()}
    B, SEQ, _ = inp["x_prompt"].shape
    NB = inp["x_sample"].shape[0]
    past = inp["cache_mla_latent"].shape[2]
    ncores = 8
    assert B * 2 == ncores and NB == ncores
    TP = SEQ // 2
    BLK = 512
    NBC = TP // BLK
    if TP not in _NC_CACHE:
        _NC_CACHE[TP] = build_nc(TP)
    nc = _NC_CACHE[TP]
    ident = np.eye(128, dtype=np.float32).astype(ml_dtypes.bfloat16)
    tril = np.tril(np.ones((128, 128), np.float32))
    pos_half = [np.concatenate([(2 * j + c) * BLK + np.arange(BLK) for j in range(NBC)]) for c in range(2)]
    cs_half = [_rope_table(p) for p in pos_half]
    cs_s = _rope_table(past + np.arange(TS))
    kk = np.arange(128)[:, None]
    qq = np.arange(BLK)[None, :]
    diag = np.stack([((r * 128 + kk) // 64 <= qq // 64) for r in range(4)], axis=1).astype(np.float32)
    masks = [np.concatenate([diag, np.zeros_like(diag)], axis=1), np.concatenate([np.ones_like(diag), diag], axis=1)]
    masks = [m.astype(ml_dtypes.bfloat16) for m in masks]
    wnames = ["norm_mix_g", "w_in", "a_norm_g", "a_ws", "a_bs", "b_dw_w", "b_dw_b", "b_ln_g", "b_ln_b", "c_qa_g", "c_w_uq",
              "c_kva_g", "c_w_ukv", "c_qn_g", "c_qr_g", "c_kn_g", "c_kr_g", "w_out", "norm_mem_g", "mem_norm_g", "w_mq",
              "w_mk", "w_mv", "w_mo", "m_q_g", "m_k_g", "norm_ffn_g", "w_ff1", "w_ff2"]
    shared = {n: np.ascontiguousarray(inp[n], dtype=np.float32) for n in wnames}
    shared.update(ident=ident, tril=tril, cs_s=cs_s)
    in_maps = []
    for c in range(ncores):
        b, half = c // 2, c % 2
        m = dict(shared)
        m["x_p"] = np.ascontiguousarray(inp["x_prompt"][b][pos_half[half]])
        m["cs_p"] = cs_half[half]
        m["flag"] = np.full((128, 1), float(half), np.float32)
        m["mask"] = masks[half]
        m["x_s"] = np.ascontiguousarray(inp["x_sample"][c])
        m["mem"] = np.ascontiguousarray(inp["mem_prompt"][b])
        m["c_lat"] = np.ascontiguousarray(inp["cache_mla_latent"][:, c])
        m["c_kr"] = np.ascontiguousarray(inp["cache_mla_krope"][:, c])
        m["c_conv"] = np.ascontiguousarray(inp["cache_conv"][:, c])
        m["c_mk"] = np.ascontiguousarray(inp["cache_mem_k"][:, c]).reshape(DEPTH, 256, 512)
        m["c_mv"] = np.ascontiguousarray(inp["cache_mem_v"][:, c]).reshape(DEPTH, 256, 512)
        in_maps.append(m)
    res = run_bass_kernel_spmd(nc, in_maps, core_ids=list(range(ncores)))
    r = res.results

    def halves(name, axis):
        outs = []
        for b in range(B):
            a0, a1 = r[2 * b][name], r[2 * b + 1][name]
            full = np.empty(a0.shape[:axis] + (SEQ,) + a0.shape[axis + 1:], a0.dtype)
            idx = [slice(None)] * full.ndim
            for half, arr in ((0, a0), (1, a1)):
                idx[axis] = pos_half[half]
                full[tuple(idx)] = arr
            outs.append(full)
        return np.stack(outs)
    y_p = halves("y_p", 0)
    y_s = np.stack([r[c]["y_s"] for c in range(NB)])
    lat_p = np.moveaxis(halves("o_lat_p", 1), 0, 1)
    kr_p = np.moveaxis(halves("o_kr_p", 1), 0, 1)
    conv_p = np.stack([r[2 * b + 1]["o_conv_p"] for b in range(B)], axis=1)
    mk_p = np.stack([r[2 * b]["o_mk_p"] for b in range(B)], axis=1).reshape(DEPTH, B, 256, 4, 128)
    mv_p = np.stack([r[2 * b]["o_mv_p"] for b in range(B)], axis=1).reshape(DEPTH, B, 256, 4, 128)
    lat_s = np.stack([r[c]["o_lat_s"] for c in range(NB)], axis=1)
    kr_s = np.stack([r[c]["o_kr_s"] for c in range(NB)], axis=1)
    conv_s = np.stack([r[c]["o_conv_s"] for c in range(NB)], axis=1)
    gv_s = np.stack([r[c]["o_gv_s"] for c in range(NB)], axis=1)
    outs = (y_p, y_s, lat_p, kr_p, conv_p, mk_p, mv_p, lat_s, kr_s, conv_s, gv_s)
    return tuple(np.ascontiguousarray(o, dtype=np.float32) for o in outs)
```

```python
import contextlib
import math
import types
import numpy as np
import ml_dtypes
import concourse.bass as bass
import concourse.mybir as mybir
from concourse.bass_utils import run_bass_kernel_spmd

F32 = mybir.dt.float32
BF16 = mybir.dt.bfloat16
ALU = mybir.AluOpType
AF = mybir.ActivationFunctionType
AX = mybir.AxisListType

D = 1024
DEPTH = 2
EPS = 1e-6
IN_W = 1440
NPAST = 1024
TS = 16
SBUF_LO = 16512
SBUF_HI = 229344


def _freeze(fn):
    if fn.__closure__ is None:
        return fn
    cells = []
    for c in fn.__closure__:
        try:
            cells.append(types.CellType(c.cell_contents))
        except ValueError:
            cells.append(c)
    return types.FunctionType(fn.__code__, fn.__globals__, fn.__name__, fn.__defaults__, tuple(cells))


class Sched:
    ENGS = ("pe", "act", "dve", "pool", "sp")

    def __init__(self, nc):
        self.nc = nc
        self.q = {e: [] for e in self.ENGS}
        self.cnt = {}
        self.last_w = {}
        self.readers = {}
        self.seen = {e: {} for e in self.ENGS}
        self.pending = {e: {} for e in self.ENGS}
        self.n_ops = 0

    def _deps(self, eng, reads, writes):
        need = dict(self.pending[eng])
        writes = list(writes) + [r for r in reads if r.startswith("pf") or r.startswith("pb")]
        self.pending[eng] = {}

        def add(tok):
            if tok is not None and need.get(tok[0], 0) < tok[1]:
                need[tok[0]] = tok[1]
        for r in reads:
            add(self.last_w.get(r))
        for w in writes:
            add(self.last_w.get(w))
            for t in self.readers.get(w, ()):
                add(t)
        seen = self.seen[eng]
        waits = []
        if eng == "pe":
            need.pop("pe", None)
        for k, v in need.items():
            if seen.get(k, 0) < v:
                seen[k] = v
                waits.append((k, v))
        return waits

    def _commit(self, tok, reads, writes):
        writes = list(writes) + [r for r in reads if r.startswith("pf") or r.startswith("pb")]
        for r in reads:
            self.readers.setdefault(r, []).append(tok)
        for w in writes:
            self.last_w[w] = tok
            self.readers[w] = []

    def op(self, eng, fn, reads=(), writes=(), signal=True):
        waits = self._deps(eng, reads, writes)
        if signal:
            self.cnt[eng] = self.cnt.get(eng, 0) + 1
            tok = (eng, self.cnt[eng])
        else:
            tok = (eng, self.cnt.get(eng, 0) + 1)
        self.q[eng].append((_freeze(fn), waits, eng, 1 if signal else 0))
        self._commit(tok, reads, writes)
        self.n_ops += 1
        return tok

    def dma(self, tag, fn, reads=(), writes=(), eng="sp", inc=16):
        key = "dma:" + tag
        if self.cnt.get(key, 0):
            p = self.pending[eng]
            p[key] = max(p.get(key, 0), self.cnt[key])
        waits = self._deps(eng, reads, writes)
        self.cnt[key] = self.cnt.get(key, 0) + inc
        tok = (key, self.cnt[key])
        self.q[eng].append((_freeze(fn), waits, key, inc))
        self._commit(tok, reads, writes)
        self.n_ops += 1
        return tok

    def barrier(self):
        for e in self.ENGS:
            p = self.pending[e]
            for k, v in self.cnt.items():
                if p.get(k, 0) < v:
                    p[k] = v

    def emit(self):
        nc = self.nc
        with contextlib.ExitStack() as st:
            sems = {}
            for k in self.cnt:
                sems[k] = st.enter_context(nc.semaphore("s_" + k.replace(":", "_")))
            block = st.enter_context(nc.Block())
            fin = list(self.cnt.items())

            def run(name, e):
                for fn, waits, key, inc in self.q[name]:
                    for k, v in waits:
                        e.wait_ge(sems[k], v)
                    ins = fn(e)
                    if inc:
                        ins.then_inc(sems[key], inc)
                if name == "sp":
                    for k, v in fin:
                        e.wait_ge(sems[k], v)

            @block.tensor
            def _(e):
                run("pe", e)

            @block.scalar
            def _(e):
                run("act", e)

            @block.vector
            def _(e):
                run("dve", e)

            @block.gpsimd
            def _(e):
                run("pool", e)

            @block.sync
            def _(e):
                run("sp", e)


class Arena:
    def __init__(self, nc):
        self.nc = nc
        self.off = SBUF_LO
        self.n = 0
        self.peak = 0

    def alloc(self, name, shape, dt):
        per = 1
        for s in shape[1:]:
            per *= s
        nb = per * (4 if dt == F32 else 2)
        nb = (nb + 63) // 64 * 64
        assert self.off + nb <= SBUF_HI, f"SBUF overflow at {name}: {self.off + nb}"
        self.n += 1
        t = self.nc.alloc_sbuf_tensor_at(f"{name}_{self.n}", list(shape), dt, offset=self.off)
        self.off += nb
        self.peak = max(self.peak, self.off)
        return t

    def mark(self):
        return self.off

    def reset(self, m):
        self.off = m


class Grp:
    pass


def build_nc(TP):
    nc = bass.Bass("TRN2", target_bir_lowering=False)
    S = Sched(nc)
    AR = Arena(nc)

    def din(name, shape, dt=F32):
        return nc.dram_tensor(name, list(shape), dt, kind="ExternalInput").ap()

    def dout(name, shape, dt=F32):
        return nc.dram_tensor(name, list(shape), dt, kind="ExternalOutput").ap()

    def dscr(name, shape, dt):
        return nc.dram_tensor(name, list(shape), dt).ap()

    x_p = din("x_p", [TP, D]); x_s = din("x_s", [TS, D]); mem_in = din("mem", [256, D])
    c_lat = din("c_lat", [DEPTH, NPAST, 128]); c_kr = din("c_kr", [DEPTH, NPAST, 32])
    c_conv = din("c_conv", [DEPTH, 30, 256])
    c_mk = din("c_mk", [DEPTH, 256, 512]); c_mv = din("c_mv", [DEPTH, 256, 512])
    W = {}
    for nm, shp in [("norm_mix_g", [DEPTH, D]), ("w_in", [DEPTH, D, IN_W]), ("a_norm_g", [DEPTH, 256]),
                    ("a_ws", [DEPTH, 4, 128, 128]), ("a_bs", [DEPTH, 4, 128]), ("b_dw_w", [DEPTH, 31, 256]),
                    ("b_dw_b", [DEPTH, 256]), ("b_ln_g", [DEPTH, 256]), ("b_ln_b", [DEPTH, 256]),
                    ("c_qa_g", [DEPTH, 256]), ("c_w_uq", [DEPTH, 256, 768]), ("c_kva_g", [DEPTH, 128]),
                    ("c_w_ukv", [DEPTH, 128, 1024]), ("c_qn_g", [DEPTH, 64]), ("c_qr_g", [DEPTH, 32]),
                    ("c_kn_g", [DEPTH, 64]), ("c_kr_g", [DEPTH, 32]), ("w_out", [DEPTH, D, D]),
                    ("norm_mem_g", [DEPTH, D]), ("mem_norm_g", [DEPTH, D]), ("w_mq", [DEPTH, D, 512]),
                    ("w_mk", [DEPTH, D, 512]), ("w_mv", [DEPTH, D, 512]), ("w_mo", [DEPTH, 512, D]),
                    ("m_q_g", [DEPTH, 128]), ("m_k_g", [DEPTH, 128]), ("norm_ffn_g", [DEPTH, D]),
                    ("w_ff1", [DEPTH, D, 4 * D]), ("w_ff2", [DEPTH, 4 * D, D])]:
        W[nm] = din(nm, shp)
    ident_d = din("ident", [128, 128], BF16)
    tril_d = din("tril", [128, 128])
    cs_p = din("cs_p", [TP, 32]); cs_s = din("cs_s", [TS, 32])
    flag_d = din("flag", [128, 1])
    mask_d = din("mask", [128, 8, 512], BF16)

    y_p = dout("y_p", [TP, D]); y_s = dout("y_s", [TS, D])
    o_lat_p = dout("o_lat_p", [DEPTH, TP, 128]); o_kr_p = dout("o_kr_p", [DEPTH, TP, 32])
    o_conv_p = dout("o_conv_p", [DEPTH, 30, 256])
    o_mk_p = dout("o_mk_p", [DEPTH, 256, 512]); o_mv_p = dout("o_mv_p", [DEPTH, 256, 512])
    o_lat_s = dout("o_lat_s", [DEPTH, TS, 128]); o_kr_s = dout("o_kr_s", [DEPTH, TS, 32])
    o_conv_s = dout("o_conv_s", [DEPTH, 30, 256]); o_gv_s = dout("o_gv_s", [DEPTH, TS, 256])

    def mk_group(name, T, NP, x_in, cs, y_out, o_lat, o_kr, o_conv):
        G = Grp()
        G.name, G.T, G.NP = name, T, NP
        G.P = min(128, T); G.BS = min(512, T); G.NSUB = G.BS // G.P; G.NBLK = T // G.BS
        G.NK = NP + T
        G.x_in, G.cs, G.y_out, G.o_lat, G.o_kr, G.o_conv = x_in, cs, y_out, o_lat, o_kr, o_conv
        G.QT = dscr("QT_" + name, [8, 128, T], BF16)
        G.mixT = dscr("mixT_" + name, [D, T], BF16)
        G.x1 = dscr("x1_" + name, [T, D], F32)
        G.xm = dscr("xm_" + name, [T, D], F32)
        G.xnf = dscr("xnf_" + name, [D, T], BF16)
        G.remote = False
        return G
    GP = mk_group("p", TP, TP, x_p, cs_p, y_p, o_lat_p, o_kr_p, o_conv_p)
    GP.remote = True
    NBP = TP // 512
    GP.xin = [nc.dram_tensor("xin_l", [128, TP], BF16), nc.dram_tensor("xin_k", [32, TP], BF16), nc.dram_tensor("xin_z", [128, NBP * 60], BF16)]
    GP.xout = [nc.dram_tensor("xout_l", [256, TP], BF16), nc.dram_tensor("xout_k", [64, TP], BF16), nc.dram_tensor("xout_z", [256, NBP * 60], BF16)]
    GS = mk_group("s", TS, NPAST, x_s, cs_s, y_s, o_lat_s, o_kr_s, o_conv_s)
    GROUPS = [GP, GS]

    PF = [nc.alloc_psum_tensor(f"pf{i}", [128, 512], F32) for i in range(6)]
    PB = [nc.alloc_psum_tensor(f"pb{i}", [128, 8, 128], BF16) for i in range(2)]

    rot = {"i": 0}

    def R(t):
        return t.name

    def PE(fn, r, w, sig=True):
        return S.op("pe", fn, r, w, signal=sig)

    def ACT(fn, r, w):
        return S.op("act", fn, r, w)

    def DVE(fn, r, w):
        return S.op("dve", fn, r, w)

    def POOL(fn, r, w):
        return S.op("pool", fn, r, w)

    def DMA(tag, out, in_, r, w, eng="sp", slow=False):
        if slow:
            return S.dma(tag, lambda e: e.dma_start(out=out, in_=in_, allow_slow_non_contiguous=True), r, w, eng)
        return S.dma(tag, lambda e: e.dma_start(out=out, in_=in_), r, w, eng)

    def rsqrt_inplace(ap, n, names):
        ACT(lambda e: e.activation(out=ap, in_=ap, func=AF.Ln, bias=EPS, scale=1.0 / n), names, names)
        ACT(lambda e: e.activation(out=ap, in_=ap, func=AF.Exp, scale=-0.5), names, names)

    ident = AR.alloc("ident", [128, 128], BF16)
    ident_f = AR.alloc("identf", [128, 128], F32)
    tril = AR.alloc("tril", [128, 128], F32)
    ones_bf = AR.alloc("ones_bf", [128, 128], BF16)
    ones_f = AR.alloc("ones_f", [128, 64], F32)
    stg = [AR.alloc(f"stg{i}", [128, 1024], F32) for i in range(2)]
    DMA("c0", ident[:], ident_d, [], [R(ident)])
    DMA("c0", tril[:], tril_d, [], [R(tril)])
    flag = AR.alloc("flag", [128, 1], F32)
    nflag = AR.alloc("nflag", [128, 1], F32)
    DMA("c0", flag[:], flag_d, [], [R(flag)])
    POOL(lambda e: e.tensor_scalar(out=nflag[:], in0=flag[:], scalar1=-1.0, scalar2=1.0, op0=ALU.mult, op1=ALU.add), [R(flag)], [R(nflag)])
    POOL(lambda e: e.memset(ones_bf[:], 1.0), [], [R(ones_bf)])
    POOL(lambda e: e.memset(ones_f[:], 1.0), [], [R(ones_f)])
    POOL(lambda e: e.tensor_copy(out=ident_f[:], in_=ident[:]), [R(ident)], [R(ident_f)])
    BASE0 = AR.mark()

    cast_rot = {"i": 0}

    def cast(out, in_, r, w, gain=None):
        i = cast_rot["i"] = (cast_rot["i"] + 1) % 3
        if gain is None:
            if i <= 1:
                DVE(lambda e: e.tensor_copy(out=out, in_=in_), r, w)
            else:
                ACT(lambda e: e.activation(out=out, in_=in_, func=AF.Copy), r, w)
        else:
            if i == 0:
                DVE(lambda e: e.tensor_single_scalar(out=out, in_=in_, scalar=gain, op=ALU.mult), r, w)
            elif i == 1:
                DVE(lambda e: e.tensor_single_scalar(out=out, in_=in_, scalar=gain, op=ALU.mult), r, w)
            else:
                ACT(lambda e: e.activation(out=out, in_=in_, func=AF.Copy, scale=gain), r, w)

    stg_i = {"i": 0}

    def load_w(dst, src2d, KT, N, gain=None, gname=None):
        CW = min(N, 1024)
        for k in range(KT):
            for c0 in range(0, N, CW):
                cw = min(CW, N - c0)
                st = stg[stg_i["i"] % 2]
                stg_i["i"] += 1
                DMA("stg" + str(stg_i["i"] % 2), st[:, 0:cw], src2d[k * 128:(k + 1) * 128, c0:c0 + cw], [], [R(st)])
                rr = [R(st)] + ([gname] if gain is not None else [])
                cast(dst[:, k, c0:c0 + cw], st[:, 0:cw], rr, [R(dst)],
                     gain=None if gain is None else gain[:, k:k + 1])

    def load_gain_pk(dst, vec, KT):
        DMA("small", dst[:, 0:KT], vec.rearrange("(k p) -> p k", p=128), [], [R(dst)], slow=True)

    def load_bcast(dst, vec, n, P=128):
        DMA("small", dst[0:P, 0:n], vec.partition_broadcast(P), [], [R(dst)])

    def transposes(pb, src, nslot, P, r):
        for j in range(nslot):
            PE(lambda e, j=j: e.transpose(out=pb[:, j, 0:P], in_=src[0:P, j, :], identity=ident[0:P, 0:P]),
               r + [R(ident)], [R(pb)], sig=(j == nslot - 1))

    def rope(out1, out2, x1, x2, cos, sin, tmp, r, w):
        (t1, n1), (t2, n2), (t3, n3), (t4, n4) = tmp
        DVE(lambda e: e.tensor_tensor(out=t1, in0=x1, in1=cos, op=ALU.mult), r, [n1])
        POOL(lambda e: e.tensor_tensor(out=t2, in0=x2, in1=sin, op=ALU.mult), r, [n2])
        DVE(lambda e: e.tensor_tensor(out=t3, in0=x1, in1=sin, op=ALU.mult), r, [n3])
        POOL(lambda e: e.tensor_tensor(out=t4, in0=x2, in1=cos, op=ALU.mult), r, [n4])
        DVE(lambda e: e.tensor_tensor(out=out1, in0=t1, in1=t2, op=ALU.subtract), [n1, n2] + w, w)
        POOL(lambda e: e.tensor_tensor(out=out2, in0=t3, in1=t4, op=ALU.add), [n3, n4] + w, w)

    def rms_to_bf16(x_ap, xn_ap, junk_ap, ss_ap, P, n, r, names, on_pool=False):
        ACT(lambda e: e.activation(out=junk_ap, in_=x_ap, func=AF.Square, accum_out=ss_ap), r, [names[0], names[1]])
        rsqrt_inplace(ss_ap, n, [names[1]])
        if on_pool:
            DVE(lambda e: e.tensor_single_scalar(out=xn_ap, in_=x_ap, scalar=ss_ap, op=ALU.mult), r + [names[1]], [names[2]])
        else:
            ACT(lambda e: e.activation(out=xn_ap, in_=x_ap, func=AF.Copy, scale=ss_ap), r + [names[1]], [names[2]])

    for l in range(DEPTH):
        S.barrier()
        AR.reset(BASE0)
        for G in GROUPS:
            G.latT = AR.alloc("latT" + G.name, [128, G.NK], BF16)
            G.krT = AR.alloc("krT" + G.name, [128, G.NK], BF16)
        BASE = AR.mark()
        Win = AR.alloc("Win", [128, 8, IN_W], BF16)
        Wuq = AR.alloc("Wuq", [128, 2, 768], BF16)
        wmT = AR.alloc("wmT", [128, 4, 128], BF16)
        cdiag = AR.alloc("cdiag", [128, 2, 31, 128], BF16)
        g_mix = AR.alloc("g_mix", [128, 8], F32)
        g_qa = AR.alloc("g_qa", [128, 2], F32)
        ga_bc = AR.alloc("ga_bc", [128, 256], F32)
        gqn_bc = AR.alloc("gqn_bc", [128, 64], F32)
        gqr_bc = AR.alloc("gqr_bc", [128, 32], F32)
        gkva_bc = AR.alloc("gkva_bc", [128, 128], F32)
        gkr_bc = AR.alloc("gkr_bc", [128, 32], F32)
        bsT = AR.alloc("bsT", [128, 4], F32)
        bdw = AR.alloc("bdw", [128, 2], F32)
        lng = AR.alloc("lng", [128, 2], F32)
        lnb = AR.alloc("lnb", [128, 2], F32)
        wdw = AR.alloc("wdw", [128, 2, 31], F32)
        wnat = AR.alloc("wnat", [128, 256], F32)
        wnat_b = AR.alloc("wnat_b", [128, 2, 128], BF16)
        wsb = AR.alloc("wsb", [128, 4, 128], BF16)

        load_gain_pk(g_mix, W["norm_mix_g"][l], 8)
        load_gain_pk(g_qa, W["c_qa_g"][l], 2)
        load_gain_pk(bdw, W["b_dw_b"][l], 2)
        load_gain_pk(lng, W["b_ln_g"][l], 2)
        load_gain_pk(lnb, W["b_ln_b"][l], 2)
        load_bcast(ga_bc, W["a_norm_g"][l], 256)
        load_bcast(gqn_bc, W["c_qn_g"][l], 64)
        load_bcast(gqr_bc, W["c_qr_g"][l], 32)
        load_bcast(gkva_bc, W["c_kva_g"][l], 128)
        load_bcast(gkr_bc, W["c_kr_g"][l], 32)
        DMA("small", bsT[:, :], W["a_bs"][l].rearrange("g i -> i g"), [], [R(bsT)], slow=True)
        load_w(Win, W["w_in"][l], 8, IN_W, gain=g_mix, gname=R(g_mix))
        for k in range(2):
            st = stg[stg_i["i"] % 2]
            stg_i["i"] += 1
            src = W["c_w_uq"][l][k * 128:(k + 1) * 128, :].rearrange("p (h d) -> p h d", d=96)
            tg = "stg" + str(stg_i["i"] % 2)
            DMA(tg, st[:, 0:512].rearrange("p (h d) -> p h d", d=64), src[:, :, 0:64], [], [R(st)])
            DMA(tg, st[:, 512:768].rearrange("p (h d) -> p h d", d=32), src[:, :, 64:96], [], [R(st)])
            cast(Wuq[:, k, :], st[:, 0:768], [R(st), R(g_qa)], [R(Wuq)], gain=g_qa[:, k:k + 1])
        for g in range(4):
            st = stg[stg_i["i"] % 2]
            stg_i["i"] += 1
            DMA("stg" + str(stg_i["i"] % 2), st[:, 0:128], W["a_ws"][l, g], [], [R(st)])
            DVE(lambda e, st=st, g=g: e.tensor_tensor(out=wsb[:, g, :], in0=st[:, 0:128], in1=tril[:], op=ALU.mult),
                [R(st), R(tril)], [R(wsb)])
        transposes(PB[0], wsb, 4, 128, [R(wsb)])
        DVE(lambda e: e.tensor_copy(out=wmT[:], in_=PB[0][:, 0:4, 0:128]), [R(PB[0])], [R(wmT)])
        POOL(lambda e: e.memset(wnat[:], 0.0), [], [R(wnat)])
        DMA("small", wnat[0:31, :], W["b_dw_w"][l], [R(wnat)], [R(wnat)])
        POOL(lambda e: e.tensor_copy(out=wnat_b[:].rearrange("p a b -> p (a b)"), in_=wnat[:]), [R(wnat)], [R(wnat_b)])
        transposes(PB[0], wnat_b, 2, 128, [R(wnat_b)])
        DVE(lambda e: e.tensor_copy(out=wdw[:], in_=PB[0][:, 0:2, 0:31]), [R(PB[0])], [R(wdw)])
        for ft in range(2):
            for k in range(31):
                eng = POOL if (k % 2 == 0) else DVE
                eng(lambda e, ft=ft, k=k: e.tensor_single_scalar(out=cdiag[:, ft, k, :], in_=ident_f[:],
                                                                 scalar=wdw[:, ft, k:k + 1], op=ALU.mult),
                    [R(ident_f), R(wdw)], [R(cdiag)])

        S1 = AR.mark()
        for G in GROUPS:
            S.barrier()
            AR.reset(S1)
            P, BS, NSUB, T, NP = G.P, G.BS, G.NSUB, G.T, G.NP
            latT, krT = G.latT, G.krT
            x_src = G.x_in if l == 0 else G.x1
            xs = [AR.alloc("x_sb", [128, D], F32) for _ in range(3)]
            junk_2 = [AR.alloc("junk", [128, D], F32)] * 2
            ss_2 = [AR.alloc("ss", [128, 8], F32) for _ in range(2)]
            xn_2 = [AR.alloc("xn", [128, 8, 128], BF16) for _ in range(2)]
            xnT_2 = [AR.alloc("xnT", [128, 8, 128], BF16) for _ in range(2)]
            u_sb_2 = [AR.alloc("u_sb", [128, 256], F32) for _ in range(2)]
            gv_sb_2 = [AR.alloc("gv_sb", [128, 256], F32) for _ in range(2)]
            v32_2 = [AR.alloc("v32", [128, 256], F32) for _ in range(2)]
            v_bf_2 = [AR.alloc("v_bf", [128, 256], BF16) for _ in range(2)]
            sg_2 = [AR.alloc("sg", [128, 256], F32) for _ in range(2)]
            z32_2 = [AR.alloc("z32", [128, 256], F32) for _ in range(2)]
            TM_2 = [AR.alloc("TM", [128, 8, 128], BF16) for _ in range(2)]
            cqnT_2 = [AR.alloc("cqnT", [128, 2, 128], BF16) for _ in range(2)]
            sq_2 = [AR.alloc("sq", [128, 768], F32) for _ in range(2)]
            ssn_2 = [AR.alloc("ssn", [128, 16], F32) for _ in range(2)]
            qn32_2 = [AR.alloc("qn32", [128, 8, 64], F32) for _ in range(2)]
            qr32_2 = [AR.alloc("qr32", [128, 8, 32], F32) for _ in range(2)]
            rt_2 = [[AR.alloc(f"rt{i}", [128, 8, 16], F32) for i in range(4)] for _ in range(2)]
            Qtm_2 = [AR.alloc("Qtm", [128, 8, 128], BF16) for _ in range(2)]
            lat32_2 = [AR.alloc("lat32", [128, 128], F32) for _ in range(2)]
            kr32_2 = [AR.alloc("kr32", [128, 32], F32) for _ in range(2)]
            kro32_2 = [AR.alloc("kro32", [128, 32], F32) for _ in range(2)]
            cst_2 = [AR.alloc("cst", [128, 32], F32) for _ in range(4)]
            mixblk = AR.alloc("mixblk", [128, 4, 512], BF16)
            QTblk = AR.alloc("QTblk", [128, 8, 512], BF16)
            zTb = AR.alloc("zTb", [128, 2, 30 + 512], BF16)
            y32 = AR.alloc("y32", [128, 2, 512], F32)
            ybf = AR.alloc("ybf", [128, 2, 512], BF16)
            ysq = AR.alloc("ysq", [128, 2, 512], BF16)
            mean = AR.alloc("mean", [128, 512], F32)
            msq = AR.alloc("msq", [128, 512], F32)
            rstd = AR.alloc("rstd", [128, 512], F32)
            tln = AR.alloc("tln", [128, 512], F32)
            hb0, hb1, cv0, cv1 = PF[0], PF[1], PF[3], PF[4]
            HB = [PF[0:3], PF[3:6]]
            AUXF = [p_[:].rearrange("p a b -> p (a b)").bitcast(F32) for p_ in PB]
            pbA, pbB = PB

            POOL(lambda e: e.memset(TM_2[0][:], 0.0), [], [R(TM_2[0])])
            POOL(lambda e: e.memset(TM_2[1][:], 0.0), [], [R(TM_2[1])])
            POOL(lambda e: e.memset(Qtm_2[0][:], 0.0), [], [R(Qtm_2[0])])
            POOL(lambda e: e.memset(Qtm_2[1][:], 0.0), [], [R(Qtm_2[1])])
            zTs = [zTb] * G.NBLK
            if G.remote:
                zTs = [zTb] + [AR.alloc("zTall", [128, 2, 30 + 512], BF16) for _ in range(G.NBLK - 1)]
                mixblk0_2 = [AR.alloc("mixblk0", [128, 2, 512], BF16) for _ in range(2)]
                tz = AR.alloc("tz", [128, 2, G.NBLK * 60], BF16)
            elif NP > 0:
                npt = NP // 128
                pl = AR.alloc("pl", [128, npt, 128], F32)
                pk_ = AR.alloc("pk", [128, npt, 32], F32)
                plb = AR.alloc("plb", [128, npt, 128], BF16)
                pkb = AR.alloc("pkb", [128, npt, 128], BF16)
                DMA("past", pl[:], c_lat[l].rearrange("(t p) c -> p t c", p=128), [], [R(pl)])
                DMA("past", pk_[:], c_kr[l].rearrange("(t p) c -> p t c", p=128), [], [R(pk_)])
                POOL(lambda e: e.tensor_copy(out=plb[:], in_=pl[:]), [R(pl)], [R(plb)])
                POOL(lambda e: e.memset(pkb[:], 0.0), [], [R(pkb)])
                POOL(lambda e: e.tensor_copy(out=pkb[:, :, 64:96], in_=pk_[:]), [R(pk_), R(pkb)], [R(pkb)])
                transposes(pbA, plb, npt, 128, [R(plb)])
                DVE(lambda e: e.tensor_copy(out=latT[:, 0:NP].rearrange("p (t c) -> p t c", c=128), in_=pbA[:, 0:npt, :]),
                    [R(pbA)], [R(latT)])
                transposes(pbA, pkb, npt, 128, [R(pkb)])
                DVE(lambda e: e.tensor_copy(out=krT[:, 0:NP].rearrange("p (t c) -> p t c", c=128), in_=pbA[:, 0:npt, :]),
                    [R(pbA)], [R(krT)])
                ch = AR.alloc("ch", [128, 256], F32)
                chb = AR.alloc("chb", [128, 2, 128], BF16)
                DMA("past", ch[0:30, :], c_conv[l], [], [R(ch)])
                POOL(lambda e: e.tensor_copy(out=chb[0:30].rearrange("p a b -> p (a b)"), in_=ch[0:30, :]), [R(ch)], [R(chb)])
                for ft in range(2):
                    PE(lambda e, ft=ft: e.transpose(out=pbA[:, ft, 0:30], in_=chb[0:30, ft, :], identity=ident[0:30, 0:30]),
                       [R(chb), R(ident)], [R(pbA)])
                DVE(lambda e: e.tensor_copy(out=zTb[:, :, 0:30], in_=pbA[:, 0:2, 0:30]), [R(pbA)], [R(zTb)])
                DMA("oc", G.o_conv[l, 0:14, :], c_conv[l, 16:30, :], [], [])
            else:
                POOL(lambda e: e.memset(zTb[:, :, 0:30], 0.0), [], [R(zTb)])

            def load_sub(n_):
                if n_ >= G.NBLK * NSUB:
                    return
                t0_ = n_ * P
                DMA("x" + str(n_ % 3), xs[n_ % 3][0:P, :], x_src[t0_:t0_ + P, :], [G.name + "x1"] if l else [], [R(xs[n_ % 3])])
                DMA("cs" + str(n_ % 4), cst_2[n_ % 4][0:P, :], G.cs[t0_:t0_ + P, :], [], [R(cst_2[n_ % 4])])
            load_sub(0)
            load_sub(1)
            for blk in range(G.NBLK):
                def subtile(blk, s, par):
                    (junk, ss, xn, xnT, u_sb, gv_sb, v32, v_bf, sg, z32, TM, cqnT, sq, ssn, qn32, qr32, Qtm, lat32, kr32, kro32, cst) = (
                        junk_2[par], ss_2[par], xn_2[par], xnT_2[par], u_sb_2[par], gv_sb_2[par], v32_2[par], v_bf_2[par], sg_2[par], z32_2[par],
                        TM_2[par], cqnT_2[par], sq_2[par], ssn_2[par], qn32_2[par], qr32_2[par], Qtm_2[par], lat32_2[par], kr32_2[par], kro32_2[par], cst_2[(blk * NSUB + s) % 4])
                    rt = rt_2[par]
                    hb0, hb1, hb2 = HB[par]
                    pm = AUXF[par]
                    pbA = pbB = PB[par]
                    ssx = f"ssx{par}_"
                    zTb = zTs[blk]
                    t0 = blk * BS + s * P
                    x_sb = xs[(blk * NSUB + s) % 3]
                    load_sub(blk * NSUB + s + 2)
                    rms_to_bf16(x_sb[0:P, :], xn[0:P].rearrange("p a b -> p (a b)"), junk[0:P, :], ss[0:P, 0:1], P, D,
                                [R(x_sb)], [R(junk), ssx + "ss0", R(xn)], on_pool=True)
                    yield
                    transposes(pbA, xn, 8, P, [R(xn)])
                    DVE(lambda e: e.tensor_copy(out=xnT[:, :, 0:P], in_=pbA[:, :, 0:P]), [R(pbA)], [R(xnT)])
                    yield
                    for c, (c0, cw) in enumerate([(0, 512), (512, 512), (1024, 416)]):
                        hb = HB[par][c]
                        for k in range(8):
                            PE(lambda e, hb=hb, k=k, c0=c0, cw=cw: e.matmul(hb[0:P, 0:cw], lhsT=xnT[:, k, 0:P], rhs=Win[:, k, c0:c0 + cw],
                                                                            start=(k == 0), stop=(k == 7)),
                               [R(xnT), R(Win)], [R(hb)], sig=(k == 7))
                    yield
                    ACT(lambda e: e.activation(out=u_sb[0:P, :], in_=hb0[0:P, 0:256], func=AF.Gelu), [R(hb0)], [R(u_sb)])
                    ACT(lambda e: e.activation(out=gv_sb[0:P, :], in_=hb0[0:P, 256:512], func=AF.Gelu), [R(hb0)], [R(gv_sb)])
                    ACT(lambda e: e.activation(out=junk[0:P, 0:256], in_=gv_sb[0:P, :], func=AF.Square, accum_out=ss[0:P, 1:2]),
                        [R(gv_sb)], [R(junk), ssx + "ss1"])
                    rsqrt_inplace(ss[0:P, 1:2], 256, [ssx + "ss1"])
                    DVE(lambda e: e.scalar_tensor_tensor(out=v32[0:P, :], in0=gv_sb[0:P, :], scalar=ss[0:P, 1:2], in1=ga_bc[0:P, :],
                                                         op0=ALU.mult, op1=ALU.mult),
                        [R(gv_sb), ssx + "ss1", R(ga_bc)], [R(v32)])
                    if G is GS:
                        DMA("ogv", o_gv_s[l, t0:t0 + P, :], v32[0:P, :], [R(v32)], [])
                    yield
                    POOL(lambda e: e.tensor_copy(out=v_bf[0:P, :], in_=v32[0:P, :]), [R(v32)], [R(v_bf)])
                    for g in range(4):
                        PE(lambda e, g=g: e.matmul(pm[0:P, g * 64:(g + 1) * 64], lhsT=wmT[0:P, g, 0:P], rhs=v_bf[0:P, g * 64:(g + 1) * 64],
                                                   start=True, stop=True),
                           [R(wmT), R(v_bf)], [R(pm)])
                    for g in range(4):
                        DVE(lambda e, g=g: e.scalar_tensor_tensor(out=TM[0:P, g // 2, (g % 2) * 64:(g % 2) * 64 + 64],
                                                                  in0=pm[0:P, g * 64:(g + 1) * 64], scalar=bsT[0:P, g:g + 1],
                                                                  in1=u_sb[0:P, g * 64:(g + 1) * 64], op0=ALU.add, op1=ALU.mult),
                            [R(pm), R(bsT), R(u_sb)], [R(TM)])
                    yield
                    ACT(lambda e: e.activation(out=sg[0:P, :], in_=hb1[0:P, 256:512], func=AF.Sigmoid), [R(hb1)], [R(sg)])
                    DVE(lambda e: e.tensor_tensor(out=z32[0:P, :], in0=hb1[0:P, 0:256], in1=sg[0:P, :], op=ALU.mult),
                        [R(hb1), R(sg)], [R(z32)])
                    POOL(lambda e: e.tensor_copy(out=TM[0:P, 2:4, :].rearrange("p a b -> p (a b)"), in_=z32[0:P, :]), [R(z32), R(TM)], [R(TM)])
                    if blk == G.NBLK - 1 and s == NSUB - 1:
                        if NP > 0 and not G.remote:
                            DMA("oc", G.o_conv[l, 14:30, :], z32[0:16, :], [R(z32)], [])
                        else:
                            DMA("oc", G.o_conv[l, :, :], z32[P - 30:P, :], [R(z32)], [])
                    yield
                    ACT(lambda e: e.activation(out=junk[0:P, 0:256], in_=hb2[0:P, 0:256], func=AF.Square, accum_out=ss[0:P, 2:3]),
                        [R(hb2)], [R(junk), ssx + "ss2"])
                    rsqrt_inplace(ss[0:P, 2:3], 256, [ssx + "ss2"])
                    ACT(lambda e: e.activation(out=TM[0:P, 4:6, :].rearrange("p a b -> p (a b)"), in_=hb2[0:P, 0:256], func=AF.Copy, scale=ss[0:P, 2:3]),
                        [R(hb2), ssx + "ss2", R(TM)], [R(TM)])
                    yield
                    ACT(lambda e: e.activation(out=junk[0:P, 0:128], in_=hb2[0:P, 256:384], func=AF.Square, accum_out=ss[0:P, 3:4]),
                        [R(hb2)], [R(junk), ssx + "ss3"])
                    rsqrt_inplace(ss[0:P, 3:4], 128, [ssx + "ss3"])
                    DVE(lambda e: e.scalar_tensor_tensor(out=lat32[0:P, :], in0=hb2[0:P, 256:384], scalar=ss[0:P, 3:4], in1=gkva_bc[0:P, :],
                                                         op0=ALU.mult, op1=ALU.mult),
                        [R(hb2), ssx + "ss3", R(gkva_bc)], [R(lat32)])
                    DMA("olat", G.o_lat[l, t0:t0 + P, :], lat32[0:P, :], [R(lat32)], [])
                    POOL(lambda e: e.tensor_copy(out=TM[0:P, 6, :], in_=lat32[0:P, :]), [R(lat32), R(TM)], [R(TM)])
                    yield
                    ACT(lambda e: e.activation(out=junk[0:P, 0:32], in_=hb2[0:P, 384:416], func=AF.Square, accum_out=ss[0:P, 4:5]),
                        [R(hb2)], [R(junk), ssx + "ss4"])
                    rsqrt_inplace(ss[0:P, 4:5], 32, [ssx + "ss4"])
                    DVE(lambda e: e.scalar_tensor_tensor(out=kr32[0:P, :], in0=hb2[0:P, 384:416], scalar=ss[0:P, 4:5], in1=gkr_bc[0:P, :],
                                                         op0=ALU.mult, op1=ALU.mult),
                        [R(hb2), ssx + "ss4", R(gkr_bc)], [R(kr32)])
                    yield
                    rope(kro32[0:P, 0:16], kro32[0:P, 16:32], kr32[0:P, 0:16], kr32[0:P, 16:32], cst[0:P, 0:16], cst[0:P, 16:32],
                         [(rt[i][0:P, 0, :], R(rt[i])) for i in range(4)], [R(kr32), R(cst)], [R(kro32)])
                    DMA("okr", G.o_kr[l, t0:t0 + P, :], kro32[0:P, :], [R(kro32)], [])
                    POOL(lambda e: e.tensor_copy(out=TM[0:P, 7, 64:96], in_=kro32[0:P, :]), [R(kro32), R(TM)], [R(TM)])
                    yield
                    transposes(pbB, TM, 8, P, [R(TM)])
                    DVE(lambda e, s=s: e.tensor_copy(out=mixblk[:, 0:2, s * P:(s + 1) * P], in_=pbB[:, 0:2, 0:P]), [R(pbB)], [R(mixblk)])
                    DVE(lambda e, s=s: e.tensor_copy(out=zTb[:, :, 30 + s * P:30 + (s + 1) * P], in_=pbB[:, 2:4, 0:P]), [R(pbB)], [R(zTb)])
                    DVE(lambda e: e.tensor_copy(out=cqnT[:, :, 0:P], in_=pbB[:, 4:6, 0:P]), [R(pbB)], [R(cqnT)])
                    DVE(lambda e, t0=t0: e.tensor_copy(out=latT[:, NP + t0:NP + t0 + P], in_=pbB[:, 6, 0:P]), [R(pbB)], [R(latT)])
                    DVE(lambda e, t0=t0: e.tensor_copy(out=krT[:, NP + t0:NP + t0 + P], in_=pbB[:, 7, 0:P]), [R(pbB)], [R(krT)])
                    yield
                    for (hb, c0, cw) in [(hb0, 0, 512), (hb1, 512, 256)]:
                        for k in range(2):
                            PE(lambda e, hb=hb, k=k, c0=c0, cw=cw: e.matmul(hb[0:P, 0:cw], lhsT=cqnT[:, k, 0:P], rhs=Wuq[:, k, c0:c0 + cw],
                                                                            start=(k == 0), stop=(k == 1)),
                               [R(cqnT), R(Wuq)], [R(hb)], sig=(k == 1))
                    yield
                    ACT(lambda e: e.activation(out=sq[0:P, 0:512], in_=hb0[0:P, :], func=AF.Square), [R(hb0)], [R(sq)])
                    ACT(lambda e: e.activation(out=sq[0:P, 512:768], in_=hb1[0:P, 0:256], func=AF.Square), [R(hb1)], [R(sq)])
                    DVE(lambda e: e.tensor_reduce(out=ssn[0:P, 0:8], in_=sq[0:P, 0:512].rearrange("p (h d) -> p h d", d=64), axis=AX.X, op=ALU.add),
                        [R(sq)], [ssx + "ssn"])
                    DVE(lambda e: e.tensor_reduce(out=ssn[0:P, 8:16], in_=sq[0:P, 512:768].rearrange("p (h d) -> p h d", d=32), axis=AX.X, op=ALU.add),
                        [R(sq)], [ssx + "ssr"])
                    yield
                    rsqrt_inplace(ssn[0:P, 0:8], 64, [ssx + "ssn"])
                    rsqrt_inplace(ssn[0:P, 8:16], 32, [ssx + "ssr"])
                    DVE(lambda e: e.tensor_tensor(out=qn32[0:P], in0=hb0[0:P, :].rearrange("p (h d) -> p h d", d=64),
                                                  in1=ssn[0:P, 0:8].unsqueeze(2).to_broadcast([P, 8, 64]), op=ALU.mult),
                        [R(hb0), ssx + "ssn"], [R(qn32)])
                    POOL(lambda e: e.tensor_tensor(out=Qtm[0:P, :, 0:64], in0=qn32[0:P], in1=gqn_bc[0:P, :].unsqueeze(1).to_broadcast([P, 8, 64]), op=ALU.mult),
                         [R(qn32), R(gqn_bc), R(Qtm)], [R(Qtm)])
                    yield
                    DVE(lambda e: e.tensor_tensor(out=qr32[0:P], in0=hb1[0:P, 0:256].rearrange("p (h d) -> p h d", d=32),
                                                  in1=ssn[0:P, 8:16].unsqueeze(2).to_broadcast([P, 8, 32]), op=ALU.mult),
                        [R(hb1), ssx + "ssr"], [R(qr32)])
                    POOL(lambda e: e.tensor_tensor(out=qr32[0:P], in0=qr32[0:P], in1=gqr_bc[0:P, :].unsqueeze(1).to_broadcast([P, 8, 32]), op=ALU.mult),
                         [R(qr32), R(gqr_bc)], [R(qr32)])
                    yield
                    cosb = cst[0:P, 0:16].unsqueeze(1).to_broadcast([P, 8, 16])
                    sinb = cst[0:P, 16:32].unsqueeze(1).to_broadcast([P, 8, 16])
                    rope(Qtm[0:P, :, 64:80], Qtm[0:P, :, 80:96], qr32[0:P, :, 0:16], qr32[0:P, :, 16:32], cosb, sinb,
                         [(rt[i][0:P], R(rt[i])) for i in range(4)], [R(qr32), R(cst)], [R(Qtm)])
                    yield
                    transposes(pbA, Qtm, 8, P, [R(Qtm)])
                    DVE(lambda e, s=s: e.tensor_copy(out=QTblk[:, :, s * P:(s + 1) * P], in_=pbA[:, :, 0:P]), [R(pbA)], [R(QTblk)])
                gens = [subtile(blk, s_, (blk * NSUB + s_) % 2) for s_ in range(NSUB)]
                active = [gens.pop(0)]
                for _ in range(8):
                    next(active[0])
                while gens or active:
                    if len(active) < 2 and gens:
                        active.append(gens.pop(0))
                    for g_ in list(active):
                        try:
                            next(g_)
                        except StopIteration:
                            active.remove(g_)
                def conv_epilogue(zsrc, dst, dslot):
                    for ft in range(2):
                        cv = (cv0, cv1)[ft]
                        for k in range(31):
                            PE(lambda e, cv=cv, ft=ft, k=k: e.matmul(cv[:, 0:BS], lhsT=cdiag[:, ft, k, :], rhs=zsrc[:, ft, k:k + BS],
                                                                     start=(k == 0), stop=(k == 30)),
                               [R(cdiag), R(zsrc)], [R(cv)], sig=(k == 30))
                        ACT(lambda e, cv=cv, ft=ft: e.activation(out=y32[:, ft, 0:BS], in_=cv[:, 0:BS], func=AF.Identity, bias=bdw[:, ft:ft + 1]),
                            [R(cv), R(bdw)], [R(y32)])
                        POOL(lambda e, ft=ft: e.tensor_copy(out=ybf[:, ft, 0:BS], in_=y32[:, ft, 0:BS]), [R(y32)], [R(ybf)])
                        ACT(lambda e, ft=ft: e.activation(out=ysq[:, ft, 0:BS], in_=y32[:, ft, 0:BS], func=AF.Square), [R(y32)], [R(ysq)])
                    for ft in range(2):
                        PE(lambda e, ft=ft: e.matmul(hb0[:, 0:BS], lhsT=ones_bf[:], rhs=ybf[:, ft, 0:BS], start=(ft == 0), stop=(ft == 1)),
                           [R(ones_bf), R(ybf)], [R(hb0)], sig=(ft == 1))
                    for ft in range(2):
                        PE(lambda e, ft=ft: e.matmul(hb1[:, 0:BS], lhsT=ones_bf[:], rhs=ysq[:, ft, 0:BS], start=(ft == 0), stop=(ft == 1)),
                           [R(ones_bf), R(ysq)], [R(hb1)], sig=(ft == 1))
                    ACT(lambda e: e.activation(out=mean[:, 0:BS], in_=hb0[:, 0:BS], func=AF.Copy, scale=1.0 / 256), [R(hb0)], [R(mean)])
                    POOL(lambda e: e.tensor_tensor(out=msq[:, 0:BS], in0=mean[:, 0:BS], in1=mean[:, 0:BS], op=ALU.mult), [R(mean)], [R(msq)])
                    DVE(lambda e: e.scalar_tensor_tensor(out=rstd[:, 0:BS], in0=hb1[:, 0:BS], scalar=1.0 / 256, in1=msq[:, 0:BS],
                                                         op0=ALU.mult, op1=ALU.subtract),
                        [R(hb1), R(msq)], [R(rstd)])
                    ACT(lambda e: e.activation(out=rstd[:, 0:BS], in_=rstd[:, 0:BS], func=AF.Sqrt, bias=EPS, scale=1.0), [R(rstd)], [R(rstd)])
                    DVE(lambda e: e.reciprocal(out=rstd[:, 0:BS], in_=rstd[:, 0:BS]), [R(rstd)], [R(rstd)])
                    for ft in range(2):
                        POOL(lambda e, ft=ft: e.tensor_tensor(out=tln[:, 0:BS], in0=y32[:, ft, 0:BS], in1=mean[:, 0:BS], op=ALU.subtract),
                             [R(y32), R(mean)], [R(tln)])
                        DVE(lambda e: e.tensor_tensor(out=tln[:, 0:BS], in0=tln[:, 0:BS], in1=rstd[:, 0:BS], op=ALU.mult), [R(tln), R(rstd)], [R(tln)])
                        ACT(lambda e, ft=ft: e.activation(out=dst[:, dslot + ft, 0:BS], in_=tln[:, 0:BS], func=AF.Silu,
                                                          bias=lnb[:, ft:ft + 1], scale=lng[:, ft:ft + 1]),
                            [R(tln), R(lnb), R(lng)], [R(dst)])
                c0 = blk * BS
                if G.remote:
                    DMA("mixw", G.mixT[0:256, c0:c0 + BS].rearrange("(k p) t -> p k t", p=128), mixblk[:, 0:2, 0:BS], [R(mixblk)], [G.name + "mixT"])
                else:
                    conv_epilogue(zTb, mixblk, 2)
                    DMA("mixw", G.mixT[0:512, c0:c0 + BS].rearrange("(k p) t -> p k t", p=128), mixblk[:, :, 0:BS], [R(mixblk)], [G.name + "mixT"])
                DMA("qtw", G.QT[:, :, c0:c0 + BS].rearrange("h p t -> p h t"), QTblk[:, :, 0:BS], [R(QTblk)], [G.name + "QT"])
                if blk + 1 < G.NBLK and not G.remote:
                    POOL(lambda e: e.tensor_copy(out=zTb[:, :, 0:30], in_=zTb[:, :, BS:BS + 30]), [R(zTb)], [R(zTb)])
            if G.remote:
                xin = [t.ap() for t in G.xin]
                xout = [t.ap() for t in G.xout]
                DMA("xi0", xin[0][:, :], latT[:, NP:NP + T], [R(latT)], [G.name + "xin0"])
                DMA("xi1", xin[1][:, :], krT[64:96, NP:NP + T], [R(krT)], [G.name + "xin1"])
                for j in range(G.NBLK):
                    DMA("xi2", xin[2][:, j * 60:(j + 1) * 60].rearrange("p (a b) -> p a b", a=2), zTs[j][:, :, BS:BS + 30], [R(zTs[j])], [G.name + "xin2"])
                for i in range(3):
                    S.dma("cc", lambda e, i=i: e.collective_compute("AllGather", ALU.bypass, replica_groups=[[0, 1], [2, 3], [4, 5], [6, 7]],
                                                                    ins=[G.xin[i].ap().opt()], outs=[G.xout[i].ap().opt()]),
                          [G.name + f"xin{i}"], [G.name + f"xout{i}"], eng="pool", inc=1)
                DMA("xo0", latT[:, 0:2 * T].rearrange("p (r t) -> p r t", r=2), xout[0].rearrange("(r p) t -> p r t", p=128), [G.name + "xout0"], [R(latT)])
                DMA("xo1", krT[64:96, 0:2 * T].rearrange("p (r t) -> p r t", r=2), xout[1].rearrange("(r p) t -> p r t", p=32), [G.name + "xout1"], [R(krT)])
                DMA("xo2", tz[:, :, :], xout[2].rearrange("(r p) t -> p r t", p=128), [G.name + "xout2"], [R(tz)])
                for j in range(G.NBLK):
                    zj = zTs[j]
                    POOL(lambda e, j=j, zj=zj: e.tensor_single_scalar(out=zj[:, :, 0:30], in_=tz[:, 0, j * 60:(j + 1) * 60].rearrange("p (a b) -> p a b", a=2),
                                                                      scalar=flag[:, 0:1], op=ALU.mult),
                         [R(tz), R(flag), R(zj)], [R(zj)])
                    if j >= 1:
                        DVE(lambda e, j=j, zj=zj: e.scalar_tensor_tensor(out=zj[:, :, 0:30], in0=tz[:, 1, (j - 1) * 60:j * 60].rearrange("p (a b) -> p a b", a=2),
                                                                          scalar=nflag[:, 0:1], in1=zj[:, :, 0:30], op0=ALU.mult, op1=ALU.add),
                             [R(tz), R(nflag), R(zj)], [R(zj)])
                    mb0 = mixblk0_2[j % 2]
                    conv_epilogue(zj, mb0, 0)
                    DMA("mixw0" + str(j % 2), G.mixT[256:512, j * BS:(j + 1) * BS].rearrange("(k p) t -> p k t", p=128), mb0[:, :, 0:BS], [R(mb0)], [G.name + "mixT"])

        S.barrier()
        AR.reset(BASE)
        Wkv_t = AR.alloc("Wkv", [128, 1, 1024], BF16)
        Wkv = Wkv_t[:, 0, :].rearrange("p (h c) -> p h c", c=128)
        gkn = AR.alloc("gkn", [128, 1], F32)
        load_w(Wkv_t, W["c_w_ukv"][l], 1, 1024)
        DMA("small", gkn[0:64, :], W["c_kn_g"][l].rearrange("(p o) -> p o", o=1), [], [R(gkn)])
        S2 = AR.mark()
        sm_scale = 1.0 / math.sqrt(96.0)
        for G in GROUPS:
            S.barrier()
            AR.reset(S2)
            P, BS, T, NP, NK = G.P, G.BS, G.T, G.NP, G.NK
            latT, krT = G.latT, G.krT
            ktiles = [(k0, min(128, NK - k0)) for k0 in range(0, NK, 128)]
            NKT = len(ktiles)
            KT2 = [[AR.alloc(f"KT{e}", [128, NK], BF16) for e in range(2)] for _ in range(2)]
            Vp2 = [AR.alloc("Vp", [128, NKT, 2, 66], BF16) for _ in range(2)]
            QTs2 = [[AR.alloc(f"QTs{e}", [128, T], BF16) for e in range(2)] for _ in range(2)]
            sqb1 = AR.alloc("sqb", [128, 512], BF16)
            sd1 = AR.alloc("sd", [128, 512], F32)
            PT = [AR.alloc(f"PT{i}", [128, 512], BF16) for i in range(6)]
            o_sb2 = [AR.alloc(f"o_sb{i}", [128, 512], F32) for i in range(2)]
            ycT = [AR.alloc(f"ycT{i}", [128, 512], BF16) for i in range(2)]
            pk, pst, pv, ps0, ps1, ps2 = PF
            SB = [p_[:].rearrange("p a b -> p (a b)").bitcast(F32) for p_ in PB] + [ps0, pv]
            pending_epi = []
            NRT = 0
            if G.remote:
                maskt = AR.alloc("maskt", [128, 8, 512], BF16)
                DMA("maskl", maskt[:], mask_d, [], [R(maskt)])
            for st_ in range(2):
                POOL(lambda e, st_=st_: e.memset(Vp2[st_][:], 1.0), [], [R(Vp2[st_])])
                if NRT:
                    POOL(lambda e, st_=st_: e.tensor_single_scalar(out=Vp2[st_][:, 0:NRT, :, 64:66], in_=Vp2[st_][:, 0:NRT, :, 64:66], scalar=flag[:, 0:1], op=ALU.mult),
                         [R(Vp2[st_]), R(flag)], [R(Vp2[st_])])
                for e_ in range(2):
                    POOL(lambda e, e_=e_, st_=st_: e.tensor_copy(out=KT2[st_][e_][64:96, :], in_=krT[64:96, :]), [R(krT)], [R(KT2[st_][e_])])
            pti = 0

            def kvmat(hg):
                st_ = hg % 2
                KT, Vp, QTs = KT2[st_], Vp2[st_], QTs2[st_]
                for e_ in range(2):
                    h = 2 * hg + e_
                    DMA("qtr" + str(st_) + str(e_), QTs[e_][:, :], G.QT[h], [G.name + "QT"], [R(QTs[e_])])
                    for k0 in range(0, NK, 512):
                        kw = min(512, NK - k0)
                        PE(lambda e, h=h, k0=k0, kw=kw: e.matmul(pk[0:64, 0:kw], lhsT=Wkv[:, h, 0:64], rhs=latT[:, k0:k0 + kw], start=True, stop=True),
                           [R(Wkv_t), R(latT)], [R(pk)])
                        ACT(lambda e, kw=kw: e.activation(out=sqb1[0:64, 0:kw], in_=pk[0:64, 0:kw], func=AF.Square), [R(pk)], [R(sqb1)])
                        PE(lambda e, kw=kw: e.matmul(pst[0:64, 0:kw], lhsT=ones_bf[0:64, 0:64], rhs=sqb1[0:64, 0:kw], start=True, stop=True),
                           [R(ones_bf), R(sqb1)], [R(pst)])
                        ACT(lambda e, kw=kw: e.activation(out=sd1[0:64, 0:kw], in_=pst[0:64, 0:kw], func=AF.Ln, bias=EPS, scale=1.0 / 64), [R(pst)], [R(sd1)])
                        ACT(lambda e, kw=kw: e.activation(out=sd1[0:64, 0:kw], in_=sd1[0:64, 0:kw], func=AF.Exp, scale=-0.5), [R(sd1)], [R(sd1)])
                        DVE(lambda e, e_=e_, k0=k0, kw=kw: e.scalar_tensor_tensor(out=KT[e_][0:64, k0:k0 + kw], in0=pk[0:64, 0:kw], scalar=gkn[0:64, 0:1],
                                                                                   in1=sd1[0:64, 0:kw], op0=ALU.mult, op1=ALU.mult),
                            [R(pk), R(gkn), R(sd1)], [R(KT[e_])])
                        yield
                for j0 in range(0, NKT, 4):
                    grp = ktiles[j0:j0 + 4]
                    for j, (k0, ksz) in enumerate(grp):
                        PE(lambda e, j=j, k0=k0, ksz=ksz, hg=hg: e.matmul(pk[0:ksz, j * 128:(j + 1) * 128], lhsT=latT[:, k0:k0 + ksz],
                                                                          rhs=Wkv[:, 2 * hg:2 * hg + 2, 64:128], start=True, stop=True),
                           [R(latT), R(Wkv_t)], [R(pk)])
                    full = [g_ for g_ in grp if g_[1] == 128]
                    if full:
                        n = len(full)
                        if j0 < NRT:
                            DVE(lambda e, j0=j0, n=n: e.tensor_single_scalar(out=Vp[:, j0:j0 + n, :, 0:64],
                                                                             in_=pk[:, 0:n * 128].rearrange("p (t h d) -> p t h d", h=2, d=64),
                                                                             scalar=flag[:, 0:1], op=ALU.mult),
                                [R(pk), R(flag)], [R(Vp)])
                        else:
                            DVE(lambda e, j0=j0, n=n: e.tensor_copy(out=Vp[:, j0:j0 + n, :, 0:64],
                                                                    in_=pk[:, 0:n * 128].rearrange("p (t h d) -> p t h d", h=2, d=64)),
                                [R(pk)], [R(Vp)])
                    for j, (k0, ksz) in enumerate(grp):
                        if ksz < 128:
                            DVE(lambda e, j=j, j0=j0, ksz=ksz: e.tensor_copy(out=Vp[0:ksz, j0 + j, :, 0:64],
                                                                             in_=pk[0:ksz, j * 128:(j + 1) * 128].rearrange("p (h d) -> p h d", d=64)),
                                [R(pk)], [R(Vp)])
                    yield

            for _ in kvmat(0):
                pass
            for hg in range(4):
                KT, Vp, QTs = KT2[hg % 2], Vp2[hg % 2], QTs2[hg % 2]
                nxt = kvmat(hg + 1) if hg + 1 < 4 else None
                ucount = 0
                for e_ in range(2):
                    h = 2 * hg + e_
                    for qb in range(G.NBLK):
                        q0 = qb * BS
                        if G.remote:
                            NT = T // 128
                            vis = []
                            for part in range(2):
                                for jb in range(qb + 1):
                                    for r in range(4):
                                        kt = part * NT + jb * 4 + r
                                        vis.append((kt, ktiles[kt][0], 128, (part * 4 + r) if jb == qb else None))
                        elif NP > 0:
                            vis = [(kt, k0, ksz, None) for kt, (k0, ksz) in enumerate(ktiles)]
                        else:
                            vis = []
                            for kt, (k0, ksz) in enumerate(ktiles):
                                if k0 >= q0 + BS:
                                    break
                                vis.append((kt, k0, ksz, (k0 - q0) // 128 if k0 >= q0 else None))
                        acc = (ps1, ps2)[(h * G.NBLK + qb) % 2]
                        n = len(vis)
                        LA = 3
                        units = []
                        for i in range(n + LA):
                            if i < n:
                                kt, k0, ksz, dg = vis[i]
                                c0 = dg * 128 if (dg is not None and not G.remote) else 0
                                pt = PT[pti % len(PT)]
                                sb = SB[pti % len(SB)]
                                pti += 1
                                units.append((pt, kt, ksz, c0))
                                PE(lambda e, e_=e_, k0=k0, ksz=ksz, c0=c0, q0=q0, sb=sb: e.matmul(sb[0:ksz, c0:BS], lhsT=KT[e_][0:96, k0:k0 + ksz],
                                                                                                  rhs=QTs[e_][0:96, q0 + c0:q0 + BS], start=True, stop=True),
                                   [R(KT[e_]), R(QTs[e_])], [R(sb)])
                                ACT(lambda e, pt=pt, ksz=ksz, c0=c0, sb=sb: e.activation(out=pt[0:ksz, c0:BS], in_=sb[0:ksz, c0:BS], func=AF.Exp, scale=sm_scale),
                                    [R(sb)], [R(pt)])
                                if dg is not None and G.remote:
                                    DVE(lambda e, pt=pt, dg=dg: e.tensor_tensor(out=pt[:, 0:BS], in0=pt[:, 0:BS], in1=maskt[:, dg, 0:BS], op=ALU.mult),
                                        [R(pt), R(maskt)], [R(pt)])
                                elif dg is not None:
                                    POOL(lambda e, pt=pt, c0=c0: e.memset(pt[64:128, c0:c0 + 64], 0.0), [R(pt)], [R(pt)])
                            j = i - LA
                            if j >= 0:
                                pt, kt, ksz, c0 = units[j]
                                PE(lambda e, pt=pt, kt=kt, ksz=ksz, c0=c0, e_=e_, acc=acc, j=j, n=n: e.matmul(
                                    acc[0:65, c0:BS], lhsT=Vp[0:ksz, kt, e_, 0:65], rhs=pt[0:ksz, c0:BS], start=(j == 0), stop=(j == n - 1)),
                                   [R(Vp), R(pt)], [R(acc)])
                            if i == min(14, n - 1) and pending_epi:
                                pending_epi.pop(0)()
                            ucount += 1
                            if nxt is not None and ucount % 10 == 0:
                                if next(nxt, "done") == "done":
                                    nxt = None

                        par = (h * G.NBLK + qb) % 2
                        yc, osb = ycT[par], o_sb2[par]
                        rd = osb
                        DVE(lambda e: e.tensor_copy(out=osb[0:65, 0:BS], in_=acc[0:65, 0:BS]), [R(acc)], [R(osb)])
                        DVE(lambda e: e.reciprocal(out=rd[64:65, 0:BS], in_=osb[64:65, 0:BS]), [R(osb)], [R(rd)])

                        def epilogue(h=h, q0=q0, par=par, yc=yc, osb=osb, rd=rd):
                            PE(lambda e: e.matmul(pst[0:64, 0:BS], lhsT=ones_f[64:65, 0:64], rhs=rd[64:65, 0:BS], start=True, stop=True),
                               [R(ones_f), R(rd)], [R(pst)])
                            DVE(lambda e: e.tensor_tensor(out=yc[0:64, 0:BS], in0=pst[0:64, 0:BS], in1=osb[0:64, 0:BS], op=ALU.mult),
                                [R(pst), R(osb)], [R(yc)])
                            DMA("ycw" + str(par), G.mixT[512 + h * 64:512 + (h + 1) * 64, q0:q0 + BS], yc[0:64, 0:BS], [R(yc)], [G.name + "mixT"])
                        pending_epi.append(epilogue)
                while pending_epi:
                    pending_epi.pop(0)()
                if nxt is not None:
                    for _ in nxt:
                        pass

        S.barrier()
        AR.reset(BASE0)
        Wout = AR.alloc("Wout", [128, 8, D], BF16)
        Wmq = AR.alloc("Wmq", [128, 8, 512], BF16)
        Wmo = AR.alloc("Wmo", [128, 4, D], BF16)
        Wmk = AR.alloc("Wmk", [128, 8, 512], BF16)
        Wmv = AR.alloc("Wmv", [128, 8, 512], BF16)
        g_nm = AR.alloc("g_nm", [128, 8], F32)
        g_mn = AR.alloc("g_mn", [128, 8], F32)
        g_mq = AR.alloc("g_mq", [128, 1], F32)
        gmk_bc = AR.alloc("gmk_bc", [128, 128], F32)
        load_gain_pk(g_nm, W["norm_mem_g"][l], 8)
        load_gain_pk(g_mn, W["mem_norm_g"][l], 8)
        DMA("small", g_mq[:, :], W["m_q_g"][l].rearrange("(p o) -> p o", o=1), [], [R(g_mq)])
        load_bcast(gmk_bc, W["m_k_g"][l], 128)
        load_w(Wmk, W["w_mk"][l], 8, 512, gain=g_mn, gname=R(g_mn))
        load_w(Wmv, W["w_mv"][l], 8, 512, gain=g_mn, gname=R(g_mn))
        load_w(Wout, W["w_out"][l], 8, D)
        load_w(Wmq, W["w_mq"][l], 8, 512, gain=g_nm, gname=R(g_nm))
        load_w(Wmo, W["w_mo"][l], 4, D)
        mm_scale = 1.0 / math.sqrt(128.0)
        S3 = AR.mark()
        for G in GROUPS:
            S.barrier()
            AR.reset(S3)
            P, BS, NSUB, T = G.P, G.BS, G.NSUB, G.T
            memKT = AR.alloc("memKT", [128, 4, 256], BF16)
            memV = AR.alloc("memV", [128, 2, 512], BF16)
            mx = AR.alloc("mx", [128, 2, D], F32)
            mxn = AR.alloc("mxn", [128, 8, 128], BF16)
            mnT = AR.alloc("mnT", [128, 8, 256], BF16)
            junk = AR.alloc("junk", [128, D], F32)
            ss = AR.alloc("ss", [128, 8], F32)
            mk32 = AR.alloc("mk32", [128, 512], F32)
            mkb = AR.alloc("mkb", [128, 4, 128], BF16)
            mv32 = AR.alloc("mv32", [128, 512], F32)
            pA, pB, pC, pD, pE_, pF_ = PF
            pbA, pbB = PB
            for s in range(2):
                if G is GP:
                    DMA("mx", mx[:, s, :], mem_in[s * 128:(s + 1) * 128, :], [], [R(mx)])
                    rms_to_bf16(mx[:, s, :], mxn[:].rearrange("p a b -> p (a b)"), junk[:, :], ss[:, 0:1], 128, D, [R(mx)], [R(junk), "ss0", R(mxn)])
                    transposes(pbA, mxn, 8, 128, [R(mxn)])
                    DVE(lambda e, s=s: e.tensor_copy(out=mnT[:, :, s * 128:(s + 1) * 128], in_=pbA[:, :, 0:128]), [R(pbA)], [R(mnT)])
                    for k in range(8):
                        PE(lambda e, k=k, s=s: e.matmul(pA[:, :], lhsT=mnT[:, k, s * 128:(s + 1) * 128], rhs=Wmk[:, k, :], start=(k == 0), stop=(k == 7)),
                           [R(mnT), R(Wmk)], [R(pA)], sig=(k == 7))
                    for k in range(8):
                        PE(lambda e, k=k, s=s: e.matmul(pB[:, :], lhsT=mnT[:, k, s * 128:(s + 1) * 128], rhs=Wmv[:, k, :], start=(k == 0), stop=(k == 7)),
                           [R(mnT), R(Wmv)], [R(pB)], sig=(k == 7))
                    ACT(lambda e: e.activation(out=junk[:, 0:512], in_=pA[:, :], func=AF.Square), [R(pA)], [R(junk)])
                    DVE(lambda e: e.tensor_reduce(out=ss[:, 4:8], in_=junk[:, 0:512].rearrange("p (h d) -> p h d", d=128), axis=AX.X, op=ALU.add),
                        [R(junk)], ["ss4"])
                    rsqrt_inplace(ss[:, 4:8], 128, ["ss4"])
                    DVE(lambda e: e.tensor_tensor(out=mk32[:].rearrange("p (h d) -> p h d", d=128), in0=pA[:, :].rearrange("p (h d) -> p h d", d=128),
                                                  in1=ss[:, 4:8].unsqueeze(2).to_broadcast([128, 4, 128]), op=ALU.mult),
                        [R(pA), "ss4"], [R(mk32)])
                    POOL(lambda e: e.tensor_tensor(out=mk32[:].rearrange("p (h d) -> p h d", d=128), in0=mk32[:].rearrange("p (h d) -> p h d", d=128),
                                                   in1=gmk_bc[:, :].unsqueeze(1).to_broadcast([128, 4, 128]), op=ALU.mult),
                         [R(mk32), R(gmk_bc)], [R(mk32)])
                    ACT(lambda e: e.activation(out=mv32[:, :], in_=pB[:, :], func=AF.Copy), [R(pB)], [R(mv32)])
                    DMA("omk", o_mk_p[l, s * 128:(s + 1) * 128, :], mk32[:, :], [R(mk32)], [])
                    DMA("omv", o_mv_p[l, s * 128:(s + 1) * 128, :], mv32[:, :], [R(mv32)], [])
                else:
                    DMA("mx", mk32[:, :], c_mk[l, s * 128:(s + 1) * 128, :], [], [R(mk32)])
                    DMA("mx", mv32[:, :], c_mv[l, s * 128:(s + 1) * 128, :], [], [R(mv32)])
                POOL(lambda e: e.tensor_copy(out=mkb[:].rearrange("p a b -> p (a b)"), in_=mk32[:, :]), [R(mk32)], [R(mkb)])
                POOL(lambda e, s=s: e.tensor_copy(out=memV[:, s, :], in_=mv32[:, :]), [R(mv32)], [R(memV)])
                transposes(pbB, mkb, 4, 128, [R(mkb)])
                DVE(lambda e, s=s: e.tensor_copy(out=memKT[:, :, s * 128:(s + 1) * 128], in_=pbB[:, 0:4, 0:128]), [R(pbB)], [R(memKT)])
            xb2 = [AR.alloc("xb", [128, 4, D], F32) for _ in range(2)]
            mixb2 = [AR.alloc("mixb", [128, 8, 512], BF16) for _ in range(2)]
            xn2 = [AR.alloc("xn", [128, 8, 128], BF16) for _ in range(2)]
            xnT2 = [AR.alloc("xnT", [128, 8, 512], BF16) for _ in range(2)]
            junk2 = [AR.alloc("junk", [128, D], F32) for _ in range(2)]
            ssb2 = [AR.alloc("ssb", [128, 8], F32) for _ in range(2)]
            sqb2 = [AR.alloc("sqb", [128, 512], BF16) for _ in range(2)]
            sd2 = [AR.alloc("sd", [128, 512], F32) for _ in range(2)]
            qmn2 = [AR.alloc("qmn", [128, 512], BF16) for _ in range(2)]
            PTm2 = [[AR.alloc(f"PTm{i}", [128, 512], BF16) for i in range(2)] for _ in range(2)]
            rden2 = [AR.alloc("rden", [128, 512], F32) for _ in range(2)]
            omT2 = [AR.alloc("omT", [128, 4, 512], BF16) for _ in range(2)]
            XB = [PF[0:3], PF[3:6]]
            AUXF3 = [p_[:].rearrange("p a b -> p (a b)").bitcast(F32) for p_ in PB]
            x_src = G.x_in if l == 0 else G.x1

            def block3a(blk, par):
                xb, mixb, xn, xnT, junk, ss, sqb, sd, qmn, PTm, rden, omT = (xb2[par], mixb2[par], xn2[par], xnT2[par], junk2[par], ssb2[par],
                                                                              sqb2[par], sd2[par], qmn2[par], PTm2[par], rden2[par], omT2[par])
                X0, X1, X2 = XB[par]
                pbT, auxf = PB[par], AUXF3[par]
                ssn_ = f"ss3a{par}"
                c0 = blk * BS
                DMA("xb" + str(par), xb[0:P, 0:NSUB, :], x_src[c0:c0 + BS, :].rearrange("(s p) d -> p s d", p=P), [G.name + "x1"] if l else [], [R(xb)])
                DMA("mixr" + str(par), mixb[:, :, 0:BS], G.mixT[:, c0:c0 + BS].rearrange("(k p) t -> p k t", p=128), [G.name + "mixT"], [R(mixb)])
                yield
                for s in range(NSUB):
                    for nh in range(2):
                        acc = (X0, X1)[nh]
                        for k in range(8):
                            PE(lambda e, k=k, s=s, nh=nh, acc=acc: e.matmul(acc[0:P, :], lhsT=mixb[:, k, s * P:(s + 1) * P], rhs=Wout[:, k, nh * 512:(nh + 1) * 512],
                                                                            start=(k == 0), stop=(k == 7)),
                               [R(mixb), R(Wout)], [R(acc)], sig=(k == 7))
                        DVE(lambda e, s=s, nh=nh, acc=acc: e.tensor_tensor(out=xb[0:P, s, nh * 512:(nh + 1) * 512], in0=acc[0:P, :],
                                                                           in1=xb[0:P, s, nh * 512:(nh + 1) * 512], op=ALU.add),
                            [R(acc), R(xb)], [R(xb)])
                    yield
                    rms_to_bf16(xb[0:P, s, :], xn[0:P].rearrange("p a b -> p (a b)"), junk[0:P, :], ss[0:P, 0:1], P, D, [R(xb)], [R(junk), ssn_, R(xn)])
                    yield
                    transposes(pbT, xn, 8, P, [R(xn)])
                    DVE(lambda e, s=s: e.tensor_copy(out=xnT[:, :, s * P:(s + 1) * P], in_=pbT[:, :, 0:P]), [R(pbT)], [R(xnT)])
                    yield
                for h in range(4):
                    for k in range(8):
                        PE(lambda e, k=k, h=h: e.matmul(X1[:, 0:BS], lhsT=Wmq[:, k, h * 128:(h + 1) * 128], rhs=xnT[:, k, 0:BS], start=(k == 0), stop=(k == 7)),
                           [R(Wmq), R(xnT)], [R(X1)], sig=(k == 7))
                    ACT(lambda e: e.activation(out=sqb[:, 0:BS], in_=X1[:, 0:BS], func=AF.Square), [R(X1)], [R(sqb)])
                    yield
                    PE(lambda e: e.matmul(X2[:, 0:BS], lhsT=ones_bf[:, :], rhs=sqb[:, 0:BS], start=True, stop=True), [R(ones_bf), R(sqb)], [R(X2)])
                    ACT(lambda e: e.activation(out=sd[:, 0:BS], in_=X2[:, 0:BS], func=AF.Ln, bias=EPS, scale=1.0 / 128), [R(X2)], [R(sd)])
                    ACT(lambda e: e.activation(out=sd[:, 0:BS], in_=sd[:, 0:BS], func=AF.Exp, scale=-0.5), [R(sd)], [R(sd)])
                    yield
                    DVE(lambda e: e.scalar_tensor_tensor(out=qmn[:, 0:BS], in0=X1[:, 0:BS], scalar=g_mq[:, 0:1], in1=sd[:, 0:BS], op0=ALU.mult, op1=ALU.mult),
                        [R(X1), R(g_mq), R(sd)], [R(qmn)])
                    yield
                    for kt in range(2):
                        sc = (X1, X2)[kt]
                        PE(lambda e, kt=kt, h=h, sc=sc: e.matmul(sc[:, 0:BS], lhsT=memKT[:, h, kt * 128:(kt + 1) * 128], rhs=qmn[:, 0:BS], start=True, stop=True),
                           [R(memKT), R(qmn)], [R(sc)])
                        ACT(lambda e, kt=kt, sc=sc: e.activation(out=PTm[kt][:, 0:BS], in_=sc[:, 0:BS], func=AF.Exp, scale=mm_scale), [R(sc)], [R(PTm[kt])])
                    yield
                    for kt in range(2):
                        PE(lambda e, kt=kt, h=h: e.matmul(X0[:, 0:BS], lhsT=memV[:, kt, h * 128:(h + 1) * 128], rhs=PTm[kt][:, 0:BS], start=(kt == 0), stop=(kt == 1)),
                           [R(memV), R(PTm[kt])], [R(X0)], sig=(kt == 1))
                    for kt in range(2):
                        PE(lambda e, kt=kt: e.matmul(auxf[:, 0:BS], lhsT=ones_bf[:, :], rhs=PTm[kt][:, 0:BS], start=(kt == 0), stop=(kt == 1)),
                           [R(ones_bf), R(PTm[kt])], [R(auxf)], sig=(kt == 1))
                    yield
                    DVE(lambda e: e.reciprocal(out=rden[:, 0:BS], in_=auxf[:, 0:BS]), [R(auxf)], [R(rden)])
                    DVE(lambda e, h=h: e.tensor_tensor(out=omT[:, h, 0:BS], in0=X0[:, 0:BS], in1=rden[:, 0:BS], op=ALU.mult), [R(X0), R(rden)], [R(omT)])
                    yield
                for s in range(NSUB):
                    for nh in range(2):
                        acc = (X0, X1)[nh]
                        for h in range(4):
                            PE(lambda e, h=h, s=s, nh=nh, acc=acc: e.matmul(acc[0:P, :], lhsT=omT[:, h, s * P:(s + 1) * P], rhs=Wmo[:, h, nh * 512:(nh + 1) * 512],
                                                                            start=(h == 0), stop=(h == 3)),
                               [R(omT), R(Wmo)], [R(acc)], sig=(h == 3))
                        DVE(lambda e, s=s, nh=nh, acc=acc: e.tensor_tensor(out=xb[0:P, s, nh * 512:(nh + 1) * 512], in0=acc[0:P, :],
                                                                           in1=xb[0:P, s, nh * 512:(nh + 1) * 512], op=ALU.add),
                            [R(acc), R(xb)], [R(xb)])
                    yield
                DMA("xmw" + str(par), G.xm[c0:c0 + BS, :].rearrange("(s p) d -> p s d", p=P), xb[0:P, 0:NSUB, :], [R(xb)], [G.name + "xm"])
                for s in range(NSUB):
                    rms_to_bf16(xb[0:P, s, :], xn[0:P].rearrange("p a b -> p (a b)"), junk[0:P, :], ss[0:P, 0:1], P, D, [R(xb)], [R(junk), ssn_, R(xn)])
                    yield
                    transposes(pbT, xn, 8, P, [R(xn)])
                    DVE(lambda e, s=s: e.tensor_copy(out=xnT[:, :, s * P:(s + 1) * P], in_=pbT[:, :, 0:P]), [R(pbT)], [R(xnT)])
                    yield
                DMA("xnfw" + str(par), G.xnf[:, c0:c0 + BS].rearrange("(k p) t -> p k t", p=128), xnT[:, :, 0:BS], [R(xnT)], [G.name + "xnf"])

            gens = [block3a(blk, blk % 2) for blk in range(G.NBLK)]
            active = [gens.pop(0)]
            for _ in range(14):
                next(active[0])
            while gens or active:
                if len(active) < 2 and gens:
                    active.append(gens.pop(0))
                for g_ in list(active):
                    try:
                        next(g_)
                    except StopIteration:
                        active.remove(g_)

        S.barrier()
        AR.reset(BASE0)
        W1 = AR.alloc("W1", [128, 8, 4 * D], BF16)
        W2 = AR.alloc("W2", [128, 32, D], BF16)
        g_ff = AR.alloc("g_ff", [128, 8], F32)
        load_gain_pk(g_ff, W["norm_ffn_g"][l], 8)
        load_w(W1, W["w_ff1"][l], 8, 4 * D, gain=g_ff, gname=R(g_ff))
        load_w(W2, W["w_ff2"][l], 32, D)
        S4 = AR.mark()
        for G in GROUPS:
            S.barrier()
            AR.reset(S4)
            P, BS, NSUB, T = G.P, G.BS, G.NSUB, G.T
            xb = AR.alloc("xb", [128, 4, D], F32)
            xnT2b = [AR.alloc("xnT", [128, 8, 512], BF16) for _ in range(2)]
            rl = [AR.alloc(f"rl{i}", [128, 512], F32) for i in range(2)]
            h1T = AR.alloc("h1T", [128, 16, 512], BF16)
            pA, pB, pC, pD, pE_, pF_ = PF
            pbA, pbB = PB
            x_dst = G.x1 if l == 0 else G.y_out

            def load_xn(b_):
                if b_ < G.NBLK:
                    DMA("xnfr" + str(b_ % 2), xnT2b[b_ % 2][:, :, 0:BS], G.xnf[:, b_ * BS:(b_ + 1) * BS].rearrange("(k p) t -> p k t", p=128),
                        [G.name + "xnf"], [R(xnT2b[b_ % 2])])
            load_xn(0)
            for blk in range(G.NBLK):
                c0 = blk * BS
                xnT = xnT2b[blk % 2]
                load_xn(blk + 1)
                DMA("xb", xb[0:P, 0:NSUB, :], G.xm[c0:c0 + BS, :].rearrange("(s p) d -> p s d", p=P), [G.name + "xm"], [R(xb)])
                for fh in range(2):
                    for f in range(16):
                        ff = fh * 16 + f
                        pp = (pA, pB)[f % 2]
                        r_ = rl[f % 2]
                        for k in range(8):
                            PE(lambda e, k=k, ff=ff, pp=pp: e.matmul(pp[:, 0:BS], lhsT=W1[:, k, ff * 128:(ff + 1) * 128], rhs=xnT[:, k, 0:BS],
                                                                     start=(k == 0), stop=(k == 7)),
                               [R(W1), R(xnT)], [R(pp)], sig=(k == 7))
                        ACT(lambda e, pp=pp, r_=r_: e.activation(out=r_[:, 0:BS], in_=pp[:, 0:BS], func=AF.Relu), [R(pp)], [R(r_)])
                        DVE(lambda e, pp=pp, r_=r_, f=f: e.tensor_tensor(out=h1T[:, f, 0:BS], in0=pp[:, 0:BS], in1=r_[:, 0:BS], op=ALU.mult),
                            [R(pp), R(r_)], [R(h1T)])
                    for s in range(NSUB):
                        for nh in range(2):
                            pp = (pC, pD)[nh]
                            for f in range(16):
                                PE(lambda e, f=f, s=s, nh=nh, pp=pp, fh=fh: e.matmul(pp[0:P, :], lhsT=h1T[:, f, s * P:(s + 1) * P],
                                                                                     rhs=W2[:, fh * 16 + f, nh * 512:(nh + 1) * 512],
                                                                                     start=(f == 0), stop=(f == 15)),
                                   [R(h1T), R(W2)], [R(pp)], sig=(f == 15))
                            DVE(lambda e, s=s, nh=nh, pp=pp: e.tensor_tensor(out=xb[0:P, s, nh * 512:(nh + 1) * 512], in0=pp[0:P, :],
                                                                             in1=xb[0:P, s, nh * 512:(nh + 1) * 512], op=ALU.add),
                                [R(pp), R(xb)], [R(xb)])
                wr = [G.name + "x1"] if l == 0 else []
                DMA("xow", x_dst[c0:c0 + BS, :].rearrange("(s p) d -> p s d", p=P), xb[0:P, 0:NSUB, :], [R(xb)], wr)

    S.emit()
    nc._n_ops = S.n_ops
    nc._peak = AR.peak
    return nc


def _rope_table(pos):
    half = 16
    inv = (np.float32(10000.0) ** (-(np.arange(half, dtype=np.float32)) / np.float32(half))).astype(np.float32)
    ang = (pos.astype(np.float32)[:, None] * inv[None, :]).astype(np.float32)
    return np.concatenate([np.cos(ang.astype(np.float64)), np.sin(ang.astype(np.float64))], axis=1).astype(np.float32)


_NC_CACHE = {}


def kernel(**inp):
    inp = {k: np.asarray(v) for k, v in inp.items()}
    B, SEQ, _ = inp["x_prompt"].shape
    NB = inp["x_sample"].shape[0]
    past = inp["cache_mla_latent"].shape[2]
    ncores = 8
    assert B * 2 == ncores and NB == ncores
    TP = SEQ // 2
    BLK = 512
    NBC = TP // BLK
    if TP not in _NC_CACHE:
        _NC_CACHE[TP] = build_nc(TP)
    nc = _NC_CACHE[TP]
    ident = np.eye(128, dtype=np.float32).astype(ml_dtypes.bfloat16)
    tril = np.tril(np.ones((128, 128), np.float32))
    pos_half = [np.concatenate([(2 * j + c) * BLK + np.arange(BLK) for j in range(NBC)]) for c in range(2)]
    cs_half = [_rope_table(p) for p in pos_half]
    cs_s = _rope_table(past + np.arange(TS))
    kk = np.arange(128)[:, None]
    qq = np.arange(BLK)[None, :]
    diag = np.stack([((r * 128 + kk) // 64 <= qq // 64) for r in range(4)], axis=1).astype(np.float32)
    masks = [np.concatenate([diag, np.zeros_like(diag)], axis=1), np.concatenate([np.ones_like(diag), diag], axis=1)]
    masks = [m.astype(ml_dtypes.bfloat16) for m in masks]
    wnames = ["norm_mix_g", "w_in", "a_norm_g", "a_ws", "a_bs", "b_dw_w", "b_dw_b", "b_ln_g", "b_ln_b", "c_qa_g", "c_w_uq",
              "c_kva_g", "c_w_ukv", "c_qn_g", "c_qr_g", "c_kn_g", "c_kr_g", "w_out", "norm_mem_g", "mem_norm_g", "w_mq",
              "w_mk", "w_mv", "w_mo", "m_q_g", "m_k_g", "norm_ffn_g", "w_ff1", "w_ff2"]
    shared = {n: np.ascontiguousarray(inp[n], dtype=np.float32) for n in wnames}
    shared.update(ident=ident, tril=tril, cs_s=cs_s)
    in_maps = []
    for c in range(ncores):
        b, half = c // 2, c % 2
        m = dict(shared)
        m["x_p"] = np.ascontiguousarray(inp["x_prompt"][b][pos_half[half]])
        m["cs_p"] = cs_half[half]
        m["flag"] = np.full((128, 1), float(half), np.float32)
        m["mask"] = masks[half]
        m["x_s"] = np.ascontiguousarray(inp["x_sample"][c])
        m["mem"] = np.ascontiguousarray(inp["mem_prompt"][b])
        m["c_lat"] = np.ascontiguousarray(inp["cache_mla_latent"][:, c])
        m["c_kr"] = np.ascontiguousarray(inp["cache_mla_krope"][:, c])
        m["c_conv"] = np.ascontiguousarray(inp["cache_conv"][:, c])
        m["c_mk"] = np.ascontiguousarray(inp["cache_mem_k"][:, c]).reshape(DEPTH, 256, 512)
        m["c_mv"] = np.ascontiguousarray(inp["cache_mem_v"][:, c]).reshape(DEPTH, 256, 512)
        in_maps.append(m)
    res = run_bass_kernel_spmd(nc, in_maps, core_ids=list(range(ncores)))
    r = res.results

    def halves(name, axis):
        outs = []
        for b in range(B):
            a0, a1 = r[2 * b][name], r[2 * b + 1][name]
            full = np.empty(a0.shape[:axis] + (SEQ,) + a0.shape[axis + 1:], a0.dtype)
            idx = [slice(None)] * full.ndim
            for half, arr in ((0, a0), (1, a1)):
                idx[axis] = pos_half[half]
                full[tuple(idx)] = arr
            outs.append(full)
        return np.stack(outs)
    y_p = halves("y_p", 0)
    y_s = np.stack([r[c]["y_s"] for c in range(NB)])
    lat_p = np.moveaxis(halves("o_lat_p", 1), 0, 1)
    kr_p = np.moveaxis(halves("o_kr_p", 1), 0, 1)
    conv_p = np.stack([r[2 * b + 1]["o_conv_p"] for b in range(B)], axis=1)
    mk_p = np.stack([r[2 * b]["o_mk_p"] for b in range(B)], axis=1).reshape(DEPTH, B, 256, 4, 128)
    mv_p = np.stack([r[2 * b]["o_mv_p"] for b in range(B)], axis=1).reshape(DEPTH, B, 256, 4, 128)
    lat_s = np.stack([r[c]["o_lat_s"] for c in range(NB)], axis=1)
    kr_s = np.stack([r[c]["o_kr_s"] for c in range(NB)], axis=1)
    conv_s = np.stack([r[c]["o_conv_s"] for c in range(NB)], axis=1)
    gv_s = np.stack([r[c]["o_gv_s"] for c in range(NB)], axis=1)
    outs = (y_p, y_s, lat_p, kr_p, conv_p, mk_p, mv_p, lat_s, kr_s, conv_s, gv_s)
    return tuple(np.ascontiguousarray(o, dtype=np.float32) for o in outs)
```

```python
import contextlib
import math
import types
import numpy as np
import ml_dtypes
import concourse.bass as bass
import concourse.mybir as mybir
from concourse.bass_utils import run_bass_kernel_spmd

F32 = mybir.dt.float32
BF16 = mybir.dt.bfloat16
ALU = mybir.AluOpType
AF = mybir.ActivationFunctionType
AX = mybir.AxisListType

D = 1024
DEPTH = 2
EPS = 1e-6
IN_W = 1440
NPAST = 1024
TS = 16
SBUF_LO = 16512
SBUF_HI = 229344


def _freeze(fn):
    if fn.__closure__ is None:
        return fn
    cells = []
    for c in fn.__closure__:
        try:
            cells.append(types.CellType(c.cell_contents))
        except ValueError:
            cells.append(c)
    return types.FunctionType(fn.__code__, fn.__globals__, fn.__name__, fn.__defaults__, tuple(cells))


class Sched:
    ENGS = ("pe", "act", "dve", "pool", "sp")

    def __init__(self, nc):
        self.nc = nc
        self.q = {e: [] for e in self.ENGS}
        self.cnt = {}
        self.last_w = {}
        self.readers = {}
        self.seen = {e: {} for e in self.ENGS}
        self.pending = {e: {} for e in self.ENGS}
        self.n_ops = 0

    def _deps(self, eng, reads, writes):
        need = dict(self.pending[eng])
        writes = list(writes) + [r for r in reads if r.startswith("pf") or r.startswith("pb")]
        self.pending[eng] = {}

        def add(tok):
            if tok is not None and need.get(tok[0], 0) < tok[1]:
                need[tok[0]] = tok[1]
        for r in reads:
            add(self.last_w.get(r))
        for w in writes:
            add(self.last_w.get(w))
            for t in self.readers.get(w, ()):
                add(t)
        seen = self.seen[eng]
        waits = []
        if eng == "pe":
            need.pop("pe", None)
        for k, v in need.items():
            if seen.get(k, 0) < v:
                seen[k] = v
                waits.append((k, v))
        return waits

    def _commit(self, tok, reads, writes):
        writes = list(writes) + [r for r in reads if r.startswith("pf") or r.startswith("pb")]
        for r in reads:
            self.readers.setdefault(r, []).append(tok)
        for w in writes:
            self.last_w[w] = tok
            self.readers[w] = []

    def op(self, eng, fn, reads=(), writes=(), signal=True):
        waits = self._deps(eng, reads, writes)
        if signal:
            self.cnt[eng] = self.cnt.get(eng, 0) + 1
            tok = (eng, self.cnt[eng])
        else:
            tok = (eng, self.cnt.get(eng, 0) + 1)
        self.q[eng].append((_freeze(fn), waits, eng, 1 if signal else 0))
        self._commit(tok, reads, writes)
        self.n_ops += 1
        return tok

    def dma(self, tag, fn, reads=(), writes=(), eng="sp", inc=16):
        key = "dma:" + tag
        if self.cnt.get(key, 0):
            p = self.pending[eng]
            p[key] = max(p.get(key, 0), self.cnt[key])
        waits = self._deps(eng, reads, writes)
        self.cnt[key] = self.cnt.get(key, 0) + inc
        tok = (key, self.cnt[key])
        self.q[eng].append((_freeze(fn), waits, key, inc))
        self._commit(tok, reads, writes)
        self.n_ops += 1
        return tok

    def barrier(self):
        for e in self.ENGS:
            p = self.pending[e]
            for k, v in self.cnt.items():
                if p.get(k, 0) < v:
                    p[k] = v

    def emit(self):
        nc = self.nc
        with contextlib.ExitStack() as st:
            sems = {}
            for k in self.cnt:
                sems[k] = st.enter_context(nc.semaphore("s_" + k.replace(":", "_")))
            block = st.enter_context(nc.Block())
            fin = list(self.cnt.items())

            def run(name, e):
                for fn, waits, key, inc in self.q[name]:
                    for k, v in waits:
                        e.wait_ge(sems[k], v)
                    ins = fn(e)
                    if inc:
                        ins.then_inc(sems[key], inc)
                if name == "sp":
                    for k, v in fin:
                        e.wait_ge(sems[k], v)

            @block.tensor
            def _(e):
                run("pe", e)

            @block.scalar
            def _(e):
                run("act", e)

            @block.vector
            def _(e):
                run("dve", e)

            @block.gpsimd
            def _(e):
                run("pool", e)

            @block.sync
            def _(e):
                run("sp", e)


class Arena:
    def __init__(self, nc):
        self.nc = nc
        self.off = SBUF_LO
        self.n = 0
        self.peak = 0

    def alloc(self, name, shape, dt):
        per = 1
        for s in shape[1:]:
            per *= s
        nb = per * (4 if dt == F32 else 2)
        nb = (nb + 63) // 64 * 64
        assert self.off + nb <= SBUF_HI, f"SBUF overflow at {name}: {self.off + nb}"
        self.n += 1
        t = self.nc.alloc_sbuf_tensor_at(f"{name}_{self.n}", list(shape), dt, offset=self.off)
        self.off += nb
        self.peak = max(self.peak, self.off)
        return t

    def mark(self):
        return self.off

    def reset(self, m):
        self.off = m


class Grp:
    pass


def build_nc(TP):
    nc = bass.Bass("TRN2", target_bir_lowering=False)
    S = Sched(nc)
    AR = Arena(nc)

    def din(name, shape, dt=F32):
        return nc.dram_tensor(name, list(shape), dt, kind="ExternalInput").ap()

    def dout(name, shape, dt=F32):
        return nc.dram_tensor(name, list(shape), dt, kind="ExternalOutput").ap()

    def dscr(name, shape, dt):
        return nc.dram_tensor(name, list(shape), dt).ap()

    x_p = din("x_p", [TP, D]); x_s = din("x_s", [TS, D]); mem_in = din("mem", [256, D])
    c_lat = din("c_lat", [DEPTH, NPAST, 128]); c_kr = din("c_kr", [DEPTH, NPAST, 32])
    c_conv = din("c_conv", [DEPTH, 30, 256])
    c_mk = din("c_mk", [DEPTH, 256, 512]); c_mv = din("c_mv", [DEPTH, 256, 512])
    W = {}
    for nm, shp in [("norm_mix_g", [DEPTH, D]), ("w_in", [DEPTH, D, IN_W]), ("a_norm_g", [DEPTH, 256]),
                    ("a_ws", [DEPTH, 4, 128, 128]), ("a_bs", [DEPTH, 4, 128]), ("b_dw_w", [DEPTH, 31, 256]),
                    ("b_dw_b", [DEPTH, 256]), ("b_ln_g", [DEPTH, 256]), ("b_ln_b", [DEPTH, 256]),
                    ("c_qa_g", [DEPTH, 256]), ("c_w_uq", [DEPTH, 256, 768]), ("c_kva_g", [DEPTH, 128]),
                    ("c_w_ukv", [DEPTH, 128, 1024]), ("c_qn_g", [DEPTH, 64]), ("c_qr_g", [DEPTH, 32]),
                    ("c_kn_g", [DEPTH, 64]), ("c_kr_g", [DEPTH, 32]), ("w_out", [DEPTH, D, D]),
                    ("norm_mem_g", [DEPTH, D]), ("mem_norm_g", [DEPTH, D]), ("w_mq", [DEPTH, D, 512]),
                    ("w_mk", [DEPTH, D, 512]), ("w_mv", [DEPTH, D, 512]), ("w_mo", [DEPTH, 512, D]),
                    ("m_q_g", [DEPTH, 128]), ("m_k_g", [DEPTH, 128]), ("norm_ffn_g", [DEPTH, D]),
                    ("w_ff1", [DEPTH, D, 4 * D]), ("w_ff2", [DEPTH, 4 * D, D])]:
        W[nm] = din(nm, shp)
    ident_d = din("ident", [128, 128], BF16)
    tril_d = din("tril", [128, 128])
    cs_p = din("cs_p", [TP, 32]); cs_s = din("cs_s", [TS, 32])
    flag_d = din("flag", [128, 1])
    mask_d = din("mask", [128, 8, 512], BF16)

    y_p = dout("y_p", [TP, D]); y_s = dout("y_s", [TS, D])
    o_lat_p = dout("o_lat_p", [DEPTH, TP, 128]); o_kr_p = dout("o_kr_p", [DEPTH, TP, 32])
    o_conv_p = dout("o_conv_p", [DEPTH, 30, 256])
    o_mk_p = dout("o_mk_p", [DEPTH, 256, 512]); o_mv_p = dout("o_mv_p", [DEPTH, 256, 512])
    o_lat_s = dout("o_lat_s", [DEPTH, TS, 128]); o_kr_s = dout("o_kr_s", [DEPTH, TS, 32])
    o_conv_s = dout("o_conv_s", [DEPTH, 30, 256]); o_gv_s = dout("o_gv_s", [DEPTH, TS, 256])

    def mk_group(name, T, NP, x_in, cs, y_out, o_lat, o_kr, o_conv):
        G = Grp()
        G.name, G.T, G.NP = name, T, NP
        G.P = min(128, T); G.BS = min(512, T); G.NSUB = G.BS // G.P; G.NBLK = T // G.BS
        G.NK = NP + T
        G.x_in, G.cs, G.y_out, G.o_lat, G.o_kr, G.o_conv = x_in, cs, y_out, o_lat, o_kr, o_conv
        G.QT = dscr("QT_" + name, [8, 128, T], BF16)
        G.mixT = dscr("mixT_" + name, [D, T], BF16)
        G.x1 = dscr("x1_" + name, [T, D], F32)
        G.xm = dscr("xm_" + name, [T, D], F32)
        G.xnf = dscr("xnf_" + name, [D, T], BF16)
        G.remote = False
        return G
    GP = mk_group("p", TP, TP, x_p, cs_p, y_p, o_lat_p, o_kr_p, o_conv_p)
    GP.remote = True
    NBP = TP // 512
    GP.xin = [nc.dram_tensor("xin_l", [128, TP], BF16), nc.dram_tensor("xin_k", [32, TP], BF16), nc.dram_tensor("xin_z", [128, NBP * 60], BF16)]
    GP.xout = [nc.dram_tensor("xout_l", [256, TP], BF16), nc.dram_tensor("xout_k", [64, TP], BF16), nc.dram_tensor("xout_z", [256, NBP * 60], BF16)]
    GS = mk_group("s", TS, NPAST, x_s, cs_s, y_s, o_lat_s, o_kr_s, o_conv_s)
    GROUPS = [GP, GS]

    PF = [nc.alloc_psum_tensor(f"pf{i}", [128, 512], F32) for i in range(6)]
    PB = [nc.alloc_psum_tensor(f"pb{i}", [128, 8, 128], BF16) for i in range(2)]

    rot = {"i": 0}

    def R(t):
        return t.name

    def PE(fn, r, w, sig=True):
        return S.op("pe", fn, r, w, signal=sig)

    def ACT(fn, r, w):
        return S.op("act", fn, r, w)

    def DVE(fn, r, w):
        return S.op("dve", fn, r, w)

    def POOL(fn, r, w):
        return S.op("pool", fn, r, w)

    def DMA(tag, out, in_, r, w, eng="sp", slow=False):
        if slow:
            return S.dma(tag, lambda e: e.dma_start(out=out, in_=in_, allow_slow_non_contiguous=True), r, w, eng)
        return S.dma(tag, lambda e: e.dma_start(out=out, in_=in_), r, w, eng)

    def rsqrt_inplace(ap, n, names):
        ACT(lambda e: e.activation(out=ap, in_=ap, func=AF.Ln, bias=EPS, scale=1.0 / n), names, names)
        ACT(lambda e: e.activation(out=ap, in_=ap, func=AF.Exp, scale=-0.5), names, names)

    ident = AR.alloc("ident", [128, 128], BF16)
    ident_f = AR.alloc("identf", [128, 128], F32)
    tril = AR.alloc("tril", [128, 128], F32)
    ones_bf = AR.alloc("ones_bf", [128, 128], BF16)
    ones_f = AR.alloc("ones_f", [128, 64], F32)
    stg = [AR.alloc(f"stg{i}", [128, 1024], F32) for i in range(2)]
    DMA("c0", ident[:], ident_d, [], [R(ident)])
    DMA("c0", tril[:], tril_d, [], [R(tril)])
    flag = AR.alloc("flag", [128, 1], F32)
    nflag = AR.alloc("nflag", [128, 1], F32)
    DMA("c0", flag[:], flag_d, [], [R(flag)])
    POOL(lambda e: e.tensor_scalar(out=nflag[:], in0=flag[:], scalar1=-1.0, scalar2=1.0, op0=ALU.mult, op1=ALU.add), [R(flag)], [R(nflag)])
    POOL(lambda e: e.memset(ones_bf[:], 1.0), [], [R(ones_bf)])
    POOL(lambda e: e.memset(ones_f[:], 1.0), [], [R(ones_f)])
    POOL(lambda e: e.tensor_copy(out=ident_f[:], in_=ident[:]), [R(ident)], [R(ident_f)])
    BASE0 = AR.mark()

    cast_rot = {"i": 0}

    def cast(out, in_, r, w, gain=None):
        i = cast_rot["i"] = (cast_rot["i"] + 1) % 3
        if gain is None:
            if i <= 1:
                DVE(lambda e: e.tensor_copy(out=out, in_=in_), r, w)
            else:
                ACT(lambda e: e.activation(out=out, in_=in_, func=AF.Copy), r, w)
        else:
            if i == 0:
                DVE(lambda e: e.tensor_single_scalar(out=out, in_=in_, scalar=gain, op=ALU.mult), r, w)
            elif i == 1:
                DVE(lambda e: e.tensor_single_scalar(out=out, in_=in_, scalar=gain, op=ALU.mult), r, w)
            else:
                ACT(lambda e: e.activation(out=out, in_=in_, func=AF.Copy, scale=gain), r, w)

    stg_i = {"i": 0}

    def load_w(dst, src2d, KT, N, gain=None, gname=None):
        CW = min(N, 1024)
        for k in range(KT):
            for c0 in range(0, N, CW):
                cw = min(CW, N - c0)
                st = stg[stg_i["i"] % 2]
                stg_i["i"] += 1
                DMA("stg" + str(stg_i["i"] % 2), st[:, 0:cw], src2d[k * 128:(k + 1) * 128, c0:c0 + cw], [], [R(st)])
                rr = [R(st)] + ([gname] if gain is not None else [])
                cast(dst[:, k, c0:c0 + cw], st[:, 0:cw], rr, [R(dst)],
                     gain=None if gain is None else gain[:, k:k + 1])

    def load_gain_pk(dst, vec, KT):
        DMA("small", dst[:, 0:KT], vec.rearrange("(k p) -> p k", p=128), [], [R(dst)], slow=True)

    def load_bcast(dst, vec, n, P=128):
        DMA("small", dst[0:P, 0:n], vec.partition_broadcast(P), [], [R(dst)])

    def transposes(pb, src, nslot, P, r):
        for j in range(nslot):
            PE(lambda e, j=j: e.transpose(out=pb[:, j, 0:P], in_=src[0:P, j, :], identity=ident[0:P, 0:P]),
               r + [R(ident)], [R(pb)], sig=(j == nslot - 1))

    def rope(out1, out2, x1, x2, cos, sin, tmp, r, w):
        (t1, n1), (t2, n2), (t3, n3), (t4, n4) = tmp
        DVE(lambda e: e.tensor_tensor(out=t1, in0=x1, in1=cos, op=ALU.mult), r, [n1])
        POOL(lambda e: e.tensor_tensor(out=t2, in0=x2, in1=sin, op=ALU.mult), r, [n2])
        DVE(lambda e: e.tensor_tensor(out=t3, in0=x1, in1=sin, op=ALU.mult), r, [n3])
        POOL(lambda e: e.tensor_tensor(out=t4, in0=x2, in1=cos, op=ALU.mult), r, [n4])
        DVE(lambda e: e.tensor_tensor(out=out1, in0=t1, in1=t2, op=ALU.subtract), [n1, n2] + w, w)
        POOL(lambda e: e.tensor_tensor(out=out2, in0=t3, in1=t4, op=ALU.add), [n3, n4] + w, w)

    def rms_to_bf16(x_ap, xn_ap, junk_ap, ss_ap, P, n, r, names, on_pool=False):
        ACT(lambda e: e.activation(out=junk_ap, in_=x_ap, func=AF.Square, accum_out=ss_ap), r, [names[0], names[1]])
        rsqrt_inplace(ss_ap, n, [names[1]])
        if on_pool:
            DVE(lambda e: e.tensor_single_scalar(out=xn_ap, in_=x_ap, scalar=ss_ap, op=ALU.mult), r + [names[1]], [names[2]])
        else:
            ACT(lambda e: e.activation(out=xn_ap, in_=x_ap, func=AF.Copy, scale=ss_ap), r + [names[1]], [names[2]])

    for l in range(DEPTH):
        S.barrier()
        AR.reset(BASE0)
        for G in GROUPS:
            G.latT = AR.alloc("latT" + G.name, [128, G.NK], BF16)
            G.krT = AR.alloc("krT" + G.name, [128, G.NK], BF16)
        BASE = AR.mark()
        Win = AR.alloc("Win", [128, 8, IN_W], BF16)
        Wuq = AR.alloc("Wuq", [128, 2, 768], BF16)
        wmT = AR.alloc("wmT", [128, 4, 128], BF16)
        cdiag = AR.alloc("cdiag", [128, 2, 31, 128], BF16)
        g_mix = AR.alloc("g_mix", [128, 8], F32)
        g_qa = AR.alloc("g_qa", [128, 2], F32)
        ga_bc = AR.alloc("ga_bc", [128, 256], F32)
        gqn_bc = AR.alloc("gqn_bc", [128, 64], F32)
        gqr_bc = AR.alloc("gqr_bc", [128, 32], F32)
        gkva_bc = AR.alloc("gkva_bc", [128, 128], F32)
        gkr_bc = AR.alloc("gkr_bc", [128, 32], F32)
        bsT = AR.alloc("bsT", [128, 4], F32)
        bdw = AR.alloc("bdw", [128, 2], F32)
        lng = AR.alloc("lng", [128, 2], F32)
        lnb = AR.alloc("lnb", [128, 2], F32)
        wdw = AR.alloc("wdw", [128, 2, 31], F32)
        wnat = AR.alloc("wnat", [128, 256], F32)
        wnat_b = AR.alloc("wnat_b", [128, 2, 128], BF16)
        wsb = AR.alloc("wsb", [128, 4, 128], BF16)

        load_gain_pk(g_mix, W["norm_mix_g"][l], 8)
        load_gain_pk(g_qa, W["c_qa_g"][l], 2)
        load_gain_pk(bdw, W["b_dw_b"][l], 2)
        load_gain_pk(lng, W["b_ln_g"][l], 2)
        load_gain_pk(lnb, W["b_ln_b"][l], 2)
        load_bcast(ga_bc, W["a_norm_g"][l], 256)
        load_bcast(gqn_bc, W["c_qn_g"][l], 64)
        load_bcast(gqr_bc, W["c_qr_g"][l], 32)
        load_bcast(gkva_bc, W["c_kva_g"][l], 128)
        load_bcast(gkr_bc, W["c_kr_g"][l], 32)
        DMA("small", bsT[:, :], W["a_bs"][l].rearrange("g i -> i g"), [], [R(bsT)], slow=True)
        load_w(Win, W["w_in"][l], 8, IN_W, gain=g_mix, gname=R(g_mix))
        for k in range(2):
            st = stg[stg_i["i"] % 2]
            stg_i["i"] += 1
            src = W["c_w_uq"][l][k * 128:(k + 1) * 128, :].rearrange("p (h d) -> p h d", d=96)
            tg = "stg" + str(stg_i["i"] % 2)
            DMA(tg, st[:, 0:512].rearrange("p (h d) -> p h d", d=64), src[:, :, 0:64], [], [R(st)])
            DMA(tg, st[:, 512:768].rearrange("p (h d) -> p h d", d=32), src[:, :, 64:96], [], [R(st)])
            cast(Wuq[:, k, :], st[:, 0:768], [R(st), R(g_qa)], [R(Wuq)], gain=g_qa[:, k:k + 1])
        for g in range(4):
            st = stg[stg_i["i"] % 2]
            stg_i["i"] += 1
            DMA("stg" + str(stg_i["i"] % 2), st[:, 0:128], W["a_ws"][l, g], [], [R(st)])
            DVE(lambda e, st=st, g=g: e.tensor_tensor(out=wsb[:, g, :], in0=st[:, 0:128], in1=tril[:], op=ALU.mult),
                [R(st), R(tril)], [R(wsb)])
        transposes(PB[0], wsb, 4, 128, [R(wsb)])
        DVE(lambda e: e.tensor_copy(out=wmT[:], in_=PB[0][:, 0:4, 0:128]), [R(PB[0])], [R(wmT)])
        POOL(lambda e: e.memset(wnat[:], 0.0), [], [R(wnat)])
        DMA("small", wnat[0:31, :], W["b_dw_w"][l], [R(wnat)], [R(wnat)])
        POOL(lambda e: e.tensor_copy(out=wnat_b[:].rearrange("p a b -> p (a b)"), in_=wnat[:]), [R(wnat)], [R(wnat_b)])
        transposes(PB[0], wnat_b, 2, 128, [R(wnat_b)])
        DVE(lambda e: e.tensor_copy(out=wdw[:], in_=PB[0][:, 0:2, 0:31]), [R(PB[0])], [R(wdw)])
        for ft in range(2):
            for k in range(31):
                eng = POOL if (k % 2 == 0) else DVE
                eng(lambda e, ft=ft, k=k: e.tensor_single_scalar(out=cdiag[:, ft, k, :], in_=ident_f[:],
                                                                 scalar=wdw[:, ft, k:k + 1], op=ALU.mult),
                    [R(ident_f), R(wdw)], [R(cdiag)])

        S1 = AR.mark()
        for G in GROUPS:
            S.barrier()
            AR.reset(S1)
            P, BS, NSUB, T, NP = G.P, G.BS, G.NSUB, G.T, G.NP
            latT, krT = G.latT, G.krT
            x_src = G.x_in if l == 0 else G.x1
            xs = [AR.alloc("x_sb", [128, D], F32) for _ in range(3)]
            junk_2 = [AR.alloc("junk", [128, D], F32)] * 2
            ss_2 = [AR.alloc("ss", [128, 8], F32) for _ in range(2)]
            xn_2 = [AR.alloc("xn", [128, 8, 128], BF16) for _ in range(2)]
            xnT_2 = [AR.alloc("xnT", [128, 8, 128], BF16) for _ in range(2)]
            u_sb_2 = [AR.alloc("u_sb", [128, 256], F32) for _ in range(2)]
            gv_sb_2 = [AR.alloc("gv_sb", [128, 256], F32) for _ in range(2)]
            v32_2 = [AR.alloc("v32", [128, 256], F32) for _ in range(2)]
            v_bf_2 = [AR.alloc("v_bf", [128, 256], BF16) for _ in range(2)]
            sg_2 = [AR.alloc("sg", [128, 256], F32) for _ in range(2)]
            z32_2 = [AR.alloc("z32", [128, 256], F32) for _ in range(2)]
            TM_2 = [AR.alloc("TM", [128, 8, 128], BF16) for _ in range(2)]
            cqnT_2 = [AR.alloc("cqnT", [128, 2, 128], BF16) for _ in range(2)]
            sq_2 = [AR.alloc("sq", [128, 768], F32) for _ in range(2)]
            ssn_2 = [AR.alloc("ssn", [128, 16], F32) for _ in range(2)]
            qn32_2 = [AR.alloc("qn32", [128, 8, 64], F32) for _ in range(2)]
            qr32_2 = [AR.alloc("qr32", [128, 8, 32], F32) for _ in range(2)]
            rt_2 = [[AR.alloc(f"rt{i}", [128, 8, 16], F32) for i in range(4)] for _ in range(2)]
            Qtm_2 = [AR.alloc("Qtm", [128, 8, 128], BF16) for _ in range(2)]
            lat32_2 = [AR.alloc("lat32", [128, 128], F32) for _ in range(2)]
            kr32_2 = [AR.alloc("kr32", [128, 32], F32) for _ in range(2)]
            kro32_2 = [AR.alloc("kro32", [128, 32], F32) for _ in range(2)]
            cst_2 = [AR.alloc("cst", [128, 32], F32) for _ in range(4)]
            mixblk = AR.alloc("mixblk", [128, 4, 512], BF16)
            QTblk = AR.alloc("QTblk", [128, 8, 512], BF16)
            zTb = AR.alloc("zTb", [128, 2, 30 + 512], BF16)
            y32 = AR.alloc("y32", [128, 2, 512], F32)
            ybf = AR.alloc("ybf", [128, 2, 512], BF16)
            ysq = AR.alloc("ysq", [128, 2, 512], BF16)
            mean = AR.alloc("mean", [128, 512], F32)
            msq = AR.alloc("msq", [128, 512], F32)
            rstd = AR.alloc("rstd", [128, 512], F32)
            tln = AR.alloc("tln", [128, 512], F32)
            hb0, hb1, cv0, cv1 = PF[0], PF[1], PF[3], PF[4]
            HB = [PF[0:3], PF[3:6]]
            AUXF = [p_[:].rearrange("p a b -> p (a b)").bitcast(F32) for p_ in PB]
            pbA, pbB = PB

            POOL(lambda e: e.memset(TM_2[0][:], 0.0), [], [R(TM_2[0])])
            POOL(lambda e: e.memset(TM_2[1][:], 0.0), [], [R(TM_2[1])])
            POOL(lambda e: e.memset(Qtm_2[0][:], 0.0), [], [R(Qtm_2[0])])
            POOL(lambda e: e.memset(Qtm_2[1][:], 0.0), [], [R(Qtm_2[1])])
            zTs = [zTb] * G.NBLK
            if G.remote:
                zTs = [zTb] + [AR.alloc("zTall", [128, 2, 30 + 512], BF16) for _ in range(G.NBLK - 1)]
                mixblk0_2 = [AR.alloc("mixblk0", [128, 2, 512], BF16) for _ in range(2)]
                tz = AR.alloc("tz", [128, 2, G.NBLK * 60], BF16)
            elif NP > 0:
                npt = NP // 128
                pl = AR.alloc("pl", [128, npt, 128], F32)
                pk_ = AR.alloc("pk", [128, npt, 32], F32)
                plb = AR.alloc("plb", [128, npt, 128], BF16)
                pkb = AR.alloc("pkb", [128, npt, 128], BF16)
                DMA("past", pl[:], c_lat[l].rearrange("(t p) c -> p t c", p=128), [], [R(pl)])
                DMA("past", pk_[:], c_kr[l].rearrange("(t p) c -> p t c", p=128), [], [R(pk_)])
                POOL(lambda e: e.tensor_copy(out=plb[:], in_=pl[:]), [R(pl)], [R(plb)])
                POOL(lambda e: e.memset(pkb[:], 0.0), [], [R(pkb)])
                POOL(lambda e: e.tensor_copy(out=pkb[:, :, 64:96], in_=pk_[:]), [R(pk_), R(pkb)], [R(pkb)])
                transposes(pbA, plb, npt, 128, [R(plb)])
                DVE(lambda e: e.tensor_copy(out=latT[:, 0:NP].rearrange("p (t c) -> p t c", c=128), in_=pbA[:, 0:npt, :]),
                    [R(pbA)], [R(latT)])
                transposes(pbA, pkb, npt, 128, [R(pkb)])
                DVE(lambda e: e.tensor_copy(out=krT[:, 0:NP].rearrange("p (t c) -> p t c", c=128), in_=pbA[:, 0:npt, :]),
                    [R(pbA)], [R(krT)])
                ch = AR.alloc("ch", [128, 256], F32)
                chb = AR.alloc("chb", [128, 2, 128], BF16)
                DMA("past", ch[0:30, :], c_conv[l], [], [R(ch)])
                POOL(lambda e: e.tensor_copy(out=chb[0:30].rearrange("p a b -> p (a b)"), in_=ch[0:30, :]), [R(ch)], [R(chb)])
                for ft in range(2):
                    PE(lambda e, ft=ft: e.transpose(out=pbA[:, ft, 0:30], in_=chb[0:30, ft, :], identity=ident[0:30, 0:30]),
                       [R(chb), R(ident)], [R(pbA)])
                DVE(lambda e: e.tensor_copy(out=zTb[:, :, 0:30], in_=pbA[:, 0:2, 0:30]), [R(pbA)], [R(zTb)])
                DMA("oc", G.o_conv[l, 0:14, :], c_conv[l, 16:30, :], [], [])
            else:
                POOL(lambda e: e.memset(zTb[:, :, 0:30], 0.0), [], [R(zTb)])

            def load_sub(n_):
                if n_ >= G.NBLK * NSUB:
                    return
                t0_ = n_ * P
                DMA("x" + str(n_ % 3), xs[n_ % 3][0:P, :], x_src[t0_:t0_ + P, :], [G.name + "x1"] if l else [], [R(xs[n_ % 3])])
                DMA("cs" + str(n_ % 4), cst_2[n_ % 4][0:P, :], G.cs[t0_:t0_ + P, :], [], [R(cst_2[n_ % 4])])
            load_sub(0)
            load_sub(1)
            for blk in range(G.NBLK):
                def subtile(blk, s, par):
                    (junk, ss, xn, xnT, u_sb, gv_sb, v32, v_bf, sg, z32, TM, cqnT, sq, ssn, qn32, qr32, Qtm, lat32, kr32, kro32, cst) = (
                        junk_2[par], ss_2[par], xn_2[par], xnT_2[par], u_sb_2[par], gv_sb_2[par], v32_2[par], v_bf_2[par], sg_2[par], z32_2[par],
                        TM_2[par], cqnT_2[par], sq_2[par], ssn_2[par], qn32_2[par], qr32_2[par], Qtm_2[par], lat32_2[par], kr32_2[par], kro32_2[par], cst_2[(blk * NSUB + s) % 4])
                    rt = rt_2[par]
                    hb0, hb1, hb2 = HB[par]
                    pm = AUXF[par]
                    pbA = pbB = PB[par]
                    ssx = f"ssx{par}_"
                    zTb = zTs[blk]
                    t0 = blk * BS + s * P
                    x_sb = xs[(blk * NSUB + s) % 3]
                    load_sub(blk * NSUB + s + 2)
                    rms_to_bf16(x_sb[0:P, :], xn[0:P].rearrange("p a b -> p (a b)"), junk[0:P, :], ss[0:P, 0:1], P, D,
                                [R(x_sb)], [R(junk), ssx + "ss0", R(xn)], on_pool=True)
                    yield
                    transposes(pbA, xn, 8, P, [R(xn)])
                    DVE(lambda e: e.tensor_copy(out=xnT[:, :, 0:P], in_=pbA[:, :, 0:P]), [R(pbA)], [R(xnT)])
                    yield
                    for c, (c0, cw) in enumerate([(0, 512), (512, 512), (1024, 416)]):
                        hb = HB[par][c]
                        for k in range(8):
                            PE(lambda e, hb=hb, k=k, c0=c0, cw=cw: e.matmul(hb[0:P, 0:cw], lhsT=xnT[:, k, 0:P], rhs=Win[:, k, c0:c0 + cw],
                                                                            start=(k == 0), stop=(k == 7)),
                               [R(xnT), R(Win)], [R(hb)], sig=(k == 7))
                    yield
                    ACT(lambda e: e.activation(out=u_sb[0:P, :], in_=hb0[0:P, 0:256], func=AF.Gelu), [R(hb0)], [R(u_sb)])
                    ACT(lambda e: e.activation(out=gv_sb[0:P, :], in_=hb0[0:P, 256:512], func=AF.Gelu), [R(hb0)], [R(gv_sb)])
                    ACT(lambda e: e.activation(out=junk[0:P, 0:256], in_=gv_sb[0:P, :], func=AF.Square, accum_out=ss[0:P, 1:2]),
                        [R(gv_sb)], [R(junk), ssx + "ss1"])
                    rsqrt_inplace(ss[0:P, 1:2], 256, [ssx + "ss1"])
                    DVE(lambda e: e.scalar_tensor_tensor(out=v32[0:P, :], in0=gv_sb[0:P, :], scalar=ss[0:P, 1:2], in1=ga_bc[0:P, :],
                                                         op0=ALU.mult, op1=ALU.mult),
                        [R(gv_sb), ssx + "ss1", R(ga_bc)], [R(v32)])
                    if G is GS:
                        DMA("ogv", o_gv_s[l, t0:t0 + P, :], v32[0:P, :], [R(v32)], [])
                    yield
                    POOL(lambda e: e.tensor_copy(out=v_bf[0:P, :], in_=v32[0:P, :]), [R(v32)], [R(v_bf)])
                    for g in range(4):
                        PE(lambda e, g=g: e.matmul(pm[0:P, g * 64:(g + 1) * 64], lhsT=wmT[0:P, g, 0:P], rhs=v_bf[0:P, g * 64:(g + 1) * 64],
                                                   start=True, stop=True),
                           [R(wmT), R(v_bf)], [R(pm)])
                    for g in range(4):
                        DVE(lambda e, g=g: e.scalar_tensor_tensor(out=TM[0:P, g // 2, (g % 2) * 64:(g % 2) * 64 + 64],
                                                                  in0=pm[0:P, g * 64:(g + 1) * 64], scalar=bsT[0:P, g:g + 1],
                                                                  in1=u_sb[0:P, g * 64:(g + 1) * 64], op0=ALU.add, op1=ALU.mult),
                            [R(pm), R(bsT), R(u_sb)], [R(TM)])
                    yield
                    ACT(lambda e: e.activation(out=sg[0:P, :], in_=hb1[0:P, 256:512], func=AF.Sigmoid), [R(hb1)], [R(sg)])
                    DVE(lambda e: e.tensor_tensor(out=z32[0:P, :], in0=hb1[0:P, 0:256], in1=sg[0:P, :], op=ALU.mult),
                        [R(hb1), R(sg)], [R(z32)])
                    POOL(lambda e: e.tensor_copy(out=TM[0:P, 2:4, :].rearrange("p a b -> p (a b)"), in_=z32[0:P, :]), [R(z32), R(TM)], [R(TM)])
                    if blk == G.NBLK - 1 and s == NSUB - 1:
                        if NP > 0 and not G.remote:
                            DMA("oc", G.o_conv[l, 14:30, :], z32[0:16, :], [R(z32)], [])
                        else:
                            DMA("oc", G.o_conv[l, :, :], z32[P - 30:P, :], [R(z32)], [])
                    yield
                    ACT(lambda e: e.activation(out=junk[0:P, 0:256], in_=hb2[0:P, 0:256], func=AF.Square, accum_out=ss[0:P, 2:3]),
                        [R(hb2)], [R(junk), ssx + "ss2"])
                    rsqrt_inplace(ss[0:P, 2:3], 256, [ssx + "ss2"])
                    ACT(lambda e: e.activation(out=TM[0:P, 4:6, :].rearrange("p a b -> p (a b)"), in_=hb2[0:P, 0:256], func=AF.Copy, scale=ss[0:P, 2:3]),
                        [R(hb2), ssx + "ss2", R(TM)], [R(TM)])
                    yield
                    ACT(lambda e: e.activation(out=junk[0:P, 0:128], in_=hb2[0:P, 256:384], func=AF.Square, accum_out=ss[0:P, 3:4]),
                        [R(hb2)], [R(junk), ssx + "ss3"])
                    rsqrt_inplace(ss[0:P, 3:4], 128, [ssx + "ss3"])
                    DVE(lambda e: e.scalar_tensor_tensor(out=lat32[0:P, :], in0=hb2[0:P, 256:384], scalar=ss[0:P, 3:4], in1=gkva_bc[0:P, :],
                                                         op0=ALU.mult, op1=ALU.mult),
                        [R(hb2), ssx + "ss3", R(gkva_bc)], [R(lat32)])
                    DMA("olat", G.o_lat[l, t0:t0 + P, :], lat32[0:P, :], [R(lat32)], [])
                    POOL(lambda e: e.tensor_copy(out=TM[0:P, 6, :], in_=lat32[0:P, :]), [R(lat32), R(TM)], [R(TM)])
                    yield
                    ACT(lambda e: e.activation(out=junk[0:P, 0:32], in_=hb2[0:P, 384:416], func=AF.Square, accum_out=ss[0:P, 4:5]),
                        [R(hb2)], [R(junk), ssx + "ss4"])
                    rsqrt_inplace(ss[0:P, 4:5], 32, [ssx + "ss4"])
                    DVE(lambda e: e.scalar_tensor_tensor(out=kr32[0:P, :], in0=hb2[0:P, 384:416], scalar=ss[0:P, 4:5], in1=gkr_bc[0:P, :],
                                                         op0=ALU.mult, op1=ALU.mult),
                        [R(hb2), ssx + "ss4", R(gkr_bc)], [R(kr32)])
                    yield
                    rope(kro32[0:P, 0:16], kro32[0:P, 16:32], kr32[0:P, 0:16], kr32[0:P, 16:32], cst[0:P, 0:16], cst[0:P, 16:32],
                         [(rt[i][0:P, 0, :], R(rt[i])) for i in range(4)], [R(kr32), R(cst)], [R(kro32)])
                    DMA("okr", G.o_kr[l, t0:t0 + P, :], kro32[0:P, :], [R(kro32)], [])
                    POOL(lambda e: e.tensor_copy(out=TM[0:P, 7, 64:96], in_=kro32[0:P, :]), [R(kro32), R(TM)], [R(TM)])
                    yield
                    transposes(pbB, TM, 8, P, [R(TM)])
                    DVE(lambda e, s=s: e.tensor_copy(out=mixblk[:, 0:2, s * P:(s + 1) * P], in_=pbB[:, 0:2, 0:P]), [R(pbB)], [R(mixblk)])
                    DVE(lambda e, s=s: e.tensor_copy(out=zTb[:, :, 30 + s * P:30 + (s + 1) * P], in_=pbB[:, 2:4, 0:P]), [R(pbB)], [R(zTb)])
                    DVE(lambda e: e.tensor_copy(out=cqnT[:, :, 0:P], in_=pbB[:, 4:6, 0:P]), [R(pbB)], [R(cqnT)])
                    DVE(lambda e, t0=t0: e.tensor_copy(out=latT[:, NP + t0:NP + t0 + P], in_=pbB[:, 6, 0:P]), [R(pbB)], [R(latT)])
                    DVE(lambda e, t0=t0: e.tensor_copy(out=krT[:, NP + t0:NP + t0 + P], in_=pbB[:, 7, 0:P]), [R(pbB)], [R(krT)])
                    yield
                    for (hb, c0, cw) in [(hb0, 0, 512), (hb1, 512, 256)]:
                        for k in range(2):
                            PE(lambda e, hb=hb, k=k, c0=c0, cw=cw: e.matmul(hb[0:P, 0:cw], lhsT=cqnT[:, k, 0:P], rhs=Wuq[:, k, c0:c0 + cw],
                                                                            start=(k == 0), stop=(k == 1)),
                               [R(cqnT), R(Wuq)], [R(hb)], sig=(k == 1))
                    yield
                    ACT(lambda e: e.activation(out=sq[0:P, 0:512], in_=hb0[0:P, :], func=AF.Square), [R(hb0)], [R(sq)])
                    ACT(lambda e: e.activation(out=sq[0:P, 512:768], in_=hb1[0:P, 0:256], func=AF.Square), [R(hb1)], [R(sq)])
                    DVE(lambda e: e.tensor_reduce(out=ssn[0:P, 0:8], in_=sq[0:P, 0:512].rearrange("p (h d) -> p h d", d=64), axis=AX.X, op=ALU.add),
                        [R(sq)], [ssx + "ssn"])
                    DVE(lambda e: e.tensor_reduce(out=ssn[0:P, 8:16], in_=sq[0:P, 512:768].rearrange("p (h d) -> p h d", d=32), axis=AX.X, op=ALU.add),
                        [R(sq)], [ssx + "ssr"])
                    yield
                    rsqrt_inplace(ssn[0:P, 0:8], 64, [ssx + "ssn"])
                    rsqrt_inplace(ssn[0:P, 8:16], 32, [ssx + "ssr"])
                    DVE(lambda e: e.tensor_tensor(out=qn32[0:P], in0=hb0[0:P, :].rearrange("p (h d) -> p h d", d=64),
                                                  in1=ssn[0:P, 0:8].unsqueeze(2).to_broadcast([P, 8, 64]), op=ALU.mult),
                        [R(hb0), ssx + "ssn"], [R(qn32)])
                    POOL(lambda e: e.tensor_tensor(out=Qtm[0:P, :, 0:64], in0=qn32[0:P], in1=gqn_bc[0:P, :].unsqueeze(1).to_broadcast([P, 8, 64]), op=ALU.mult),
                         [R(qn32), R(gqn_bc), R(Qtm)], [R(Qtm)])
                    yield
                    DVE(lambda e: e.tensor_tensor(out=qr32[0:P], in0=hb1[0:P, 0:256].rearrange("p (h d) -> p h d", d=32),
                                                  in1=ssn[0:P, 8:16].unsqueeze(2).to_broadcast([P, 8, 32]), op=ALU.mult),
                        [R(hb1), ssx + "ssr"], [R(qr32)])
                    POOL(lambda e: e.tensor_tensor(out=qr32[0:P], in0=qr32[0:P], in1=gqr_bc[0:P, :].unsqueeze(1).to_broadcast([P, 8, 32]), op=ALU.mult),
                         [R(qr32), R(gqr_bc)], [R(qr32)])
                    yield
                    cosb = cst[0:P, 0:16].unsqueeze(1).to_broadcast([P, 8, 16])
                    sinb = cst[0:P, 16:32].unsqueeze(1).to_broadcast([P, 8, 16])
                    rope(Qtm[0:P, :, 64:80], Qtm[0:P, :, 80:96], qr32[0:P, :, 0:16], qr32[0:P, :, 16:32], cosb, sinb,
                         [(rt[i][0:P], R(rt[i])) for i in range(4)], [R(qr32), R(cst)], [R(Qtm)])
                    yield
                    transposes(pbA, Qtm, 8, P, [R(Qtm)])
                    DVE(lambda e, s=s: e.tensor_copy(out=QTblk[:, :, s * P:(s + 1) * P], in_=pbA[:, :, 0:P]), [R(pbA)], [R(QTblk)])
                gens = [subtile(blk, s_, (blk * NSUB + s_) % 2) for s_ in range(NSUB)]
                active = [gens.pop(0)]
                for _ in range(8):
                    next(active[0])
                while gens or active:
                    if len(active) < 2 and gens:
                        active.append(gens.pop(0))
                    for g_ in list(active):
                        try:
                            next(g_)
                        except StopIteration:
                            active.remove(g_)
                def conv_epilogue(zsrc, dst, dslot):
                    for ft in range(2):
                        cv = (cv0, cv1)[ft]
                        for k in range(31):
                            PE(lambda e, cv=cv, ft=ft, k=k: e.matmul(cv[:, 0:BS], lhsT=cdiag[:, ft, k, :], rhs=zsrc[:, ft, k:k + BS],
                                                                     start=(k == 0), stop=(k == 30)),
                               [R(cdiag), R(zsrc)], [R(cv)], sig=(k == 30))
                        ACT(lambda e, cv=cv, ft=ft: e.activation(out=y32[:, ft, 0:BS], in_=cv[:, 0:BS], func=AF.Identity, bias=bdw[:, ft:ft + 1]),
                            [R(cv), R(bdw)], [R(y32)])
                        POOL(lambda e, ft=ft: e.tensor_copy(out=ybf[:, ft, 0:BS], in_=y32[:, ft, 0:BS]), [R(y32)], [R(ybf)])
                        ACT(lambda e, ft=ft: e.activation(out=ysq[:, ft, 0:BS], in_=y32[:, ft, 0:BS], func=AF.Square), [R(y32)], [R(ysq)])
                    for ft in range(2):
                        PE(lambda e, ft=ft: e.matmul(hb0[:, 0:BS], lhsT=ones_bf[:], rhs=ybf[:, ft, 0:BS], start=(ft == 0), stop=(ft == 1)),
                           [R(ones_bf), R(ybf)], [R(hb0)], sig=(ft == 1))
                    for ft in range(2):
                        PE(lambda e, ft=ft: e.matmul(hb1[:, 0:BS], lhsT=ones_bf[:], rhs=ysq[:, ft, 0:BS], start=(ft == 0), stop=(ft == 1)),
                           [R(ones_bf), R(ysq)], [R(hb1)], sig=(ft == 1))
                    ACT(lambda e: e.activation(out=mean[:, 0:BS], in_=hb0[:, 0:BS], func=AF.Copy, scale=1.0 / 256), [R(hb0)], [R(mean)])
                    POOL(lambda e: e.tensor_tensor(out=msq[:, 0:BS], in0=mean[:, 0:BS], in1=mean[:, 0:BS], op=ALU.mult), [R(mean)], [R(msq)])
                    DVE(lambda e: e.scalar_tensor_tensor(out=rstd[:, 0:BS], in0=hb1[:, 0:BS], scalar=1.0 / 256, in1=msq[:, 0:BS],
                                                         op0=ALU.mult, op1=ALU.subtract),
                        [R(hb1), R(msq)], [R(rstd)])
                    ACT(lambda e: e.activation(out=rstd[:, 0:BS], in_=rstd[:, 0:BS], func=AF.Ln, bias=EPS, scale=1.0), [R(rstd)], [R(rstd)])
                    ACT(lambda e: e.activation(out=rstd[:, 0:BS], in_=rstd[:, 0:BS], func=AF.Exp, scale=-0.5), [R(rstd)], [R(rstd)])
                    for ft in range(2):
                        POOL(lambda e, ft=ft: e.tensor_tensor(out=tln[:, 0:BS], in0=y32[:, ft, 0:BS], in1=mean[:, 0:BS], op=ALU.subtract),
                             [R(y32), R(mean)], [R(tln)])
                        DVE(lambda e: e.tensor_tensor(out=tln[:, 0:BS], in0=tln[:, 0:BS], in1=rstd[:, 0:BS], op=ALU.mult), [R(tln), R(rstd)], [R(tln)])
                        ACT(lambda e, ft=ft: e.activation(out=dst[:, dslot + ft, 0:BS], in_=tln[:, 0:BS], func=AF.Silu,
                                                          bias=lnb[:, ft:ft + 1], scale=lng[:, ft:ft + 1]),
                            [R(tln), R(lnb), R(lng)], [R(dst)])
                c0 = blk * BS
                if G.remote:
                    DMA("mixw", G.mixT[0:256, c0:c0 + BS].rearrange("(k p) t -> p k t", p=128), mixblk[:, 0:2, 0:BS], [R(mixblk)], [G.name + "mixT"])
                else:
                    conv_epilogue(zTb, mixblk, 2)
                    DMA("mixw", G.mixT[0:512, c0:c0 + BS].rearrange("(k p) t -> p k t", p=128), mixblk[:, :, 0:BS], [R(mixblk)], [G.name + "mixT"])
                DMA("qtw", G.QT[:, :, c0:c0 + BS].rearrange("h p t -> p h t"), QTblk[:, :, 0:BS], [R(QTblk)], [G.name + "QT"])
                if blk + 1 < G.NBLK and not G.remote:
                    POOL(lambda e: e.tensor_copy(out=zTb[:, :, 0:30], in_=zTb[:, :, BS:BS + 30]), [R(zTb)], [R(zTb)])
            if G.remote:
                xin = [t.ap() for t in G.xin]
                xout = [t.ap() for t in G.xout]
                DMA("xi0", xin[0][:, :], latT[:, NP:NP + T], [R(latT)], [G.name + "xin0"])
                DMA("xi1", xin[1][:, :], krT[64:96, NP:NP + T], [R(krT)], [G.name + "xin1"])
                for j in range(G.NBLK):
                    DMA("xi2", xin[2][:, j * 60:(j + 1) * 60].rearrange("p (a b) -> p a b", a=2), zTs[j][:, :, BS:BS + 30], [R(zTs[j])], [G.name + "xin2"])
                for i in range(3):
                    S.dma("cc", lambda e, i=i: e.collective_compute("AllGather", ALU.bypass, replica_groups=[[0, 1], [2, 3], [4, 5], [6, 7]],
                                                                    ins=[G.xin[i].ap().opt()], outs=[G.xout[i].ap().opt()]),
                          [G.name + f"xin{i}"], [G.name + f"xout{i}"], eng="pool", inc=1)
                DMA("xo0", latT[:, 0:2 * T].rearrange("p (r t) -> p r t", r=2), xout[0].rearrange("(r p) t -> p r t", p=128), [G.name + "xout0"], [R(latT)])
                DMA("xo1", krT[64:96, 0:2 * T].rearrange("p (r t) -> p r t", r=2), xout[1].rearrange("(r p) t -> p r t", p=32), [G.name + "xout1"], [R(krT)])
                DMA("xo2", tz[:, :, :], xout[2].rearrange("(r p) t -> p r t", p=128), [G.name + "xout2"], [R(tz)])
                for j in range(G.NBLK):
                    zj = zTs[j]
                    POOL(lambda e, j=j, zj=zj: e.tensor_single_scalar(out=zj[:, :, 0:30], in_=tz[:, 0, j * 60:(j + 1) * 60].rearrange("p (a b) -> p a b", a=2),
                                                                      scalar=flag[:, 0:1], op=ALU.mult),
                         [R(tz), R(flag), R(zj)], [R(zj)])
                    if j >= 1:
                        DVE(lambda e, j=j, zj=zj: e.scalar_tensor_tensor(out=zj[:, :, 0:30], in0=tz[:, 1, (j - 1) * 60:j * 60].rearrange("p (a b) -> p a b", a=2),
                                                                          scalar=nflag[:, 0:1], in1=zj[:, :, 0:30], op0=ALU.mult, op1=ALU.add),
                             [R(tz), R(nflag), R(zj)], [R(zj)])
                    mb0 = mixblk0_2[j % 2]
                    conv_epilogue(zj, mb0, 0)
                    DMA("mixw0" + str(j % 2), G.mixT[256:512, j * BS:(j + 1) * BS].rearrange("(k p) t -> p k t", p=128), mb0[:, :, 0:BS], [R(mb0)], [G.name + "mixT"])

        S.barrier()
        AR.reset(BASE)
        Wkv_t = AR.alloc("Wkv", [128, 1, 1024], BF16)
        Wkv = Wkv_t[:, 0, :].rearrange("p (h c) -> p h c", c=128)
        gkn = AR.alloc("gkn", [128, 1], F32)
        load_w(Wkv_t, W["c_w_ukv"][l], 1, 1024)
        DMA("small", gkn[0:64, :], W["c_kn_g"][l].rearrange("(p o) -> p o", o=1), [], [R(gkn)])
        S2 = AR.mark()
        sm_scale = 1.0 / math.sqrt(96.0)
        for G in GROUPS:
            S.barrier()
            AR.reset(S2)
            P, BS, T, NP, NK = G.P, G.BS, G.T, G.NP, G.NK
            latT, krT = G.latT, G.krT
            ktiles = [(k0, min(128, NK - k0)) for k0 in range(0, NK, 128)]
            NKT = len(ktiles)
            KT2 = [[AR.alloc(f"KT{e}", [128, NK], BF16) for e in range(2)] for _ in range(2)]
            Vp2 = [AR.alloc("Vp", [128, NKT, 2, 66], BF16) for _ in range(2)]
            QTs2 = [[AR.alloc(f"QTs{e}", [128, T], BF16) for e in range(2)] for _ in range(2)]
            sqb1 = AR.alloc("sqb", [128, 512], BF16)
            sd1 = AR.alloc("sd", [128, 512], F32)
            PT = [AR.alloc(f"PT{i}", [128, 512], BF16) for i in range(6)]
            o_sb2 = [AR.alloc(f"o_sb{i}", [128, 512], F32) for i in range(2)]
            ycT = [AR.alloc(f"ycT{i}", [128, 512], BF16) for i in range(2)]
            pk, pst, pv, ps0, ps1, ps2 = PF
            SB = [p_[:].rearrange("p a b -> p (a b)").bitcast(F32) for p_ in PB] + [ps0, pv]
            pending_epi = []
            NRT = 0
            if G.remote:
                maskt = AR.alloc("maskt", [128, 8, 512], BF16)
                DMA("maskl", maskt[:], mask_d, [], [R(maskt)])
            for st_ in range(2):
                POOL(lambda e, st_=st_: e.memset(Vp2[st_][:], 1.0), [], [R(Vp2[st_])])
                if NRT:
                    POOL(lambda e, st_=st_: e.tensor_single_scalar(out=Vp2[st_][:, 0:NRT, :, 64:66], in_=Vp2[st_][:, 0:NRT, :, 64:66], scalar=flag[:, 0:1], op=ALU.mult),
                         [R(Vp2[st_]), R(flag)], [R(Vp2[st_])])
                for e_ in range(2):
                    POOL(lambda e, e_=e_, st_=st_: e.tensor_copy(out=KT2[st_][e_][64:96, :], in_=krT[64:96, :]), [R(krT)], [R(KT2[st_][e_])])
            pti = 0

            def kvmat(hg):
                st_ = hg % 2
                KT, Vp, QTs = KT2[st_], Vp2[st_], QTs2[st_]
                for e_ in range(2):
                    h = 2 * hg + e_
                    DMA("qtr" + str(st_) + str(e_), QTs[e_][:, :], G.QT[h], [G.name + "QT"], [R(QTs[e_])])
                    for k0 in range(0, NK, 512):
                        kw = min(512, NK - k0)
                        PE(lambda e, h=h, k0=k0, kw=kw: e.matmul(pk[0:64, 0:kw], lhsT=Wkv[:, h, 0:64], rhs=latT[:, k0:k0 + kw], start=True, stop=True),
                           [R(Wkv_t), R(latT)], [R(pk)])
                        ACT(lambda e, kw=kw: e.activation(out=sqb1[0:64, 0:kw], in_=pk[0:64, 0:kw], func=AF.Square), [R(pk)], [R(sqb1)])
                        PE(lambda e, kw=kw: e.matmul(pst[0:64, 0:kw], lhsT=ones_bf[0:64, 0:64], rhs=sqb1[0:64, 0:kw], start=True, stop=True),
                           [R(ones_bf), R(sqb1)], [R(pst)])
                        ACT(lambda e, kw=kw: e.activation(out=sd1[0:64, 0:kw], in_=pst[0:64, 0:kw], func=AF.Ln, bias=EPS, scale=1.0 / 64), [R(pst)], [R(sd1)])
                        ACT(lambda e, kw=kw: e.activation(out=sd1[0:64, 0:kw], in_=sd1[0:64, 0:kw], func=AF.Exp, scale=-0.5), [R(sd1)], [R(sd1)])
                        DVE(lambda e, e_=e_, k0=k0, kw=kw: e.scalar_tensor_tensor(out=KT[e_][0:64, k0:k0 + kw], in0=pk[0:64, 0:kw], scalar=gkn[0:64, 0:1],
                                                                                   in1=sd1[0:64, 0:kw], op0=ALU.mult, op1=ALU.mult),
                            [R(pk), R(gkn), R(sd1)], [R(KT[e_])])
                        yield
                for j0 in range(0, NKT, 4):
                    grp = ktiles[j0:j0 + 4]
                    for j, (k0, ksz) in enumerate(grp):
                        PE(lambda e, j=j, k0=k0, ksz=ksz, hg=hg: e.matmul(pk[0:ksz, j * 128:(j + 1) * 128], lhsT=latT[:, k0:k0 + ksz],
                                                                          rhs=Wkv[:, 2 * hg:2 * hg + 2, 64:128], start=True, stop=True),
                           [R(latT), R(Wkv_t)], [R(pk)])
                    full = [g_ for g_ in grp if g_[1] == 128]
                    if full:
                        n = len(full)
                        if j0 < NRT:
                            DVE(lambda e, j0=j0, n=n: e.tensor_single_scalar(out=Vp[:, j0:j0 + n, :, 0:64],
                                                                             in_=pk[:, 0:n * 128].rearrange("p (t h d) -> p t h d", h=2, d=64),
                                                                             scalar=flag[:, 0:1], op=ALU.mult),
                                [R(pk), R(flag)], [R(Vp)])
                        else:
                            DVE(lambda e, j0=j0, n=n: e.tensor_copy(out=Vp[:, j0:j0 + n, :, 0:64],
                                                                    in_=pk[:, 0:n * 128].rearrange("p (t h d) -> p t h d", h=2, d=64)),
                                [R(pk)], [R(Vp)])
                    for j, (k0, ksz) in enumerate(grp):
                        if ksz < 128:
                            DVE(lambda e, j=j, j0=j0, ksz=ksz: e.tensor_copy(out=Vp[0:ksz, j0 + j, :, 0:64],
                                                                             in_=pk[0:ksz, j * 128:(j + 1) * 128].rearrange("p (h d) -> p h d", d=64)),
                                [R(pk)], [R(Vp)])
                    yield

            for _ in kvmat(0):
                pass
            for hg in range(4):
                KT, Vp, QTs = KT2[hg % 2], Vp2[hg % 2], QTs2[hg % 2]
                nxt = kvmat(hg + 1) if hg + 1 < 4 else None
                ucount = 0
                for e_ in range(2):
                    h = 2 * hg + e_
                    for qb in range(G.NBLK):
                        q0 = qb * BS
                        if G.remote:
                            NT = T // 128
                            vis = []
                            for part in range(2):
                                for jb in range(qb + 1):
                                    for r in range(4):
                                        kt = part * NT + jb * 4 + r
                                        vis.append((kt, ktiles[kt][0], 128, (part * 4 + r) if jb == qb else None))
                        elif NP > 0:
                            vis = [(kt, k0, ksz, None) for kt, (k0, ksz) in enumerate(ktiles)]
                        else:
                            vis = []
                            for kt, (k0, ksz) in enumerate(ktiles):
                                if k0 >= q0 + BS:
                                    break
                                vis.append((kt, k0, ksz, (k0 - q0) // 128 if k0 >= q0 else None))
                        acc = (ps1, ps2)[(h * G.NBLK + qb) % 2]
                        n = len(vis)
                        LA = 3
                        units = []
                        for i in range(n + LA):
                            if i < n:
                                kt, k0, ksz, dg = vis[i]
                                c0 = dg * 128 if (dg is not None and not G.remote) else 0
                                pt = PT[pti % len(PT)]
                                sb = SB[pti % len(SB)]
                                pti += 1
                                units.append((pt, kt, ksz, c0))
                                PE(lambda e, e_=e_, k0=k0, ksz=ksz, c0=c0, q0=q0, sb=sb: e.matmul(sb[0:ksz, c0:BS], lhsT=KT[e_][0:96, k0:k0 + ksz],
                                                                                                  rhs=QTs[e_][0:96, q0 + c0:q0 + BS], start=True, stop=True),
                                   [R(KT[e_]), R(QTs[e_])], [R(sb)])
                                ACT(lambda e, pt=pt, ksz=ksz, c0=c0, sb=sb: e.activation(out=pt[0:ksz, c0:BS], in_=sb[0:ksz, c0:BS], func=AF.Exp, scale=sm_scale),
                                    [R(sb)], [R(pt)])
                                if dg is not None and G.remote:
                                    DVE(lambda e, pt=pt, dg=dg: e.tensor_tensor(out=pt[:, 0:BS], in0=pt[:, 0:BS], in1=maskt[:, dg, 0:BS], op=ALU.mult),
                                        [R(pt), R(maskt)], [R(pt)])
                                elif dg is not None:
                                    POOL(lambda e, pt=pt, c0=c0: e.memset(pt[64:128, c0:c0 + 64], 0.0), [R(pt)], [R(pt)])
                            j = i - LA
                            if j >= 0:
                                pt, kt, ksz, c0 = units[j]
                                PE(lambda e, pt=pt, kt=kt, ksz=ksz, c0=c0, e_=e_, acc=acc, j=j, n=n: e.matmul(
                                    acc[0:65, c0:BS], lhsT=Vp[0:ksz, kt, e_, 0:65], rhs=pt[0:ksz, c0:BS], start=(j == 0), stop=(j == n - 1)),
                                   [R(Vp), R(pt)], [R(acc)])
                            if i == min(14, n - 1) and pending_epi:
                                pending_epi.pop(0)()
                            ucount += 1
                            if nxt is not None and ucount % 10 == 0:
                                if next(nxt, "done") == "done":
                                    nxt = None

                        par = (h * G.NBLK + qb) % 2
                        yc, osb = ycT[par], o_sb2[par]
                        rd = osb
                        DVE(lambda e: e.tensor_copy(out=osb[0:65, 0:BS], in_=acc[0:65, 0:BS]), [R(acc)], [R(osb)])
                        DVE(lambda e: e.reciprocal(out=rd[64:65, 0:BS], in_=osb[64:65, 0:BS]), [R(osb)], [R(rd)])

                        def epilogue(h=h, q0=q0, par=par, yc=yc, osb=osb, rd=rd):
                            PE(lambda e: e.matmul(pst[0:64, 0:BS], lhsT=ones_f[64:65, 0:64], rhs=rd[64:65, 0:BS], start=True, stop=True),
                               [R(ones_f), R(rd)], [R(pst)])
                            DVE(lambda e: e.tensor_tensor(out=yc[0:64, 0:BS], in0=pst[0:64, 0:BS], in1=osb[0:64, 0:BS], op=ALU.mult),
                                [R(pst), R(osb)], [R(yc)])
                            DMA("ycw" + str(par), G.mixT[512 + h * 64:512 + (h + 1) * 64, q0:q0 + BS], yc[0:64, 0:BS], [R(yc)], [G.name + "mixT"])
                        pending_epi.append(epilogue)
                while pending_epi:
                    pending_epi.pop(0)()
                if nxt is not None:
                    for _ in nxt:
                        pass

        S.barrier()
        AR.reset(BASE0)
        Wout = AR.alloc("Wout", [128, 8, D], BF16)
        Wmq = AR.alloc("Wmq", [128, 8, 512], BF16)
        Wmo = AR.alloc("Wmo", [128, 4, D], BF16)
        Wmk = AR.alloc("Wmk", [128, 8, 512], BF16)
        Wmv = AR.alloc("Wmv", [128, 8, 512], BF16)
        g_nm = AR.alloc("g_nm", [128, 8], F32)
        g_mn = AR.alloc("g_mn", [128, 8], F32)
        g_mq = AR.alloc("g_mq", [128, 1], F32)
        gmk_bc = AR.alloc("gmk_bc", [128, 128], F32)
        load_gain_pk(g_nm, W["norm_mem_g"][l], 8)
        load_gain_pk(g_mn, W["mem_norm_g"][l], 8)
        DMA("small", g_mq[:, :], W["m_q_g"][l].rearrange("(p o) -> p o", o=1), [], [R(g_mq)])
        load_bcast(gmk_bc, W["m_k_g"][l], 128)
        load_w(Wmk, W["w_mk"][l], 8, 512, gain=g_mn, gname=R(g_mn))
        load_w(Wmv, W["w_mv"][l], 8, 512, gain=g_mn, gname=R(g_mn))
        load_w(Wout, W["w_out"][l], 8, D)
        load_w(Wmq, W["w_mq"][l], 8, 512, gain=g_nm, gname=R(g_nm))
        load_w(Wmo, W["w_mo"][l], 4, D)
        mm_scale = 1.0 / math.sqrt(128.0)
        S3 = AR.mark()
        for G in GROUPS:
            S.barrier()
            AR.reset(S3)
            P, BS, NSUB, T = G.P, G.BS, G.NSUB, G.T
            memKT = AR.alloc("memKT", [128, 4, 256], BF16)
            memV = AR.alloc("memV", [128, 2, 512], BF16)
            mx = AR.alloc("mx", [128, 2, D], F32)
            mxn = AR.alloc("mxn", [128, 8, 128], BF16)
            mnT = AR.alloc("mnT", [128, 8, 256], BF16)
            junk = AR.alloc("junk", [128, D], F32)
            ss = AR.alloc("ss", [128, 8], F32)
            mk32 = AR.alloc("mk32", [128, 512], F32)
            mkb = AR.alloc("mkb", [128, 4, 128], BF16)
            mv32 = AR.alloc("mv32", [128, 512], F32)
            pA, pB, pC, pD, pE_, pF_ = PF
            pbA, pbB = PB
            for s in range(2):
                if G is GP:
                    DMA("mx", mx[:, s, :], mem_in[s * 128:(s + 1) * 128, :], [], [R(mx)])
                    rms_to_bf16(mx[:, s, :], mxn[:].rearrange("p a b -> p (a b)"), junk[:, :], ss[:, 0:1], 128, D, [R(mx)], [R(junk), "ss0", R(mxn)])
                    transposes(pbA, mxn, 8, 128, [R(mxn)])
                    DVE(lambda e, s=s: e.tensor_copy(out=mnT[:, :, s * 128:(s + 1) * 128], in_=pbA[:, :, 0:128]), [R(pbA)], [R(mnT)])
                    for k in range(8):
                        PE(lambda e, k=k, s=s: e.matmul(pA[:, :], lhsT=mnT[:, k, s * 128:(s + 1) * 128], rhs=Wmk[:, k, :], start=(k == 0), stop=(k == 7)),
                           [R(mnT), R(Wmk)], [R(pA)], sig=(k == 7))
                    for k in range(8):
                        PE(lambda e, k=k, s=s: e.matmul(pB[:, :], lhsT=mnT[:, k, s * 128:(s + 1) * 128], rhs=Wmv[:, k, :], start=(k == 0), stop=(k == 7)),
                           [R(mnT), R(Wmv)], [R(pB)], sig=(k == 7))
                    ACT(lambda e: e.activation(out=junk[:, 0:512], in_=pA[:, :], func=AF.Square), [R(pA)], [R(junk)])
                    DVE(lambda e: e.tensor_reduce(out=ss[:, 4:8], in_=junk[:, 0:512].rearrange("p (h d) -> p h d", d=128), axis=AX.X, op=ALU.add),
                        [R(junk)], ["ss4"])
                    rsqrt_inplace(ss[:, 4:8], 128, ["ss4"])
                    DVE(lambda e: e.tensor_tensor(out=mk32[:].rearrange("p (h d) -> p h d", d=128), in0=pA[:, :].rearrange("p (h d) -> p h d", d=128),
                                                  in1=ss[:, 4:8].unsqueeze(2).to_broadcast([128, 4, 128]), op=ALU.mult),
                        [R(pA), "ss4"], [R(mk32)])
                    POOL(lambda e: e.tensor_tensor(out=mk32[:].rearrange("p (h d) -> p h d", d=128), in0=mk32[:].rearrange("p (h d) -> p h d", d=128),
                                                   in1=gmk_bc[:, :].unsqueeze(1).to_broadcast([128, 4, 128]), op=ALU.mult),
                         [R(mk32), R(gmk_bc)], [R(mk32)])
                    ACT(lambda e: e.activation(out=mv32[:, :], in_=pB[:, :], func=AF.Copy), [R(pB)], [R(mv32)])
                    DMA("omk", o_mk_p[l, s * 128:(s + 1) * 128, :], mk32[:, :], [R(mk32)], [])
                    DMA("omv", o_mv_p[l, s * 128:(s + 1) * 128, :], mv32[:, :], [R(mv32)], [])
                else:
                    DMA("mx", mk32[:, :], c_mk[l, s * 128:(s + 1) * 128, :], [], [R(mk32)])
                    DMA("mx", mv32[:, :], c_mv[l, s * 128:(s + 1) * 128, :], [], [R(mv32)])
                POOL(lambda e: e.tensor_copy(out=mkb[:].rearrange("p a b -> p (a b)"), in_=mk32[:, :]), [R(mk32)], [R(mkb)])
                POOL(lambda e, s=s: e.tensor_copy(out=memV[:, s, :], in_=mv32[:, :]), [R(mv32)], [R(memV)])
                transposes(pbB, mkb, 4, 128, [R(mkb)])
                DVE(lambda e, s=s: e.tensor_copy(out=memKT[:, :, s * 128:(s + 1) * 128], in_=pbB[:, 0:4, 0:128]), [R(pbB)], [R(memKT)])
            xb2 = [AR.alloc("xb", [128, 4, D], F32) for _ in range(2)]
            mixb2 = [AR.alloc("mixb", [128, 8, 512], BF16) for _ in range(2)]
            xn2 = [AR.alloc("xn", [128, 8, 128], BF16) for _ in range(2)]
            xnT2 = [AR.alloc("xnT", [128, 8, 512], BF16) for _ in range(2)]
            junk2 = [AR.alloc("junk", [128, D], F32) for _ in range(2)]
            ssb2 = [AR.alloc("ssb", [128, 8], F32) for _ in range(2)]
            sqb2 = [AR.alloc("sqb", [128, 512], BF16) for _ in range(2)]
            sd2 = [AR.alloc("sd", [128, 512], F32) for _ in range(2)]
            qmn2 = [AR.alloc("qmn", [128, 512], BF16) for _ in range(2)]
            PTm2 = [[AR.alloc(f"PTm{i}", [128, 512], BF16) for i in range(2)] for _ in range(2)]
            rden2 = [AR.alloc("rden", [128, 512], F32) for _ in range(2)]
            omT2 = [AR.alloc("omT", [128, 4, 512], BF16) for _ in range(2)]
            XB = [PF[0:3], PF[3:6]]
            AUXF3 = [p_[:].rearrange("p a b -> p (a b)").bitcast(F32) for p_ in PB]
            x_src = G.x_in if l == 0 else G.x1

            def block3a(blk, par):
                xb, mixb, xn, xnT, junk, ss, sqb, sd, qmn, PTm, rden, omT = (xb2[par], mixb2[par], xn2[par], xnT2[par], junk2[par], ssb2[par],
                                                                              sqb2[par], sd2[par], qmn2[par], PTm2[par], rden2[par], omT2[par])
                X0, X1, X2 = XB[par]
                pbT, auxf = PB[par], AUXF3[par]
                ssn_ = f"ss3a{par}"
                c0 = blk * BS
                DMA("xb" + str(par), xb[0:P, 0:NSUB, :], x_src[c0:c0 + BS, :].rearrange("(s p) d -> p s d", p=P), [G.name + "x1"] if l else [], [R(xb)])
                DMA("mixr" + str(par), mixb[:, :, 0:BS], G.mixT[:, c0:c0 + BS].rearrange("(k p) t -> p k t", p=128), [G.name + "mixT"], [R(mixb)])
                yield
                for s in range(NSUB):
                    for nh in range(2):
                        acc = (X0, X1)[nh]
                        for k in range(8):
                            PE(lambda e, k=k, s=s, nh=nh, acc=acc: e.matmul(acc[0:P, :], lhsT=mixb[:, k, s * P:(s + 1) * P], rhs=Wout[:, k, nh * 512:(nh + 1) * 512],
                                                                            start=(k == 0), stop=(k == 7)),
                               [R(mixb), R(Wout)], [R(acc)], sig=(k == 7))
                        DVE(lambda e, s=s, nh=nh, acc=acc: e.tensor_tensor(out=xb[0:P, s, nh * 512:(nh + 1) * 512], in0=acc[0:P, :],
                                                                           in1=xb[0:P, s, nh * 512:(nh + 1) * 512], op=ALU.add),
                            [R(acc), R(xb)], [R(xb)])
                    yield
                    rms_to_bf16(xb[0:P, s, :], xn[0:P].rearrange("p a b -> p (a b)"), junk[0:P, :], ss[0:P, 0:1], P, D, [R(xb)], [R(junk), ssn_, R(xn)])
                    yield
                    transposes(pbT, xn, 8, P, [R(xn)])
                    DVE(lambda e, s=s: e.tensor_copy(out=xnT[:, :, s * P:(s + 1) * P], in_=pbT[:, :, 0:P]), [R(pbT)], [R(xnT)])
                    yield
                for h in range(4):
                    for k in range(8):
                        PE(lambda e, k=k, h=h: e.matmul(X1[:, 0:BS], lhsT=Wmq[:, k, h * 128:(h + 1) * 128], rhs=xnT[:, k, 0:BS], start=(k == 0), stop=(k == 7)),
                           [R(Wmq), R(xnT)], [R(X1)], sig=(k == 7))
                    ACT(lambda e: e.activation(out=sqb[:, 0:BS], in_=X1[:, 0:BS], func=AF.Square), [R(X1)], [R(sqb)])
                    yield
                    PE(lambda e: e.matmul(X2[:, 0:BS], lhsT=ones_bf[:, :], rhs=sqb[:, 0:BS], start=True, stop=True), [R(ones_bf), R(sqb)], [R(X2)])
                    ACT(lambda e: e.activation(out=sd[:, 0:BS], in_=X2[:, 0:BS], func=AF.Ln, bias=EPS, scale=1.0 / 128), [R(X2)], [R(sd)])
                    ACT(lambda e: e.activation(out=sd[:, 0:BS], in_=sd[:, 0:BS], func=AF.Exp, scale=-0.5), [R(sd)], [R(sd)])
                    yield
                    DVE(lambda e: e.scalar_tensor_tensor(out=qmn[:, 0:BS], in0=X1[:, 0:BS], scalar=g_mq[:, 0:1], in1=sd[:, 0:BS], op0=ALU.mult, op1=ALU.mult),
                        [R(X1), R(g_mq), R(sd)], [R(qmn)])
                    yield
                    for kt in range(2):
                        sc = (X1, X2)[kt]
                        PE(lambda e, kt=kt, h=h, sc=sc: e.matmul(sc[:, 0:BS], lhsT=memKT[:, h, kt * 128:(kt + 1) * 128], rhs=qmn[:, 0:BS], start=True, stop=True),
                           [R(memKT), R(qmn)], [R(sc)])
                        ACT(lambda e, kt=kt, sc=sc: e.activation(out=PTm[kt][:, 0:BS], in_=sc[:, 0:BS], func=AF.Exp, scale=mm_scale), [R(sc)], [R(PTm[kt])])
                    yield
                    for kt in range(2):
                        PE(lambda e, kt=kt, h=h: e.matmul(X0[:, 0:BS], lhsT=memV[:, kt, h * 128:(h + 1) * 128], rhs=PTm[kt][:, 0:BS], start=(kt == 0), stop=(kt == 1)),
                           [R(memV), R(PTm[kt])], [R(X0)], sig=(kt == 1))
                    for kt in range(2):
                        PE(lambda e, kt=kt: e.matmul(auxf[:, 0:BS], lhsT=ones_bf[:, :], rhs=PTm[kt][:, 0:BS], start=(kt == 0), stop=(kt == 1)),
                           [R(ones_bf), R(PTm[kt])], [R(auxf)], sig=(kt == 1))
                    yield
                    ACT(lambda e: e.activation(out=rden[:, 0:BS], in_=auxf[:, 0:BS], func=AF.Ln), [R(auxf)], [R(rden)])
                    ACT(lambda e: e.activation(out=rden[:, 0:BS], in_=rden[:, 0:BS], func=AF.Exp, scale=-1.0), [R(rden)], [R(rden)])
                    DVE(lambda e, h=h: e.tensor_tensor(out=omT[:, h, 0:BS], in0=X0[:, 0:BS], in1=rden[:, 0:BS], op=ALU.mult), [R(X0), R(rden)], [R(omT)])
                    yield
                for s in range(NSUB):
                    for nh in range(2):
                        acc = (X0, X1)[nh]
                        for h in range(4):
                            PE(lambda e, h=h, s=s, nh=nh, acc=acc: e.matmul(acc[0:P, :], lhsT=omT[:, h, s * P:(s + 1) * P], rhs=Wmo[:, h, nh * 512:(nh + 1) * 512],
                                                                            start=(h == 0), stop=(h == 3)),
                               [R(omT), R(Wmo)], [R(acc)], sig=(h == 3))
                        DVE(lambda e, s=s, nh=nh, acc=acc: e.tensor_tensor(out=xb[0:P, s, nh * 512:(nh + 1) * 512], in0=acc[0:P, :],
                                                                           in1=xb[0:P, s, nh * 512:(nh + 1) * 512], op=ALU.add),
                            [R(acc), R(xb)], [R(xb)])
                    yield
                DMA("xmw" + str(par), G.xm[c0:c0 + BS, :].rearrange("(s p) d -> p s d", p=P), xb[0:P, 0:NSUB, :], [R(xb)], [G.name + "xm"])
                for s in range(NSUB):
                    rms_to_bf16(xb[0:P, s, :], xn[0:P].rearrange("p a b -> p (a b)"), junk[0:P, :], ss[0:P, 0:1], P, D, [R(xb)], [R(junk), ssn_, R(xn)])
                    yield
                    transposes(pbT, xn, 8, P, [R(xn)])
                    DVE(lambda e, s=s: e.tensor_copy(out=xnT[:, :, s * P:(s + 1) * P], in_=pbT[:, :, 0:P]), [R(pbT)], [R(xnT)])
                    yield
                DMA("xnfw" + str(par), G.xnf[:, c0:c0 + BS].rearrange("(k p) t -> p k t", p=128), xnT[:, :, 0:BS], [R(xnT)], [G.name + "xnf"])

            gens = [block3a(blk, blk % 2) for blk in range(G.NBLK)]
            active = [gens.pop(0)]
            for _ in range(14):
                next(active[0])
            while gens or active:
                if len(active) < 2 and gens:
                    active.append(gens.pop(0))
                for g_ in list(active):
                    try:
                        next(g_)
                    except StopIteration:
                        active.remove(g_)

        S.barrier()
        AR.reset(BASE0)
        W1 = AR.alloc("W1", [128, 8, 4 * D], BF16)
        W2 = AR.alloc("W2", [128, 32, D], BF16)
        g_ff = AR.alloc("g_ff", [128, 8], F32)
        load_gain_pk(g_ff, W["norm_ffn_g"][l], 8)
        load_w(W1, W["w_ff1"][l], 8, 4 * D, gain=g_ff, gname=R(g_ff))
        load_w(W2, W["w_ff2"][l], 32, D)
        S4 = AR.mark()
        for G in GROUPS:
            S.barrier()
            AR.reset(S4)
            P, BS, NSUB, T = G.P, G.BS, G.NSUB, G.T
            xb = AR.alloc("xb", [128, 4, D], F32)
            xnT2b = [AR.alloc("xnT", [128, 8, 512], BF16) for _ in range(2)]
            rl = [AR.alloc(f"rl{i}", [128, 512], F32) for i in range(2)]
            h1T = AR.alloc("h1T", [128, 16, 512], BF16)
            pA, pB, pC, pD, pE_, pF_ = PF
            pbA, pbB = PB
            x_dst = G.x1 if l == 0 else G.y_out

            def load_xn(b_):
                if b_ < G.NBLK:
                    DMA("xnfr" + str(b_ % 2), xnT2b[b_ % 2][:, :, 0:BS], G.xnf[:, b_ * BS:(b_ + 1) * BS].rearrange("(k p) t -> p k t", p=128),
                        [G.name + "xnf"], [R(xnT2b[b_ % 2])])
            load_xn(0)
            for blk in range(G.NBLK):
                c0 = blk * BS
                xnT = xnT2b[blk % 2]
                load_xn(blk + 1)
                DMA("xb", xb[0:P, 0:NSUB, :], G.xm[c0:c0 + BS, :].rearrange("(s p) d -> p s d", p=P), [G.name + "xm"], [R(xb)])
                for fh in range(2):
                    for f in range(16):
                        ff = fh * 16 + f
                        pp = (pA, pB)[f % 2]
                        r_ = rl[f % 2]
                        for k in range(8):
                            PE(lambda e, k=k, ff=ff, pp=pp: e.matmul(pp[:, 0:BS], lhsT=W1[:, k, ff * 128:(ff + 1) * 128], rhs=xnT[:, k, 0:BS],
                                                                     start=(k == 0), stop=(k == 7)),
                               [R(W1), R(xnT)], [R(pp)], sig=(k == 7))
                        ACT(lambda e, pp=pp, r_=r_: e.activation(out=r_[:, 0:BS], in_=pp[:, 0:BS], func=AF.Relu), [R(pp)], [R(r_)])
                        DVE(lambda e, pp=pp, r_=r_, f=f: e.tensor_tensor(out=h1T[:, f, 0:BS], in0=pp[:, 0:BS], in1=r_[:, 0:BS], op=ALU.mult),
                            [R(pp), R(r_)], [R(h1T)])
                    for s in range(NSUB):
                        for nh in range(2):
                            pp = (pC, pD)[nh]
                            for f in range(16):
                                PE(lambda e, f=f, s=s, nh=nh, pp=pp, fh=fh: e.matmul(pp[0:P, :], lhsT=h1T[:, f, s * P:(s + 1) * P],
                                                                                     rhs=W2[:, fh * 16 + f, nh * 512:(nh + 1) * 512],
                                                                                     start=(f == 0), stop=(f == 15)),
                                   [R(h1T), R(W2)], [R(pp)], sig=(f == 15))
                            DVE(lambda e, s=s, nh=nh, pp=pp: e.tensor_tensor(out=xb[0:P, s, nh * 512:(nh + 1) * 512], in0=pp[0:P, :],
                                                                             in1=xb[0:P, s, nh * 512:(nh + 1) * 512], op=ALU.add),
                                [R(pp), R(xb)], [R(xb)])
                wr = [G.name + "x1"] if l == 0 else []
                DMA("xow", x_dst[c0:c0 + BS, :].rearrange("(s p) d -> p s d", p=P), xb[0:P, 0:NSUB, :], [R(xb)], wr)

    S.emit()
    nc._n_ops = S.n_ops
    nc._peak = AR.peak
    return nc


def _rope_table(pos):
    half = 16
    inv = (np.float32(10000.0) ** (-(np.arange(half, dtype=np.float32)) / np.float32(half))).astype(np.float32)
    ang = (pos.astype(np.float32)[:, None] * inv[None, :]).astype(np.float32)
    return np.concatenate([np.cos(ang.astype(np.float64)), np.sin(ang.astype(np.float64))], axis=1).astype(np.float32)


_NC_CACHE = {}


def kernel(**inp):
    inp = {k: np.asarray(v) for k, v in inp.items()}
    B, SEQ, _ = inp["x_prompt"].shape
    NB = inp["x_sample"].shape[0]
    past = inp["cache_mla_latent"].shape[2]
    ncores = 8
    assert B * 2 == ncores and NB == ncores
    TP = SEQ // 2
    BLK = 512
    NBC = TP // BLK
    if TP not in _NC_CACHE:
        _NC_CACHE[TP] = build_nc(TP)
    nc = _NC_CACHE[TP]
    ident = np.eye(128, dtype=np.float32).astype(ml_dtypes.bfloat16)
    tril = np.tril(np.ones((128, 128), np.float32))
    pos_half = [np.concatenate([(2 * j + c) * BLK + np.arange(BLK) for j in range(NBC)]) for c in range(2)]
    cs_half = [_rope_table(p) for p in pos_half]
    cs_s = _rope_table(past + np.arange(TS))
    kk = np.arange(128)[:, None]
    qq = np.arange(BLK)[None, :]
    diag = np.stack([((r * 128 + kk) // 64 <= qq // 64) for r in range(4)], axis=1).astype(np.float32)
    masks = [np.concatenate([diag, np.zeros_like(diag)], axis=1), np.concatenate([np.ones_like(diag), diag], axis=1)]
    masks = [m.astype(ml_dtypes.bfloat16) for m in masks]
    wnames = ["norm_mix_g", "w_in", "a_norm_g", "a_ws", "a_bs", "b_dw_w", "b_dw_b", "b_ln_g", "b_ln_b", "c_qa_g", "c_w_uq",
              "c_kva_g", "c_w_ukv", "c_qn_g", "c_qr_g", "c_kn_g", "c_kr_g", "w_out", "norm_mem_g", "mem_norm_g", "w_mq",
              "w_mk", "w_mv", "w_mo", "m_q_g", "m_k_g", "norm_ffn_g", "w_ff1", "w_ff2"]
    shared = {n: np.ascontiguousarray(inp[n], dtype=np.float32) for n in wnames}
    shared.update(ident=ident, tril=tril, cs_s=cs_s)
    in_maps = []
    for c in range(ncores):
        b, half = c // 2, c % 2
        m = dict(shared)
        m["x_p"] = np.ascontiguousarray(inp["x_prompt"][b][pos_half[half]])
        m["cs_p"] = cs_half[half]
        m["flag"] = np.full((128, 1), float(half), np.float32)
        m["mask"] = masks[half]
        m["x_s"] = np.ascontiguousarray(inp["x_sample"][c])
        m["mem"] = np.ascontiguousarray(inp["mem_prompt"][b])
        m["c_lat"] = np.ascontiguousarray(inp["cache_mla_latent"][:, c])
        m["c_kr"] = np.ascontiguousarray(inp["cache_mla_krope"][:, c])
        m["c_conv"] = np.ascontiguousarray(inp["cache_conv"][:, c])
        m["c_mk"] = np.ascontiguousarray(inp["cache_mem_k"][:, c]).reshape(DEPTH, 256, 512)
        m["c_mv"] = np.ascontiguousarray(inp["cache_mem_v"][:, c]).reshape(DEPTH, 256, 512)
        in_maps.append(m)
    res = run_bass_kernel_spmd(nc, in_maps, core_ids=list(range(ncores)))
    r = res.results

    def halves(name, axis):
        outs = []
        for b in range(B):
            a0, a1 = r[2 * b][name], r[2 * b + 1][name]
            full = np.empty(a0.shape[:axis] + (SEQ,) + a0.shape[axis + 1:], a0.dtype)
            idx = [slice(None)] * full.ndim
            for half, arr in ((0, a0), (1, a1)):
                idx[axis] = pos_half[half]
                full[tuple(idx)] = arr
            outs.append(full)
        return np.stack(outs)
    y_p = halves("y_p", 0)
    y_s = np.stack([r[c]["y_s"] for c in range(NB)])
    lat_p = np.moveaxis(halves("o_lat_p", 1), 0, 1)
    kr_p = np.moveaxis(halves("o_kr_p", 1), 0, 1)
    conv_p = np.stack([r[2 * b + 1]["o_conv_p"] for b in range(B)], axis=1)
    mk_p = np.stack([r[2 * b]["o_mk_p"] for b in range(B)], axis=1).reshape(DEPTH, B, 256, 4, 128)
    mv_p = np.stack([r[2 * b]["o_mv_p"] for b in range(B)], axis=1).reshape(DEPTH, B, 256, 4, 128)
    lat_s = np.stack([r[c]["o_lat_s"] for c in range(NB)], axis=1)
    kr_s = np.stack([r[c]["o_kr_s"] for c in range(NB)], axis=1)
    conv_s = np.stack([r[c]["o_conv_s"] for c in range(NB)], axis=1)
    gv_s = np.stack([r[c]["o_gv_s"] for c in range(NB)], axis=1)
    outs = (y_p, y_s, lat_p, kr_p, conv_p, mk_p, mv_p, lat_s, kr_s, conv_s, gv_s)
    return tuple(np.ascontiguousarray(o, dtype=np.float32) for o in outs)
```
